# Optimizing a Trainium2 kernel written in Bass

```python
import math
import jax, jax.numpy as jnp
from jax import lax
import numpy as np

D_MODEL = 1024
BATCH = 2
SEQ = 8192
DEPTH = 4

CHUNK = 64
N_A_LAYERS = DEPTH // 2
N_B_LAYERS = DEPTH - N_A_LAYERS

RET_HEADS = 8
RET_QK_DIM = D_MODEL // RET_HEADS
RET_V_DIM = 2 * RET_QK_DIM
RET_QK_WIDTH = RET_HEADS * RET_QK_DIM
RET_WIDTH = RET_HEADS * RET_V_DIM
RET_IN_WIDTH = 2 * RET_QK_WIDTH + 2 * RET_WIDTH

DIFF_HEADS = 8
DIFF_HEAD_DIM = D_MODEL // (2 * DIFF_HEADS)
DIFF_V_DIM = 2 * DIFF_HEAD_DIM
DIFF_QK_WIDTH = DIFF_HEADS * 2 * DIFF_HEAD_DIM
DIFF_WIDTH = DIFF_HEADS * DIFF_V_DIM
DIFF_IN_WIDTH = DIFF_QK_WIDTH + DIFF_WIDTH
KV_WIDTH = DIFF_QK_WIDTH + DIFF_WIDTH

ROPE_THETA = 10000.0
Q_BLOCK = 128
EPS = 1e-6

kernel_name = "yoco_retention_diffattn_sandwich"


def rmsnorm(x, g):
    xf = x.astype(jnp.float32)
    y = xf * lax.rsqrt(jnp.mean(xf * xf, axis=-1, keepdims=True) + EPS)
    return (y * g.astype(jnp.float32)).astype(x.dtype)


def rope(x):
    s, dim = x.shape[1], x.shape[-1]
    half = dim // 2
    inv = ROPE_THETA ** (-jnp.arange(half, dtype=jnp.float32) / half)
    ang = jnp.arange(s, dtype=jnp.float32)[:, None] * inv[None, :]
    shape = (1, s) + (1,) * (x.ndim - 3) + (half,)
    cos = jnp.cos(ang).reshape(shape)
    sin = jnp.sin(ang).reshape(shape)
    xf = x.astype(jnp.float32)
    x1, x2 = xf[..., :half], xf[..., half:]
    return jnp.concatenate([x1 * cos - x2 * sin, x1 * sin + x2 * cos], axis=-1).astype(x.dtype)


def retention(q, k, v):
    b, s, h, dk = q.shape
    dv = v.shape[-1]
    n = s // CHUNK
    log_gamma = jnp.log1p(-jnp.exp2(-5.0 - jnp.arange(h, dtype=jnp.float32)))
    pos = jnp.arange(CHUNK, dtype=jnp.float32)
    diff = pos[:, None] - pos[None, :]
    decay_mask = jnp.where(diff[None] >= 0,
                           jnp.exp(jnp.maximum(diff, 0.0)[None] * log_gamma[:, None, None]),
                           0.0)
    q_decay = jnp.exp((pos[None, :] + 1.0) * log_gamma[:, None])[:, :, None]
    k_decay = jnp.exp((CHUNK - 1.0 - pos[None, :]) * log_gamma[:, None])[:, :, None]
    chunk_decay = jnp.exp(CHUNK * log_gamma)[None, :, None, None]

    def to_chunks(t):
        return t.astype(jnp.float32).reshape(b, n, CHUNK, h, t.shape[-1]).transpose(1, 0, 3, 2, 4)

    qc = to_chunks(q)
    kc = to_chunks(k) * (dk ** -0.5)
    vc = to_chunks(v)
    scores = jnp.einsum('nbhid,nbhjd->nbhij', qc, kc) * decay_mask[None, None]
    intra = jnp.einsum('nbhij,nbhje->nbhie', scores, vc)

    def step(state, inp):
        q_i, k_i, v_i = inp
        inter = jnp.einsum('bhcd,bhde->bhce', q_i * q_decay, state)
        state = state * chunk_decay + jnp.einsum('bhcd,bhce->bhde', k_i * k_decay, v_i)
        return state, inter

    state0 = jnp.zeros((b, h, dk, dv), jnp.float32)
    _, inter = lax.scan(step, state0, (qc, kc, vc))
    o = (intra + inter).transpose(1, 0, 3, 2, 4).reshape(b, s, h, dv)
    mu = jnp.mean(o, axis=-1, keepdims=True)
    var = jnp.mean(jnp.square(o - mu), axis=-1, keepdims=True)
    return (o - mu) * lax.rsqrt(var + EPS)


def diff_attention(q, k, v, lam):
    b, s, h, _, dh = q.shape
    dv = v.shape[-1]
    nblk = s // Q_BLOCK
    scale = dh ** -0.5
    qb = q.reshape(b, nblk, Q_BLOCK, h, 2, dh).transpose(1, 0, 2, 3, 4, 5)
    kf = k.astype(jnp.float32)
    vf = v.astype(jnp.float32)
    key_chunk = jnp.arange(s) // CHUNK
    neg = jnp.finfo(jnp.float32).min

    def block(args):
        q_i, i = args
        q_chunk = (i * Q_BLOCK + jnp.arange(Q_BLOCK)) // CHUNK
        mask = key_chunk[None, :] <= q_chunk[:, None]
        sc = jnp.einsum('bqhtd,bkhtd->bhtqk', q_i.astype(jnp.float32), kf) * scale
        p = jax.nn.softmax(jnp.where(mask, sc, neg), axis=-1)
        w = p[:, :, 0] - lam * p[:, :, 1]
        return jnp.einsum('bhqk,bkhe->bqhe', w, vf)

    out = lax.map(block, (qb, jnp.arange(nblk)))
    return out.transpose(1, 0, 2, 3, 4).reshape(b, s, h, dv)


def setup_inputs(seed: int = 0) -> dict:
    key = jax.random.key(seed)
    ks = jax.random.split(key, 16)
    f32 = jnp.float32

    def nrm(k, shape, scale):
        return jax.random.normal(k, shape, f32) * scale

    return {
        "x": nrm(ks[0], (BATCH, SEQ, D_MODEL), 1.0),
        "pre_norm": 1.0 + nrm(ks[1], (DEPTH, D_MODEL), 0.02),
        "post_norm": 1.0 + nrm(ks[2], (DEPTH, D_MODEL), 0.02),
        "w_in_a": nrm(ks[3], (N_A_LAYERS, D_MODEL, RET_IN_WIDTH), D_MODEL ** -0.5),
        "w_out_a": nrm(ks[4], (N_A_LAYERS, RET_WIDTH, D_MODEL), RET_WIDTH ** -0.5),
        "kv_norm": 1.0 + nrm(ks[5], (D_MODEL,), 0.02),
        "w_kv": nrm(ks[6], (D_MODEL, KV_WIDTH), D_MODEL ** -0.5),
        "w_in_b": nrm(ks[7], (N_B_LAYERS, D_MODEL, DIFF_IN_WIDTH), D_MODEL ** -0.5),
        "lam_q1": nrm(ks[8], (N_B_LAYERS, DIFF_HEAD_DIM), 0.1),
        "lam_k1": nrm(ks[9], (N_B_LAYERS, DIFF_HEAD_DIM), 0.1),
        "lam_q2": nrm(ks[10], (N_B_LAYERS, DIFF_HEAD_DIM), 0.1),
        "lam_k2": nrm(ks[11], (N_B_LAYERS, DIFF_HEAD_DIM), 0.1),
        "sub_norm_b": 1.0 + nrm(ks[12], (N_B_LAYERS, DIFF_V_DIM), 0.02),
        "w_out_b": nrm(ks[13], (N_B_LAYERS, DIFF_WIDTH, D_MODEL), DIFF_WIDTH ** -0.5),
    }


def reference(x, pre_norm, post_norm, w_in_a, w_out_a, kv_norm, w_kv, w_in_b,
              lam_q1, lam_k1, lam_q2, lam_k2, sub_norm_b, w_out_b):
    b, s, _ = x.shape
    k_shared = None
    v_shared = None
    for layer in range(DEPTH):
        h = rmsnorm(x, pre_norm[layer])
        if layer < N_A_LAYERS:
            proj = h @ w_in_a[layer]
            q = proj[..., :RET_QK_WIDTH].reshape(b, s, RET_HEADS, RET_QK_DIM)
            k = proj[..., RET_QK_WIDTH:2 * RET_QK_WIDTH].reshape(b, s, RET_HEADS, RET_QK_DIM)
            v = proj[..., 2 * RET_QK_WIDTH:2 * RET_QK_WIDTH + RET_WIDTH].reshape(b, s, RET_HEADS, RET_V_DIM)
            gate = proj[..., 2 * RET_QK_WIDTH + RET_WIDTH:]
            o = retention(rope(q), rope(k), v).reshape(b, s, RET_WIDTH)
            y = (o.astype(x.dtype) * jax.nn.silu(gate)) @ w_out_a[layer]
        else:
            j = layer - N_A_LAYERS
            if k_shared is None:
                kv = rmsnorm(x, kv_norm) @ w_kv
                k_shared = rope(kv[..., :DIFF_QK_WIDTH].reshape(b, s, DIFF_HEADS, 2, DIFF_HEAD_DIM))
                v_shared = kv[..., DIFF_QK_WIDTH:].reshape(b, s, DIFF_HEADS, DIFF_V_DIM)
            lambda_init = 0.8 - 0.6 * math.exp(-0.3 * layer)
            lam = (jnp.exp(jnp.sum(lam_q1[j].astype(jnp.float32) * lam_k1[j].astype(jnp.float32)))
                   - jnp.exp(jnp.sum(lam_q2[j].astype(jnp.float32) * lam_k2[j].astype(jnp.float32)))
                   + lambda_init)
            proj = h @ w_in_b[layer - N_A_LAYERS]
            q = rope(proj[..., :DIFF_QK_WIDTH].reshape(b, s, DIFF_HEADS, 2, DIFF_HEAD_DIM))
            gate = proj[..., DIFF_QK_WIDTH:]
            o = diff_attention(q, k_shared, v_shared, lam)
            o = rmsnorm(o, sub_norm_b[j]) * (1.0 - lambda_init)
            o = o.reshape(b, s, DIFF_WIDTH).astype(x.dtype)
            y = (o * jax.nn.silu(gate)) @ w_out_b[j]
        x = x + rmsnorm(y, post_norm[layer])
    return x
```

```python
import math
from contextlib import ExitStack

import numpy as np
import ml_dtypes

import concourse.bass as bass
import concourse.mybir as mybir
from concourse.bass_utils import run_bass_kernel_spmd

F32 = mybir.dt.float32
BF16 = mybir.dt.bfloat16
AF = mybir.ActivationFunctionType
ALU = mybir.AluOpType
AX = mybir.AxisListType

NCORES = 8
D = 1024
B = 2
S = 8192
NTOK = B * S
EPS = 1e-6
TOK_PER_CORE = NTOK // NCORES
TILES_PER_CORE = TOK_PER_CORE // 128
ST = 512
NST = NTOK // ST
ST_PER_CORE = TOK_PER_CORE // ST
ST_PER_BATCH = S // ST

SAME_ENGINE_SYNC = True


import os
SKIP = set(os.environ.get("KSKIP", "").split(","))
ELT = os.environ.get("KELT", "dve")


PSUM_KEYS = {"pj", "sc", "po", "pdS", "ptr", "Y", "psT", "pl"}


class Prog:
    ENGS = ("pe", "act", "dve", "pool", "sp")

    def __init__(self, nc):
        self.nc = nc
        self.ops = {e: [] for e in self.ENGS}
        self.lastw = {}
        self.readers = {}
        self.dma_cnt = {}
        self.cur_tag = None

    def _deps(self, eng, reads, writes):
        toks = set()
        for r in reads:
            if r in self.lastw:
                toks.add(self.lastw[r])
            if (r if isinstance(r, str) else r[0]) in PSUM_KEYS:
                for t in self.readers.get(r, ()):
                    if t[0] == "eng" and t[1] != eng:
                        toks.add(t)
        for w in writes:
            if w in self.lastw:
                toks.add(self.lastw[w])
            for t in self.readers.get(w, ()):
                toks.add(t)
        out = set()
        for t in toks:
            if t[0] == "eng" and t[1] == eng:
                if eng in ("pe", "sp") or not SAME_ENGINE_SYNC:
                    continue
            out.add(t)
        return out

    def _commit(self, tok, reads, writes):
        for r in reads:
            self.readers.setdefault(r, []).append(tok)
        for w in writes:
            self.lastw[w] = tok
            self.readers[w] = []

    def op(self, eng, fn, reads=(), writes=(), tag=None):
        tag = tag or self.cur_tag
        if tag is not None and tag in SKIP:
            return
        deps = self._deps(eng, reads, writes)
        idx = len(self.ops[eng])
        self.ops[eng].append(dict(fn=fn, deps=deps, dma=None))
        self._commit(("eng", eng, idx), reads, writes)

    def dma(self, eng, fn, semkey, reads=(), writes=(), tag=None):
        if tag is not None and tag in SKIP:
            return
        deps = self._deps(eng, reads, writes)
        cnt = self.dma_cnt.get(semkey, 0) + 16
        self.dma_cnt[semkey] = cnt
        self.ops[eng].append(dict(fn=fn, deps=deps, dma=semkey))
        self._commit(("dma", semkey, cnt), reads, writes)

    def dma_multi(self, eng, fns, semkey, reads=(), writes_list=(), tag=None):
        if tag is not None and tag in SKIP:
            return
        final = self.dma_cnt.get(semkey, 0) + 16 * len(fns)
        for fn, w in zip(fns, writes_list):
            deps = self._deps(eng, reads, w)
            self.ops[eng].append(dict(fn=fn, deps=deps, dma=semkey))
            self._commit(("dma", semkey, final), reads, w)
        self.dma_cnt[semkey] = final

    def emit(self, final_waits=()):
        nc = self.nc
        self.op("sp", None, reads=tuple(final_waits))
        needed = set()
        for e in self.ENGS:
            for o in self.ops[e]:
                for t in o["deps"]:
                    if t[0] == "eng":
                        needed.add((t[1], t[2]))
        inc_count = {}
        for e in self.ENGS:
            c = 0
            for i, o in enumerate(self.ops[e]):
                if (e, i) in needed:
                    c += 1
                    inc_count[(e, i)] = c
        with ExitStack() as es:
            esem = {e: es.enter_context(nc.semaphore("s_" + e)) for e in self.ENGS}
            dsem = {k: es.enter_context(nc.semaphore("d_%d" % i))
                    for i, k in enumerate(sorted(self.dma_cnt, key=str))}
            block = es.enter_context(nc.Block())

            def run(e, h):
                waited = {}
                for i, o in enumerate(self.ops[e]):
                    want = {}
                    for t in o["deps"]:
                        if t[0] == "eng":
                            k, v = ("e", t[1]), inc_count[(t[1], t[2])]
                        else:
                            k, v = ("d", t[1]), t[2]
                        if v > want.get(k, 0):
                            want[k] = v
                    for k, v in want.items():
                        if waited.get(k, 0) >= v:
                            continue
                        waited[k] = v
                        h.wait_ge(esem[k[1]] if k[0] == "e" else dsem[k[1]], v)
                    if o["fn"] is None:
                        continue
                    ins = o["fn"](h)
                    if o["dma"] is not None:
                        ins.then_inc(dsem[o["dma"]], 16)
                    elif (e, i) in inc_count:
                        ins.then_inc(esem[e], 1)

            block.tensor(lambda h: run("pe", h))
            block.scalar(lambda h: run("act", h))
            block.vector(lambda h: run("dve", h))
            block.gpsimd(lambda h: run("pool", h))
            block.sync(lambda h: run("sp", h))


def _bf16(a):
    return np.asarray(a).astype(ml_dtypes.bfloat16)


def _rope_tables(half, dim_rows):
    inv = np.power(np.float32(10000.0), -np.arange(half, dtype=np.float32) / np.float32(half)).astype(np.float32)
    ang = (np.arange(S, dtype=np.float32)[:, None] * inv[None, :]).astype(np.float32)
    cos = np.cos(ang.astype(np.float64)).astype(np.float32)
    sin = np.sin(ang.astype(np.float64)).astype(np.float32)
    rows = np.arange(dim_rows)
    f = (rows % (2 * half)) % half
    sign = np.where((rows % (2 * half)) < half, -1.0, 1.0).astype(np.float32)
    C = cos[:, f].T
    Sg = (sin[:, f] * sign[None, :]).T
    C = np.ascontiguousarray(C.reshape(dim_rows, ST_PER_BATCH, ST).transpose(1, 0, 2))
    Sg = np.ascontiguousarray(Sg.reshape(dim_rows, ST_PER_BATCH, ST).transpose(1, 0, 2))
    return C.astype(np.float32), Sg.astype(np.float32)


def _ret_consts(h):
    lg = math.log1p(-2.0 ** (-5.0 - h))
    g = lambda e: math.exp(e * lg)
    s = 128.0 ** -0.5
    p = np.arange(128)
    MA = np.zeros((128, 4, 128), np.float64)
    MA[:, 0, :] = s * g(-128) * (p[:, None] <= p[None, :])
    for d in range(1, 4):
        MA[:, d, :] = s * g(128 * (d - 1))
    CA = np.zeros((128, 16), np.float64)
    CA[:, 0] = np.exp((127 - p) * lg)
    CA[:, 1] = EPS * np.exp(-2.0 * (p + 1) * lg)
    for a in range(4):
        CA[:, 2 + a] = g(128 * a)
        CA[:, 6 + a] = s * g(128 * (3 - a))
    CA[:, 10] = g(512)
    return MA.reshape(128, 512).astype(np.float32), CA.astype(np.float32)


def _rep(v, n=128):
    return np.ascontiguousarray(np.broadcast_to(np.asarray(v, np.float32).reshape(1, -1), (n, np.asarray(v).size)))


class Alloc:
    def __init__(self, nc, es):
        self.nc, self.es = nc, es

    def sb(self, name, shape, dt):
        return self.es.enter_context(self.nc.sbuf_tensor("sb_" + name, list(shape), dt))

    def ps(self, name, shape, dt):
        return self.es.enter_context(self.nc.psum_tensor("ps_" + name, list(shape), dt))


def _finish(P, extra=()):
    allres = list(P.lastw.keys())
    P.op("sp", lambda h: h.nop(), reads=allres, writes=["__done"])
    for e in ("pe", "act", "dve", "pool"):
        P.op(e, lambda h: h.nop(), reads=["__done"])
    P.emit(final_waits=allres)


def _load_w_bf16(P, Wsb, Wd, nkc, c0, c1, key):
    P.dma_multi("pool", [lambda h, kc=kc: h.dma_start(out=Wsb[:, kc, c0:c1], in_=Wd[kc * 128:(kc + 1) * 128, c0:c1])
                         for kc in range(nkc)], key, writes_list=[[(key, kc)] for kc in range(nkc)], tag="ldW")


def phase_T(nc, X, d):
    with ExitStack() as es:
        A = Alloc(nc, es)
        P = Prog(nc)
        t2 = d.get("t2")
        t1s = d.get("t1", [])
        I = A.sb("I", [128, 128], BF16)
        junk = A.sb("junk", [128, D], BF16)
        st_ = A.sb("stt", [128, 8], F32)
        P.dma("sp", lambda h: h.dma_start(out=I[:, :], in_=d["ident"][:, :]), "I", writes=["I"])
        if t2:
            nkc = t2["nkc"]
            Wo = A.sb("Wo", [128, nkc, D], BF16)
            Gp = A.sb("Gp", [128, D], F32)
            OG = [A.sb("OG%d" % i, [128, nkc, ST], BF16) for i in range(2)]
            tmp = A.sb("tmp", [128, D], F32)
            Y = [A.ps("Y%d" % i, [128, D], F32) for i in range(2)]
            _load_w_bf16(P, Wo, t2["wout"], nkc, 0, D, "Wo")
            P.dma("sp", lambda h: h.dma_start(out=Gp[:, :], in_=t2["gpost"][:, :]), "Gp", writes=["Gp"])
            nch = nkc // 8
        G1 = []
        for i, (g, _) in enumerate(t1s):
            Gt = A.sb("G1_%d" % i, [128, D], F32)
            P.dma("sp", lambda h, Gt=Gt, g=g: h.dma_start(out=Gt[:, :], in_=g[:, :]), "G1_%d" % i, writes=["G1_%d" % i])
            G1.append(Gt)
        if t1s:
            hb = A.sb("hb", [128, D], BF16)
            psT = A.ps("psT", [128, 8, 128], BF16)
            hTs = [[A.sb("hTs%d_%d" % (i, k), [128, 8, ST], BF16) for k in range(2)] for i in range(len(t1s))]

        def load_og(s):
            sl = s % 2
            P.dma_multi("sp", [lambda h, hh=hh, sl=sl, s=s: h.dma_start(
                out=OG[sl][:, hh * nch:(hh + 1) * nch, :], in_=t2["og"][hh, s]) for hh in range(8)],
                ("OG", sl), writes_list=[[("OG", sl, hh)] for hh in range(8)])

        if d.get("x_in") is not None:
            for grp in range(4):
                P.dma_multi("sp", [lambda h, lt=lt: h.dma_start(out=X[:, lt, :], in_=d["x_in"][lt * 128:(lt + 1) * 128, :])
                                   for lt in range(grp * 4, grp * 4 + 4)], ("Xin", grp),
                            writes_list=[[("X", lt)] for lt in range(grp * 4, grp * 4 + 4)])
        if t2:
            load_og(0)
        for lt in range(TILES_PER_CORE):
            s, a = lt // 4, lt % 4
            if t2:
                if a == 0 and s + 1 < ST_PER_CORE:
                    load_og(s + 1)
                sl = s % 2
                y = Y[lt % 2]
                for nb in range(2):
                    for kc in range(nkc):
                        P.op("pe", lambda h, y=y, nb=nb, kc=kc, sl=sl, a=a: h.matmul(
                            y[:, nb * 512:(nb + 1) * 512], lhsT=OG[sl][:, kc, a * 128:(a + 1) * 128],
                            rhs=Wo[:, kc, nb * 512:(nb + 1) * 512], start=(kc == 0), stop=(kc == nkc - 1)),
                            reads=[("OG", sl, kc // nch), ("Wo", kc)], writes=[("Y", lt % 2)], tag="t2mm")
                P.op("act", lambda h, y=y: h.activation(out=junk[:, :], in_=y[:, :], func=AF.Square, accum_out=st_[:, 0:1]),
                     reads=[("Y", lt % 2)], writes=["junk", "ssq"], tag="t2sq")
                P.op("act", lambda h: h.activation(out=st_[:, 1:2], in_=st_[:, 0:1], func=AF.Sqrt, scale=1.0 / D, bias=EPS),
                     reads=["ssq"], writes=["rstd"])
                P.op("dve", lambda h: h.reciprocal(out=st_[:, 1:2], in_=st_[:, 1:2]), reads=["rstd"], writes=["rstd"])
                P.op("dve", lambda h, y=y: h.scalar_tensor_tensor(out=tmp[:, :], in0=y[:, :], scalar=st_[:, 1:2], in1=Gp[:, :],
                                                                  op0=ALU.mult, op1=ALU.mult),
                     reads=[("Y", lt % 2), "rstd", "Gp"], writes=["tmp"], tag="t2stt")
                P.op(ELT, lambda h, lt=lt: h.tensor_tensor(out=X[:, lt, :], in0=X[:, lt, :], in1=tmp[:, :], op=ALU.add),
                     reads=[("X", lt), "tmp"], writes=[("X", lt)], tag="t2add")
            if d.get("x_out") is not None:
                P.dma("pool", lambda h, lt=lt: h.dma_start(out=d["x_out"][lt * 128:(lt + 1) * 128, :], in_=X[:, lt, :]),
                      ("Xout", lt % 4), reads=[("X", lt)], writes=[("xo", lt)])
            for i, (g, hT_out) in enumerate(t1s):
                sl = s % 2
                P.op("act", lambda h, lt=lt: h.activation(out=junk[:, :], in_=X[:, lt, :], func=AF.Square, accum_out=st_[:, 2:3]),
                     reads=[("X", lt)], writes=["junk", "ssq1"])
                P.op("act", lambda h: h.activation(out=st_[:, 3:4], in_=st_[:, 2:3], func=AF.Sqrt, scale=1.0 / D, bias=EPS),
                     reads=["ssq1"], writes=["rstd1"])
                P.op("dve", lambda h: h.reciprocal(out=st_[:, 3:4], in_=st_[:, 3:4]), reads=["rstd1"], writes=["rstd1"])
                P.op("dve", lambda h, lt=lt, i=i: h.scalar_tensor_tensor(out=hb[:, :], in0=X[:, lt, :], scalar=st_[:, 3:4],
                                                                         in1=G1[i][:, :], op0=ALU.mult, op1=ALU.mult),
                     reads=[("X", lt), "rstd1", "G1_%d" % i], writes=["hb"])
                for kc in range(8):
                    P.op("pe", lambda h, kc=kc: h.transpose(out=psT[:, kc, :], in_=hb[:, kc * 128:(kc + 1) * 128], identity=I[:, :]),
                         reads=["hb", "I"], writes=["psT"])
                P.op("act", lambda h, i=i, sl=sl, a=a: h.copy(out=hTs[i][sl][:, :, a * 128:(a + 1) * 128], in_=psT[:, :, :]),
                     reads=["psT"], writes=[("hTs", i, sl)])
                if a == 3:
                    P.dma("pool", lambda h, i=i, sl=sl, s=s, hT_out=hT_out: h.dma_start(out=hT_out[s], in_=hTs[i][sl][:, :, :]),
                          ("hTo", i, sl), reads=[("hTs", i, sl)], writes=[("hTd", i, s)])
        _finish(P)


def phase_HA(nc, d):
    with ExitStack() as es:
        A = Alloc(nc, es)
        P = Prog(nc)
        W = A.sb("W", [128, 8, 1024], BF16)
        MA = A.sb("MA", [128, 512], F32)
        CA = A.sb("CA", [128, 16], F32)
        I = A.sb("I", [128, 128], BF16)
        Sst = A.sb("Sst", [128, 256], F32)
        S0b = A.sb("S0b", [128, 4, 256], BF16)
        hTs = [A.sb("hTs%d" % i, [128, 8, ST], BF16) for i in range(2)]
        Ct = [A.sb("Ct%d" % i, [128, ST], F32) for i in range(2)]
        Sn = [A.sb("Sn%d" % i, [128, ST], F32) for i in range(2)]
        t1 = A.sb("t1", [128, ST], F32)
        t2 = A.sb("t2", [128, ST], F32)
        qT = A.sb("qT", [128, ST], BF16)
        kT = A.sb("kT", [128, ST], BF16)
        ktok = A.sb("ktok", [128, 4, 128], BF16)
        vt = A.sb("vt", [128, 4, 256], BF16)
        sg = A.sb("sg", [128, 4, 256], BF16)
        sTb = [A.sb("sTb%d" % b, [128, (4 - b) * 128], BF16) for b in range(4)]
        on = A.sb("on", [128, 4, 256], F32)
        og = A.sb("og", [128, 4, 256], BF16)
        ogT = [A.sb("ogT%d" % i, [128, 2, ST], BF16) for i in range(2)]
        stats = A.sb("stats", [128, 4, 6], F32)
        mv = A.sb("mv", [128, 4, 2], F32)
        rs = A.sb("rs", [128, 4], F32)
        nbv = A.sb("nbv", [128, 4], F32)
        pj = [A.ps("pj%d" % i, [128, 512], F32) for i in range(2)]
        sc = [A.ps("sc%d" % i, [128, 512], F32) for i in range(2)]
        po = A.ps("po", [128, 4, 256], F32)
        pdS = A.ps("pdS", [128, 512], F32)
        ptr = A.ps("ptr", [128, 1024], BF16)

        _load_w_bf16(P, W, d["w"], 8, 0, 1024, "W")
        P.dma("sp", lambda h: h.dma_start(out=MA[:, :], in_=d["MA"][:, :]), "MA", writes=["MA"], tag="ldC")
        P.dma("sp", lambda h: h.dma_start(out=CA[:, :], in_=d["CA"][:, :]), "CA", writes=["CA"], tag="ldC")
        P.dma("sp", lambda h: h.dma_start(out=I[:, :], in_=d["ident"][:, :]), "I", writes=["I"], tag="ldC")

        def load(st):
            sl = st % 2
            sti = st % ST_PER_BATCH
            P.dma("sp", lambda h: h.dma_start(out=hTs[sl][:, :, :], in_=d["hT_all"][st]), ("hTs", sl), writes=[("hTs", sl)], tag="ldS")
            P.dma("sp", lambda h: h.dma_start(out=Ct[sl][:, :], in_=d["cos"][sti]), ("Ct", sl), writes=[("Ct", sl)], tag="ldS2")
            P.dma("sp", lambda h: h.dma_start(out=Sn[sl][:, :], in_=d["sin"][sti]), ("Sn", sl), writes=[("Sn", sl)], tag="ldS2")

        def proj_rope(sl, c0, dst, dname):
            for j in range(2):
                for kc in range(8):
                    P.op("pe", lambda h, j=j, kc=kc: h.matmul(pj[j][:, :], lhsT=W[:, kc, c0 + j * 128:c0 + (j + 1) * 128],
                                                             rhs=hTs[sl][:, kc, :], start=(kc == 0), stop=(kc == 7)),
                         reads=[("W", kc), ("hTs", sl)], writes=[("pj", j)])
            P.op("dve", lambda h: h.tensor_tensor(out=t1[:, :], in0=pj[0][:, :], in1=Ct[sl][:, :], op=ALU.mult),
                 reads=[("pj", 0), ("Ct", sl)], writes=["t1"])
            P.op("dve", lambda h: h.tensor_tensor(out=t2[:, :], in0=pj[1][:, :], in1=Sn[sl][:, :], op=ALU.mult),
                 reads=[("pj", 1), ("Sn", sl)], writes=["t2"])
            P.op(ELT, lambda h: h.tensor_tensor(out=dst[:, :], in0=t1[:, :], in1=t2[:, :], op=ALU.add),
                 reads=["t1", "t2"], writes=[dname])

        load(0)

        def step(st):
            sl = st % 2
            if st + 1 < NST:
                load(st + 1)
            if st % ST_PER_BATCH == 0:
                P.op("dve", lambda h: h.memset(Sst[:, :], 0.0), writes=["S"], tag="ms")
            P.cur_tag = "s0b"
            for a in range(4):
                P.op("act", lambda h, a=a: h.activation(out=S0b[:, a, :], in_=Sst[:, :], func=AF.Copy, scale=CA[:, 2 + a:3 + a]),
                     reads=["S", "CA"], writes=[("S0b", a)])
            P.cur_tag = "rope"
            proj_rope(sl, 0, qT, "qT")
            proj_rope(sl, 256, kT, "kT")
            P.cur_tag = "ktok"
            for a in range(4):
                P.op("pe", lambda h, a=a: h.transpose(out=ptr[:, a * 128:(a + 1) * 128], in_=kT[:, a * 128:(a + 1) * 128],
                                                      identity=I[:, :]),
                     reads=["kT", "I"], writes=["ptr"])
            P.op("dve", lambda h: h.tensor_tensor(out=ktok[:, :, :], in0=ptr[:, 0:512].rearrange("p (a t) -> p a t", a=4),
                                                  in1=CA[:, 6:10].unsqueeze(2).to_broadcast([128, 4, 128]), op=ALU.mult),
                 reads=["ptr", "CA"], writes=["ktok"])
            P.cur_tag = "vg"
            for a in range(4):
                j = a % 2
                for kc in range(8):
                    P.op("pe", lambda h, a=a, j=j, kc=kc: h.matmul(pj[j][:, :], lhsT=hTs[sl][:, kc, a * 128:(a + 1) * 128],
                                                                   rhs=W[:, kc, 512:1024], start=(kc == 0), stop=(kc == 7)),
                         reads=[("W", kc), ("hTs", sl)], writes=[("pj", j)])
                P.op("dve", lambda h, a=a, j=j: h.tensor_scalar_mul(out=vt[:, a, :], in0=pj[j][:, 0:256], scalar1=CA[:, 0:1]),
                     reads=[("pj", j), "CA"], writes=[("vt", a)])
                P.op("act", lambda h, a=a, j=j: h.activation(out=sg[:, a, :], in_=pj[j][:, 256:512], func=AF.Silu),
                     reads=[("pj", j), ("vt", a)], writes=[("sg", a)])
            P.cur_tag = "scr"
            place = {0: (0, 0), 1: (1, 0), 3: (1, 384), 2: (0, 0)}
            for b in (0, 1, 3, 2):
                n = (4 - b) * 128
                bank, off = place[b]
                P.op("pe", lambda h, b=b, n=n, bank=bank, off=off: h.matmul(
                    sc[bank][:, off:off + n], lhsT=kT[:, b * 128:(b + 1) * 128], rhs=qT[:, b * 128:512], start=True, stop=True),
                    reads=["kT", "qT"], writes=[("sc", bank)])
                P.op("dve", lambda h, b=b, n=n, bank=bank, off=off: h.tensor_tensor(
                    out=sTb[b][:, :], in0=sc[bank][:, off:off + n], in1=MA[:, 0:n], op=ALU.mult),
                    reads=[("sc", bank), "MA"], writes=[("sTb", b)])
            P.cur_tag = "dS"
            for b in range(4):
                P.op("pe", lambda h, b=b: h.matmul(pdS[:, 0:256], lhsT=ktok[:, b, :], rhs=vt[:, b, :], start=(b == 0), stop=(b == 3)),
                     reads=["ktok", ("vt", b)], writes=["pdS"])
            P.cur_tag = "po"
            for a in range(4):
                for b in range(a + 1):
                    P.op("pe", lambda h, a=a, b=b: h.matmul(po[:, a, :], lhsT=sTb[b][:, (a - b) * 128:(a - b + 1) * 128],
                                                            rhs=vt[:, b, :], start=(b == 0), stop=False),
                         reads=[("sTb", b), ("vt", b)], writes=[("po", a // 2)])
                P.op("pe", lambda h, a=a: h.matmul(po[:, a, :], lhsT=qT[:, a * 128:(a + 1) * 128], rhs=S0b[:, a, :],
                                                   start=False, stop=True),
                     reads=["qT", ("S0b", a)], writes=[("po", a // 2)])
            P.cur_tag = "Supd"
            P.op("dve", lambda h: h.scalar_tensor_tensor(out=Sst[:, :], in0=Sst[:, :], scalar=CA[:, 10:11], in1=pdS[:, 0:256],
                                                         op0=ALU.mult, op1=ALU.add),
                 reads=["S", "CA", "pdS"], writes=["S"])
            P.cur_tag = "gn"
            for a in range(4):
                P.op("dve", lambda h, a=a: h.bn_stats(out=stats[:, a, :], in_=po[:, a, :]), reads=[("po", a // 2)], writes=[("stats", a)])
                P.op("dve", lambda h, a=a: h.bn_aggr(out=mv[:, a, :], in_=stats[:, a, :]), reads=[("stats", a)], writes=["mv"])
            P.op("act", lambda h: h.activation(out=rs[:, :], in_=mv[:, :, 1], func=AF.Sqrt, bias=CA[:, 1:2], scale=1.0),
                 reads=["mv", "CA"], writes=["rs"])
            P.op("dve", lambda h: h.reciprocal(out=rs[:, :], in_=rs[:, :]), reads=["rs"], writes=["rs"])
            P.op("dve", lambda h: h.scalar_tensor_tensor(out=nbv[:, :], in0=mv[:, :, 0], scalar=-1.0, in1=rs[:, :],
                                                         op0=ALU.mult, op1=ALU.mult),
                 reads=["mv", "rs"], writes=["nbv"])
            P.cur_tag = "on"
            for a in range(4):
                P.op("act", lambda h, a=a: h.activation(out=on[:, a, :], in_=po[:, a, :], func=AF.Identity,
                                                        bias=nbv[:, a:a + 1], scale=rs[:, a:a + 1]),
                     reads=[("po", a // 2), "rs", "nbv"], writes=[("on", a)])
            P.cur_tag = "og"
            P.op(ELT, lambda h: h.tensor_tensor(out=og[:, :, :], in0=on[:, :, :], in1=sg[:, :, :], op=ALU.mult),
                 reads=[("on", a) for a in range(4)] + [("sg", a) for a in range(4)], writes=["og"])
            P.cur_tag = "ogT"
            for a in range(4):
                for c in range(2):
                    P.op("pe", lambda h, a=a, c=c: h.transpose(out=ptr[:, c * 512 + a * 128:c * 512 + (a + 1) * 128],
                                                               in_=og[:, a, c * 128:(c + 1) * 128], identity=I[:, :]),
                         reads=["og", "I"], writes=["ptr"])
            P.op("act", lambda h: h.copy(out=ogT[sl][:, :, :], in_=ptr[:, :].rearrange("p (c t) -> p c t", c=2)),
                 reads=["ptr"], writes=[("ogT", sl)])
            P.cur_tag = None
            P.dma("pool", lambda h, st=st: h.dma_start(out=d["og_out"][st], in_=ogT[sl][:, :, :]), ("ogo", sl),
                  reads=[("ogT", sl)], writes=[("ogd", st)], tag="stO")

        for st in range(d.get("nst", NST)):
            step(st)
        _finish(P)


def phase_HB(nc, d):
    lam_init = d["lam_init"]
    with ExitStack() as es:
        A = Alloc(nc, es)
        P = Prog(nc)
        KT = A.sb("KT", [128, NTOK], BF16)
        V = A.sb("V", [128, NTOK // 128, 128], BF16)
        WB = A.sb("WB", [128, 8, 384], BF16)
        WK = A.sb("WK", [128, 8, 384], BF16)
        ones = A.sb("ones", [128, 128], BF16)
        onesF = A.sb("onesF", [128, 128], F32)
        lamv = A.sb("lamv", [128, 2, 2, 64], F32)
        lamt = A.sb("lamt", [128, 2, 64], F32)
        lams = A.sb("lams", [128, 4], F32)
        subg = A.sb("subg", [128, 2], F32)
        hTs = [A.sb("hTs%d" % i, [128, 8, ST], BF16) for i in range(2)]
        Ct = [A.sb("Ct%d" % i, [128, ST], F32) for i in range(2)]
        Sn = [A.sb("Sn%d" % i, [128, ST], F32) for i in range(2)]
        t1 = A.sb("t1", [128, ST], F32)
        t2 = A.sb("t2", [128, ST], F32)
        qT = A.sb("qT", [128, ST], BF16)
        sgT = A.sb("sgT", [128, ST], F32)
        PT = [A.sb("PT%d" % i, [128, 2, ST], BF16) for i in range(2)]
        r1 = A.sb("r1", [128, ST], F32)
        r2 = A.sb("r2", [128, ST], F32)
        a1 = A.sb("a1", [128, ST], F32)
        a2 = A.sb("a2", [128, ST], F32)
        sq = A.sb("sq", [128, ST], F32)
        ogT = [A.sb("ogT%d" % i, [128, 1, ST], BF16) for i in range(2)]
        sc = [A.ps("sc%d" % i, [128, 2, ST], F32) for i in range(2)]
        po = [A.ps("po%d" % i, [128, ST], F32) for i in range(2)]
        pl = [A.ps("pl%d" % i, [128, ST], F32) for i in range(2)]

        _load_w_bf16(P, WB, d["w"], 8, 0, 384, "WB")
        _load_w_bf16(P, WK, d["wkv"], 8, 0, 384, "WK")
        P.dma("sp", lambda h: h.dma_start(out=lamv[:, :, :, :], in_=d["lamv"][:, :, :, :]), "lamv", writes=["lamv"])
        P.dma("sp", lambda h: h.dma_start(out=subg[:, 0:1], in_=d["subg"][:, :]), "subg", writes=["subg"])
        P.op("dve", lambda h: h.memset(ones[:, :], 1.0), writes=["ones"])
        P.op("dve", lambda h: h.memset(onesF[:, :], 1.0), writes=["onesF"])
        P.op("dve", lambda h: h.tensor_tensor(out=lamt[:, :, :], in0=lamv[:, 0, :, :], in1=lamv[:, 1, :, :], op=ALU.mult),
             reads=["lamv"], writes=["lamt"])
        P.op("dve", lambda h: h.reduce_sum(out=lams[:, 0:2], in_=lamt[:, :, :], axis=AX.X), reads=["lamt"], writes=["lams"])
        P.op("act", lambda h: h.activation(out=lams[:, 0:2], in_=lams[:, 0:2], func=AF.Exp), reads=["lams"], writes=["lams"])
        P.op("dve", lambda h: h.tensor_tensor(out=lams[:, 2:3], in0=lams[:, 1:2], in1=lams[:, 0:1], op=ALU.subtract),
             reads=["lams"], writes=["lams"])
        P.op("dve", lambda h: h.tensor_scalar_add(out=lams[:, 3:4], in0=lams[:, 2:3], scalar1=-lam_init),
             reads=["lams"], writes=["neglam"])
        P.op("dve", lambda h: h.tensor_scalar_mul(out=subg[:, 1:2], in0=subg[:, 0:1], scalar1=1.0 - lam_init),
             reads=["subg"], writes=["subgs"])

        def load(src, st, with_h=True):
            sl = st % 2
            sti = st % ST_PER_BATCH
            P.dma("sp", lambda h: h.dma_start(out=hTs[sl][:, :, :], in_=src[st]), ("hTs", sl), writes=[("hTs", sl)])
            P.dma("sp", lambda h: h.dma_start(out=Ct[sl][:, :], in_=d["cos"][sti]), ("Ct", sl), writes=[("Ct", sl)])
            P.dma("sp", lambda h: h.dma_start(out=Sn[sl][:, :], in_=d["sin"][sti]), ("Sn", sl), writes=[("Sn", sl)])

        def proj_rope(Wt, wkey, sl, dst_ap, dname, bank):
            for j in range(2):
                for kc in range(8):
                    P.op("pe", lambda h, j=j, kc=kc: h.matmul(sc[bank][:, j, :], lhsT=Wt[:, kc, j * 128:(j + 1) * 128],
                                                             rhs=hTs[sl][:, kc, :], start=(kc == 0), stop=(kc == 7)),
                         reads=[(wkey, kc), ("hTs", sl)], writes=[("sc", bank)])
            P.op("dve", lambda h: h.tensor_tensor(out=t1[:, :], in0=sc[bank][:, 0, :], in1=Ct[sl][:, :], op=ALU.mult),
                 reads=[("sc", bank), ("Ct", sl)], writes=["t1"])
            P.op("dve", lambda h: h.tensor_tensor(out=t2[:, :], in0=sc[bank][:, 1, :], in1=Sn[sl][:, :], op=ALU.mult),
                 reads=[("sc", bank), ("Sn", sl)], writes=["t2"])
            P.op(ELT, lambda h: h.tensor_tensor(out=dst_ap, in0=t1[:, :], in1=t2[:, :], op=ALU.add),
                 reads=["t1", "t2"], writes=[dname])

        nst = d.get("nst", NST)
        load(d["hTkv_all"], 0)

        def kv_step(st):
            sl = st % 2
            if st + 1 < nst:
                load(d["hTkv_all"], st + 1)
            else:
                load(d["hT_all"], 0)
            proj_rope(WK, "WK", sl, KT[:, st * ST:(st + 1) * ST], ("KT", st), 0)
            for a in range(4):
                for kc in range(8):
                    P.op("pe", lambda h, a=a, kc=kc: h.matmul(sc[1][:, 0, a * 128:(a + 1) * 128], lhsT=hTs[sl][:, kc, a * 128:(a + 1) * 128],
                                                              rhs=WK[:, kc, 256:384], start=(kc == 0), stop=(kc == 7)),
                         reads=[("WK", kc), ("hTs", sl)], writes=[("sc", 1)])
            P.op("act", lambda h: h.copy(out=V[:, st * 4:(st + 1) * 4, :], in_=sc[1][:, 0, :].rearrange("p (a e) -> p a e", a=4)),
                 reads=[("sc", 1)], writes=[("V", st)])

        for st in range(nst):
            kv_step(st)

        def group(g):
            sl = g % 2
            gi = g % ST_PER_BATCH
            bb = g // ST_PER_BATCH
            if g + 1 < nst:
                load(d["hT_all"], g + 1)
            proj_rope(WB, "WB", sl, qT[:, :], "qT", 0)
            for kc in range(8):
                P.op("pe", lambda h, kc=kc: h.matmul(sc[1][:, 0, :], lhsT=WB[:, kc, 256:384], rhs=hTs[sl][:, kc, :],
                                                     start=(kc == 0), stop=(kc == 7)),
                     reads=[("WB", kc), ("hTs", sl)], writes=[("sc", 1)])
            P.op("act", lambda h: h.activation(out=sgT[:, :], in_=sc[1][:, 0, :], func=AF.Silu), reads=[("sc", 1)], writes=["sgT"])
            nk = 4 * (gi + 1)

            def scores(kt):
                p = kt % 2
                T = bb * (S // 128) + kt
                q0 = max(0, kt - 4 * gi) * 128
                for t in range(2):
                    P.op("pe", lambda h, t=t: h.matmul(sc[p][:, t, q0:ST], lhsT=KT[t * 64:(t + 1) * 64, T * 128:(T + 1) * 128],
                                                       rhs=qT[t * 64:(t + 1) * 64, q0:ST], start=True, stop=True),
                         reads=[("KT", T // 4), "qT"], writes=[("sc", p)])
                P.op("act", lambda h: h.activation(out=PT[p][:, :, q0:ST], in_=sc[p][:, :, q0:ST], func=AF.Exp, scale=0.125),
                     reads=[("sc", p)], writes=[("PT", p)])
                if kt >= 4 * gi:
                    P.op(ELT, lambda h: h.memset(PT[p][64:128, :, q0:q0 + 64], 0.0), reads=[("PT", p)], writes=[("PT", p)])

            def pv(kt):
                p = kt % 2
                T = bb * (S // 128) + kt
                q0 = max(0, kt - 4 * gi) * 128
                for t in range(2):
                    P.op("pe", lambda h, t=t: h.matmul(po[t][:, q0:ST], lhsT=V[:, T, :], rhs=PT[p][:, t, q0:ST],
                                                       start=(kt == 0), stop=(kt == nk - 1)),
                         reads=[("V", T // 4), ("PT", p)], writes=[("po", t)])
                    P.op("pe", lambda h, t=t: h.matmul(pl[t][:, q0:ST], lhsT=ones[:, :], rhs=PT[p][:, t, q0:ST],
                                                       start=(kt == 0), stop=(kt == nk - 1)),
                         reads=["ones", ("PT", p)], writes=[("pl", t)])

            for kt in range(nk):
                scores(kt)
                if kt > 0:
                    pv(kt - 1)
            pv(nk - 1)
            P.op("dve", lambda h: h.reciprocal(out=r1[:, :], in_=pl[0][:, :]), reads=[("pl", 0)], writes=["r1"])
            P.op("dve", lambda h: h.reciprocal(out=r2[:, :], in_=pl[1][:, :]), reads=[("pl", 1)], writes=["r2"])
            P.op("dve", lambda h: h.tensor_tensor(out=a1[:, :], in0=po[0][:, :], in1=r1[:, :], op=ALU.mult),
                 reads=[("po", 0), "r1"], writes=["a1"])
            P.op("dve", lambda h: h.tensor_tensor(out=a2[:, :], in0=po[1][:, :], in1=r2[:, :], op=ALU.mult),
                 reads=[("po", 1), "r2"], writes=["a2"])
            P.op("dve", lambda h: h.scalar_tensor_tensor(out=a1[:, :], in0=a2[:, :], scalar=lams[:, 3:4], in1=a1[:, :],
                                                         op0=ALU.mult, op1=ALU.add),
                 reads=["a1", "a2", "neglam"], writes=["a1"])
            P.op(ELT, lambda h: h.tensor_tensor(out=sq[:, :], in0=a1[:, :], in1=a1[:, :], op=ALU.mult), reads=["a1"], writes=["sq"])
            P.op("pe", lambda h: h.matmul(sc[1][:, 1, :], lhsT=onesF[:, :], rhs=sq[:, :], start=True, stop=True),
                 reads=["onesF", "sq"], writes=[("sc", 1)])
            P.op("act", lambda h: h.activation(out=r1[:, :], in_=sc[1][:, 1, :], func=AF.Sqrt, scale=1.0 / 128, bias=EPS),
                 reads=[("sc", 1)], writes=["r1"])
            P.op("dve", lambda h: h.reciprocal(out=r1[:, :], in_=r1[:, :]), reads=["r1"], writes=["r1"])
            P.op("dve", lambda h: h.scalar_tensor_tensor(out=a2[:, :], in0=a1[:, :], scalar=subg[:, 1:2], in1=r1[:, :],
                                                         op0=ALU.mult, op1=ALU.mult),
                 reads=["a1", "subgs", "r1"], writes=["a2"])
            P.op(ELT, lambda h: h.tensor_tensor(out=ogT[sl][:, 0, :], in0=a2[:, :], in1=sgT[:, :], op=ALU.mult),
                 reads=["a2", "sgT"], writes=[("ogT", sl)])
            P.dma("pool", lambda h: h.dma_start(out=d["og_out"][g], in_=ogT[sl][:, :, :]), ("ogo", sl),
                  reads=[("ogT", sl)], writes=[("ogd", g)])

        for g in range(nst):
            group(g)
        _finish(P)


def _dram(nc, name, shape, dt, kind):
    return nc.dram_tensor(name, list(shape), dt, kind=kind).ap()


def _swap_cols(w, blk):
    k, n = w.shape
    return np.ascontiguousarray(w.reshape(k, n // blk, 2, blk // 2)[:, :, ::-1, :].reshape(k, n))


_IDENT = _bf16(np.eye(128, dtype=np.float32))


def _run(nc, in_maps):
    res = run_bass_kernel_spmd(nc, in_maps, core_ids=list(range(NCORES)))
    return res.results


def _build_T(x_in, x_out, t2_nkc, n_t1):
    nc = bass.Bass("TRN2", target_bir_lowering=False)
    d = {"ident": _dram(nc, "ident", [128, 128], BF16, "ExternalInput")}
    if x_in:
        d["x_in"] = _dram(nc, "x_in", [TOK_PER_CORE, D], F32, "ExternalInput")
    if t2_nkc:
        nch = t2_nkc // 8
        d["t2"] = dict(og=_dram(nc, "og", [8, ST_PER_CORE, 128, nch, ST], BF16, "ExternalInput"),
                       wout=_dram(nc, "wout", [t2_nkc * 128, D], F32, "ExternalInput"),
                       gpost=_dram(nc, "gpost", [128, D], F32, "ExternalInput"), nkc=t2_nkc)
    d["t1"] = [(_dram(nc, "g1_%d" % i, [128, D], F32, "ExternalInput"),
                _dram(nc, "hT_%d" % i, [ST_PER_CORE, 128, 8, ST], BF16, "ExternalOutput")) for i in range(n_t1)]
    if x_out:
        d["x_out"] = _dram(nc, "x_out", [TOK_PER_CORE, D], F32, "ExternalOutput")
    with nc.sbuf_tensor("X", [128, TILES_PER_CORE, D], F32) as X:
        phase_T(nc, X, d)
    return nc


def _build_HA(nst=NST):
    nc = bass.Bass("TRN2", target_bir_lowering=False)
    d = dict(hT_all=_dram(nc, "hT_all", [NST, 128, 8, ST], BF16, "ExternalInput"),
             w=_dram(nc, "w", [D, 1024], F32, "ExternalInput"),
             MA=_dram(nc, "MA", [128, 512], F32, "ExternalInput"),
             CA=_dram(nc, "CA", [128, 16], F32, "ExternalInput"),
             cos=_dram(nc, "cos", [ST_PER_BATCH, 128, ST], F32, "ExternalInput"),
             sin=_dram(nc, "sin", [ST_PER_BATCH, 128, ST], F32, "ExternalInput"),
             ident=_dram(nc, "ident", [128, 128], BF16, "ExternalInput"),
             og_out=_dram(nc, "og_out", [NST, 128, 2, ST], BF16, "ExternalOutput"), nst=nst)
    phase_HA(nc, d)
    return nc


def _build_HB(lam_init, nst=NST):
    nc = bass.Bass("TRN2", target_bir_lowering=False)
    d = dict(hT_all=_dram(nc, "hT_all", [NST, 128, 8, ST], BF16, "ExternalInput"),
             hTkv_all=_dram(nc, "hTkv_all", [NST, 128, 8, ST], BF16, "ExternalInput"),
             w=_dram(nc, "w", [D, 384], F32, "ExternalInput"),
             wkv=_dram(nc, "wkv", [D, 384], F32, "ExternalInput"),
             cos=_dram(nc, "cos", [ST_PER_BATCH, 128, ST], F32, "ExternalInput"),
             sin=_dram(nc, "sin", [ST_PER_BATCH, 128, ST], F32, "ExternalInput"),
             lamv=_dram(nc, "lamv", [128, 2, 2, 64], F32, "ExternalInput"),
             subg=_dram(nc, "subg", [128, 1], F32, "ExternalInput"),
             og_out=_dram(nc, "og_out", [NST, 128, 1, ST], BF16, "ExternalOutput"),
             lam_init=lam_init, nst=nst)
    phase_HB(nc, d)
    return nc


def _wA(w_in, h):
    q = w_in[:, h * 128:(h + 1) * 128]
    k = w_in[:, 1024 + h * 128:1024 + (h + 1) * 128]
    v = w_in[:, 2048 + h * 256:2048 + (h + 1) * 256]
    g = w_in[:, 4096 + h * 256:4096 + (h + 1) * 256]
    return np.ascontiguousarray(np.concatenate([q, _swap_cols(q, 128), k, _swap_cols(k, 128), v, g], axis=1))


def _wB(w_in, h):
    q = w_in[:, h * 128:(h + 1) * 128]
    g = w_in[:, 1024 + h * 128:1024 + (h + 1) * 128]
    return np.ascontiguousarray(np.concatenate([q, _swap_cols(q, 64), g], axis=1))


def _wKV(w_kv, h):
    k = w_kv[:, h * 128:(h + 1) * 128]
    v = w_kv[:, 1024 + h * 128:1024 + (h + 1) * 128]
    return np.ascontiguousarray(np.concatenate([k, _swap_cols(k, 64), v], axis=1))


def _gather_hT(res, key):
    return np.ascontiguousarray(np.concatenate([np.asarray(r[key]) for r in res], axis=0))


def _a2a(res, key):
    ogs = [np.asarray(r[key]) for r in res]
    return [np.ascontiguousarray(np.stack([ogs[h][ST_PER_CORE * r:ST_PER_CORE * (r + 1)] for h in range(8)], axis=0))
            for r in range(NCORES)]


def kernel(x, pre_norm, post_norm, w_in_a, w_out_a, kv_norm, w_kv, w_in_b,
           lam_q1, lam_k1, lam_q2, lam_k2, sub_norm_b, w_out_b):
    f = lambda a: np.asarray(a, dtype=np.float32)
    x, pre_norm, post_norm, w_in_a, w_out_a, kv_norm, w_kv, w_in_b = map(f, (x, pre_norm, post_norm, w_in_a, w_out_a, kv_norm, w_kv, w_in_b))
    lam_q1, lam_k1, lam_q2, lam_k2, sub_norm_b, w_out_b = map(f, (lam_q1, lam_k1, lam_q2, lam_k2, sub_norm_b, w_out_b))
    xs = np.split(np.ascontiguousarray(x.reshape(NTOK, D)), NCORES, axis=0)
    cosA, sinA = _rope_tables(64, 128)
    cosB, sinB = _rope_tables(32, 128)
    retc = [_ret_consts(h) for h in range(8)]

    nc = _build_T(True, False, 0, 1)
    res = _run(nc, [{"ident": _IDENT, "x_in": xs[r], "g1_0": _rep(pre_norm[0])} for r in range(NCORES)])
    hT = _gather_hT(res, "hT_0")
    ncHA = _build_HA()
    hTkv = None
    for layer in range(4):
        if layer < 2:
            res = _run(ncHA, [{"hT_all": hT, "w": _wA(w_in_a[layer], h), "MA": retc[h][0], "CA": retc[h][1],
                               "cos": cosA, "sin": sinA, "ident": _IDENT} for h in range(NCORES)])
            wout, nkc = w_out_a[layer], 16
        else:
            j = layer - 2
            lam_init = 0.8 - 0.6 * math.exp(-0.3 * layer)
            ncHB = _build_HB(lam_init)
            lamv = np.stack([np.stack([lam_q1[j], lam_q2[j]]), np.stack([lam_k1[j], lam_k2[j]])])
            lamv = np.ascontiguousarray(np.broadcast_to(lamv[None], (128, 2, 2, 64))).astype(np.float32)
            res = _run(ncHB, [{"hT_all": hT, "hTkv_all": hTkv, "w": _wB(w_in_b[j], h), "wkv": _wKV(w_kv, h),
                               "cos": cosB, "sin": sinB, "lamv": lamv,
                               "subg": np.ascontiguousarray(sub_norm_b[j].reshape(128, 1))} for h in range(NCORES)])
            wout, nkc = w_out_b[j], 8
        ogr = _a2a(res, "og_out")
        last = layer == 3
        n_t1 = 0 if last else (2 if layer == 1 else 1)
        ncT = _build_T(True, True, nkc, n_t1)
        maps = []
        for r in range(NCORES):
            m = {"ident": _IDENT, "x_in": xs[r], "og": ogr[r], "wout": np.ascontiguousarray(wout), "gpost": _rep(post_norm[layer])}
            if n_t1 >= 1:
                m["g1_0"] = _rep(pre_norm[layer + 1])
            if n_t1 == 2:
                m["g1_1"] = _rep(kv_norm)
            maps.append(m)
        res = _run(ncT, maps)
        xs = [np.asarray(r["x_out"]) for r in res]
        if n_t1 >= 1:
            hT = _gather_hT(res, "hT_0")
        if n_t1 == 2:
            hTkv = _gather_hT(res, "hT_1")
    return np.concatenate(xs, axis=0).reshape(B, S, D).astype(np.float32)
```

```python
import math
from contextlib import ExitStack

import numpy as np
import ml_dtypes

import concourse.bass as bass
import concourse.mybir as mybir
from concourse.bass_utils import run_bass_kernel_spmd

F32 = mybir.dt.float32
BF16 = mybir.dt.bfloat16
AF = mybir.ActivationFunctionType
ALU = mybir.AluOpType
AX = mybir.AxisListType

NCORES = 8
D = 1024
B = 2
S = 8192
NTOK = B * S
EPS = 1e-6
TOK_PER_CORE = NTOK // NCORES
TILES_PER_CORE = TOK_PER_CORE // 128
ST = 512
NST = NTOK // ST
ST_PER_CORE = TOK_PER_CORE // ST
ST_PER_BATCH = S // ST

SAME_ENGINE_SYNC = True


import os
SKIP = set(os.environ.get("KSKIP", "").split(","))
ELT = os.environ.get("KELT", "dve")
STQ = os.environ.get("KSTQ", "sp")


PSUM_KEYS = {"pj", "sc", "po", "pdS", "ptr", "Y", "psT", "pl"}


class Prog:
    ENGS = ("pe", "act", "dve", "pool", "sp")
    n_emit = 0

    def __init__(self, nc):
        self.nc = nc
        self.ops = {e: [] for e in self.ENGS}
        self.lastw = {}
        self.readers = {}
        self.dma_cnt = {}
        self.cur_tag = None
        self.ext = {}

    def _deps(self, eng, reads, writes):
        toks = set()
        for r in reads:
            if r in self.lastw:
                toks.add(self.lastw[r])
            if (r if isinstance(r, str) else r[0]) in PSUM_KEYS:
                for t in self.readers.get(r, ()):
                    if t[0] == "eng" and t[1] != eng:
                        toks.add(t)
        for w in writes:
            if w in self.lastw:
                toks.add(self.lastw[w])
            for t in self.readers.get(w, ()):
                toks.add(t)
        out = set()
        for t in toks:
            if t[0] == "eng" and t[1] == eng:
                if eng in ("pe", "sp") or not SAME_ENGINE_SYNC:
                    continue
            out.add(t)
        return out

    def _commit(self, tok, reads, writes):
        for r in reads:
            self.readers.setdefault(r, []).append(tok)
        for w in writes:
            self.lastw[w] = tok
            self.readers[w] = []

    def op(self, eng, fn, reads=(), writes=(), tag=None):
        tag = tag or self.cur_tag
        if tag is not None and tag in SKIP:
            return
        deps = self._deps(eng, reads, writes)
        idx = len(self.ops[eng])
        self.ops[eng].append(dict(fn=fn, deps=deps, dma=None))
        self._commit(("eng", eng, idx), reads, writes)

    def dma(self, eng, fn, semkey, reads=(), writes=(), tag=None, inc=16):
        if tag is not None and tag in SKIP:
            return
        deps = self._deps(eng, reads, writes)
        if semkey in self.ext:
            self.ext[semkey]["count"] += inc
            cnt = self.ext[semkey]["count"]
            self.dma_cnt.setdefault(semkey, 0)
        else:
            cnt = self.dma_cnt.get(semkey, 0) + inc
            self.dma_cnt[semkey] = cnt
        self.ops[eng].append(dict(fn=fn, deps=deps, dma=semkey, inc=inc))
        self._commit(("dma", semkey, cnt), reads, writes)

    def dma_multi(self, eng, fns, semkey, reads=(), writes_list=(), tag=None):
        if tag is not None and tag in SKIP:
            return
        final = self.dma_cnt.get(semkey, 0) + 16 * len(fns)
        for fn, w in zip(fns, writes_list):
            deps = self._deps(eng, reads, w)
            self.ops[eng].append(dict(fn=fn, deps=deps, dma=semkey))
            self._commit(("dma", semkey, final), reads, w)
        self.dma_cnt[semkey] = final

    def emit(self, final_waits=()):
        nc = self.nc
        self.op("sp", None, reads=tuple(final_waits))
        needed = set()
        for e in self.ENGS:
            for o in self.ops[e]:
                for t in o["deps"]:
                    if t[0] == "eng":
                        needed.add((t[1], t[2]))
        inc_count = {}
        for e in self.ENGS:
            c = 0
            for i, o in enumerate(self.ops[e]):
                if (e, i) in needed:
                    c += 1
                    inc_count[(e, i)] = c
        with ExitStack() as es:
            Prog.n_emit += 1
            esem = {e: es.enter_context(nc.semaphore("s%d_%s" % (Prog.n_emit, e))) for e in self.ENGS}
            dsem = {k: (self.ext[k]["sem"] if k in self.ext else es.enter_context(nc.semaphore("d%d_%d" % (Prog.n_emit, i))))
                    for i, k in enumerate(sorted(self.dma_cnt, key=str))}
            block = es.enter_context(nc.Block())

            def run(e, h):
                waited = {}
                for i, o in enumerate(self.ops[e]):
                    want = {}
                    for t in o["deps"]:
                        if t[0] == "eng":
                            k, v = ("e", t[1]), inc_count[(t[1], t[2])]
                        else:
                            k, v = ("d", t[1]), t[2]
                        if v > want.get(k, 0):
                            want[k] = v
                    for k, v in want.items():
                        if waited.get(k, 0) >= v:
                            continue
                        waited[k] = v
                        h.wait_ge(esem[k[1]] if k[0] == "e" else dsem[k[1]], v)
                    if o["fn"] is None:
                        continue
                    ins = o["fn"](h)
                    if o["dma"] is not None:
                        ins.then_inc(dsem[o["dma"]], o.get("inc", 16))
                    elif (e, i) in inc_count:
                        ins.then_inc(esem[e], 1)

            block.tensor(lambda h: run("pe", h))
            block.scalar(lambda h: run("act", h))
            block.vector(lambda h: run("dve", h))
            block.gpsimd(lambda h: run("pool", h))
            block.sync(lambda h: run("sp", h))


def _bf16(a):
    return np.asarray(a).astype(ml_dtypes.bfloat16)


def _rope_tables(half, dim_rows):
    inv = np.power(np.float32(10000.0), -np.arange(half, dtype=np.float32) / np.float32(half)).astype(np.float32)
    ang = (np.arange(S, dtype=np.float32)[:, None] * inv[None, :]).astype(np.float32)
    cos = np.cos(ang.astype(np.float64)).astype(np.float32)
    sin = np.sin(ang.astype(np.float64)).astype(np.float32)
    rows = np.arange(dim_rows)
    f = (rows % (2 * half)) % half
    sign = np.where((rows % (2 * half)) < half, -1.0, 1.0).astype(np.float32)
    C = cos[:, f].T
    Sg = (sin[:, f] * sign[None, :]).T
    C = np.ascontiguousarray(C.reshape(dim_rows, ST_PER_BATCH, ST).transpose(1, 0, 2))
    Sg = np.ascontiguousarray(Sg.reshape(dim_rows, ST_PER_BATCH, ST).transpose(1, 0, 2))
    return C.astype(np.float32), Sg.astype(np.float32)


def _ret_consts(h):
    lg = math.log1p(-2.0 ** (-5.0 - h))
    g = lambda e: math.exp(e * lg)
    s = 128.0 ** -0.5
    p = np.arange(128)
    MA = np.zeros((128, 4, 128), np.float64)
    MA[:, 0, :] = s * g(-128) * (p[:, None] <= p[None, :])
    for d in range(1, 4):
        MA[:, d, :] = s * g(128 * (d - 1))
    CA = np.zeros((128, 16), np.float64)
    CA[:, 0] = np.exp((127 - p) * lg)
    CA[:, 1] = EPS * np.exp(-2.0 * (p + 1) * lg)
    for a in range(4):
        CA[:, 2 + a] = g(128 * a)
        CA[:, 6 + a] = s * g(128 * (3 - a))
    CA[:, 10] = g(512)
    return MA.reshape(128, 512).astype(np.float32), CA.astype(np.float32)


def _rep(v, n=128):
    return np.ascontiguousarray(np.broadcast_to(np.asarray(v, np.float32).reshape(1, -1), (n, np.asarray(v).size)))


_PID_CACHE = {}


class Alloc:
    n = 0

    def __init__(self, nc, es):
        self.nc, self.es = nc, es
        Alloc.n += 1
        self.pfx = "p%d_" % Alloc.n

    def sb(self, name, shape, dt):
        return self.es.enter_context(self.nc.sbuf_tensor(self.pfx + "sb_" + name, list(shape), dt))

    def ps(self, name, shape, dt):
        return self.es.enter_context(self.nc.psum_tensor(self.pfx + "ps_" + name, list(shape), dt))


def _finish(P, extra=()):
    allres = list(P.lastw.keys())
    P.op("sp", lambda h: h.nop(), reads=allres, writes=["__done"])
    for e in ("pe", "act", "dve", "pool"):
        P.op(e, lambda h: h.nop(), reads=["__done"])
    P.emit(final_waits=allres)


def _load_w_bf16(P, A, Wsb, Wd, nkc, c0, c1, key):
    stg = [A.sb("%s_stg%d" % (key, i), [128, c1 - c0], F32) for i in range(2)]
    for kc in range(nkc):
        i = kc % 2
        P.dma("sp", lambda h, kc=kc, i=i: h.dma_start(out=stg[i][:, :], in_=Wd[kc * 128:(kc + 1) * 128, c0:c1]),
              (key + "_stg", i), writes=[(key + "_stg", i)], tag="ldW")
        if i == 0:
            P.op("act", lambda h, kc=kc, i=i: h.copy(out=Wsb[:, kc, c0:c1], in_=stg[i][:, :]),
                 reads=[(key + "_stg", i)], writes=[(key, kc)], tag="ldW")
        else:
            P.op("dve", lambda h, kc=kc, i=i: h.tensor_copy(out=Wsb[:, kc, c0:c1], in_=stg[i][:, :]),
                 reads=[(key + "_stg", i)], writes=[(key, kc)], tag="ldW")


def phase_T(nc, X, d):
    with ExitStack() as es:
        A = Alloc(nc, es)
        P = Prog(nc)
        t2 = d.get("t2")
        t1s = d.get("t1", [])
        I = A.sb("I", [128, 128], BF16)
        junk = A.sb("junk", [128, D], BF16)
        st_ = A.sb("stt", [128, 8], F32)
        P.dma("sp", lambda h: h.dma_start(out=I[:, :], in_=d["ident"][:, :]), "I", writes=["I"])
        if t2:
            nkc = t2["nkc"]
            Wo = A.sb("Wo", [128, nkc, D], BF16)
            Gp = A.sb("Gp", [128, D], F32)
            OG = [A.sb("OG%d" % i, [128, nkc, ST], BF16) for i in range(2)]
            tmp = A.sb("tmp", [128, D], F32)
            Y = [A.ps("Y%d" % i, [128, D], F32) for i in range(2)]
            _load_w_bf16(P, A, Wo, t2["wout"], nkc, 0, D, "Wo")
            P.dma("sp", lambda h: h.dma_start(out=Gp[:, :], in_=t2["gpost"][:, :]), "Gp", writes=["Gp"])
            nch = nkc // 8
        G1 = []
        for i, (g, _) in enumerate(t1s):
            Gt = A.sb("G1_%d" % i, [128, D], F32)
            P.dma("sp", lambda h, Gt=Gt, g=g: h.dma_start(out=Gt[:, :], in_=g[:, :]), "G1_%d" % i, writes=["G1_%d" % i])
            G1.append(Gt)
        if t1s:
            hb = A.sb("hb", [128, D], BF16)
            psT = A.ps("psT", [128, 8, 128], BF16)
            hTs = [[A.sb("hTs%d_%d" % (i, k), [128, 8, ST], BF16) for k in range(2)] for i in range(len(t1s))]

        pidc = {}

        def load_og(s):
            sl = s % 2
            if t2.get("dyn"):
                def ld(h, sl=sl, s=s):
                    if "pid" not in pidc:
                        pidc["pid"] = h.partition_id()
                    src = t2["og"][:, bass.ds(pidc["pid"] * ST_PER_CORE + s, 1)]
                    return h.dma_start(out=OG[sl][:, :, :].rearrange("p (h c) t -> p h c t", h=8),
                                       in_=src.rearrange("h o p c t -> p (h o) c t"))
                P.dma(t2.get("dynq", "sp"), ld, ("OG", sl), reads=["og_dram"], writes=[("OG", sl, hh) for hh in range(8)])
            else:
                P.dma_multi("sp", [lambda h, hh=hh, sl=sl, s=s: h.dma_start(
                    out=OG[sl][:, hh * nch:(hh + 1) * nch, :], in_=t2["og"][hh, s]) for hh in range(8)],
                    ("OG", sl), writes_list=[[("OG", sl, hh)] for hh in range(8)])

        if d.get("x_in") is not None:
            for grp in range(4):
                P.dma_multi("sp", [lambda h, lt=lt: h.dma_start(out=X[:, lt, :], in_=d["x_in"][lt * 128:(lt + 1) * 128, :])
                                   for lt in range(grp * 4, grp * 4 + 4)], ("Xin", grp),
                            writes_list=[[("X", lt)] for lt in range(grp * 4, grp * 4 + 4)])
        if t2:
            load_og(0)
        def stage1(lt):
            s, a = lt // 4, lt % 4
            if t2:
                if a == 0 and s + 1 < ST_PER_CORE:
                    load_og(s + 1)
                sl = s % 2
                y = Y[lt % 2]
                for nb in range(2):
                    for kc in range(nkc):
                        P.op("pe", lambda h, y=y, nb=nb, kc=kc, sl=sl, a=a: h.matmul(
                            y[:, nb * 512:(nb + 1) * 512], lhsT=OG[sl][:, kc, a * 128:(a + 1) * 128],
                            rhs=Wo[:, kc, nb * 512:(nb + 1) * 512], start=(kc == 0), stop=(kc == nkc - 1)),
                            reads=[("OG", sl, kc // nch), ("Wo", kc)], writes=[("Y", lt % 2)], tag="t2mm")

        def stage1_post(lt):
            if t2:
                y = Y[lt % 2]
                P.op("act", lambda h, y=y: h.activation(out=junk[:, :], in_=y[:, :], func=AF.Square, accum_out=st_[:, 0:1]),
                     reads=[("Y", lt % 2)], writes=["junk", "ssq"], tag="t2sq")
                P.op("act", lambda h: h.activation(out=st_[:, 1:2], in_=st_[:, 0:1], func=AF.Sqrt, scale=1.0 / D, bias=EPS),
                     reads=["ssq"], writes=["rstd"])
                P.op("dve", lambda h: h.reciprocal(out=st_[:, 1:2], in_=st_[:, 1:2]), reads=["rstd"], writes=["rstd"])
                P.op("dve", lambda h, y=y: h.scalar_tensor_tensor(out=tmp[:, :], in0=y[:, :], scalar=st_[:, 1:2], in1=Gp[:, :],
                                                                  op0=ALU.mult, op1=ALU.mult),
                     reads=[("Y", lt % 2), "rstd", "Gp"], writes=["tmp"], tag="t2stt")
                P.op(ELT, lambda h, lt=lt: h.tensor_tensor(out=X[:, lt, :], in0=X[:, lt, :], in1=tmp[:, :], op=ALU.add),
                     reads=[("X", lt), "tmp"], writes=[("X", lt)], tag="t2add")
            if d.get("x_out") is not None:
                P.dma(STQ, lambda h, lt=lt: h.dma_start(out=d["x_out"][lt * 128:(lt + 1) * 128, :], in_=X[:, lt, :]),
                      ("Xout", lt % 4), reads=[("X", lt)], writes=[("xo", lt)])

        def stage2(lt):
            s, a = lt // 4, lt % 4
            for i, (g, hT_out) in enumerate(t1s):
                sl = s % 2
                P.op("act", lambda h, lt=lt: h.activation(out=junk[:, :], in_=X[:, lt, :], func=AF.Square, accum_out=st_[:, 2:3]),
                     reads=[("X", lt)], writes=["junk", "ssq1"])
                P.op("act", lambda h: h.activation(out=st_[:, 3:4], in_=st_[:, 2:3], func=AF.Sqrt, scale=1.0 / D, bias=EPS),
                     reads=["ssq1"], writes=["rstd1"])
                P.op("dve", lambda h: h.reciprocal(out=st_[:, 3:4], in_=st_[:, 3:4]), reads=["rstd1"], writes=["rstd1"])
                P.op("dve", lambda h, lt=lt, i=i: h.scalar_tensor_tensor(out=hb[:, :], in0=X[:, lt, :], scalar=st_[:, 3:4],
                                                                         in1=G1[i][:, :], op0=ALU.mult, op1=ALU.mult),
                     reads=[("X", lt), "rstd1", "G1_%d" % i], writes=["hb"])
                for kc in range(8):
                    P.op("pe", lambda h, kc=kc: h.transpose(out=psT[:, kc, :], in_=hb[:, kc * 128:(kc + 1) * 128], identity=I[:, :]),
                         reads=["hb", "I"], writes=["psT"])
                P.op("act", lambda h, i=i, sl=sl, a=a: h.copy(out=hTs[i][sl][:, :, a * 128:(a + 1) * 128], in_=psT[:, :, :]),
                     reads=["psT"], writes=[("hTs", i, sl)])
                if a == 3:
                    P.dma(STQ, lambda h, i=i, sl=sl, s=s, hT_out=hT_out: h.dma_start(out=hT_out[s], in_=hTs[i][sl][:, :, :]),
                          ("hTo", i, sl), reads=[("hTs", i, sl)], writes=[("hTd", i, s)])

        stage1(0)
        stage1_post(0)
        for lt in range(TILES_PER_CORE):
            if lt + 1 < TILES_PER_CORE:
                stage1(lt + 1)
            stage2(lt)
            if lt + 1 < TILES_PER_CORE:
                stage1_post(lt + 1)
        _finish(P)


def phase_HA(nc, d):
    with ExitStack() as es:
        A = Alloc(nc, es)
        P = Prog(nc)
        W = A.sb("W", [128, 8, 1024], BF16)
        MA = A.sb("MA", [128, 512], F32)
        CA = A.sb("CA", [128, 16], F32)
        I = A.sb("I", [128, 128], BF16)
        Sst = A.sb("Sst", [128, 256], F32)
        S0b = A.sb("S0b", [128, 4, 256], BF16)
        hTs = [A.sb("hTs%d" % i, [128, 8, ST], BF16) for i in range(2)]
        Ct = [A.sb("Ct%d" % i, [128, ST], F32) for i in range(2)]
        Sn = [A.sb("Sn%d" % i, [128, ST], F32) for i in range(2)]
        t1 = A.sb("t1", [128, ST], F32)
        t2 = A.sb("t2", [128, ST], F32)
        qT = A.sb("qT", [128, ST], BF16)
        kT = A.sb("kT", [128, ST], BF16)
        ktok = A.sb("ktok", [128, 4, 128], BF16)
        vt = A.sb("vt", [128, 4, 256], BF16)
        sg = A.sb("sg", [128, 4, 256], BF16)
        sTb = [A.sb("sTb%d" % b, [128, (4 - b) * 128], BF16) for b in range(4)]
        on = A.sb("on", [128, 4, 256], F32)
        og = A.sb("og", [128, 4, 256], BF16)
        ogT = [A.sb("ogT%d" % i, [128, 2, ST], BF16) for i in range(2)]
        stats = A.sb("stats", [128, 4, 6], F32)
        mv = A.sb("mv", [128, 4, 2], F32)
        rs = A.sb("rs", [128, 4], F32)
        nbv = A.sb("nbv", [128, 4], F32)
        pj = [A.ps("pj%d" % i, [128, 512], F32) for i in range(2)]
        sc = [A.ps("sc%d" % i, [128, 512], F32) for i in range(2)]
        po = A.ps("po", [128, 4, 256], F32)
        pdS = A.ps("pdS", [128, 512], F32)
        ptr = A.ps("ptr", [128, 1024], BF16)

        _load_w_bf16(P, A, W, d["w"], 8, 0, 1024, "W")
        P.dma("sp", lambda h: h.dma_start(out=MA[:, :], in_=d["MA"][:, :]), "MA", writes=["MA"], tag="ldC")
        P.dma("sp", lambda h: h.dma_start(out=CA[:, :], in_=d["CA"][:, :]), "CA", writes=["CA"], tag="ldC")
        P.dma("sp", lambda h: h.dma_start(out=I[:, :], in_=d["ident"][:, :]), "I", writes=["I"], tag="ldC")

        def load(st):
            sl = st % 2
            sti = st % ST_PER_BATCH
            P.dma("sp", lambda h: h.dma_start(out=hTs[sl][:, :, :], in_=d["hT_all"][st]), ("hTs", sl), writes=[("hTs", sl)], tag="ldS")
            P.dma("sp", lambda h: h.dma_start(out=Ct[sl][:, :], in_=d["cos"][sti]), ("Ct", sl), writes=[("Ct", sl)], tag="ldS2")
            P.dma("sp", lambda h: h.dma_start(out=Sn[sl][:, :], in_=d["sin"][sti]), ("Sn", sl), writes=[("Sn", sl)], tag="ldS2")

        def proj_rope(sl, c0, dst, dname):
            for j in range(2):
                for kc in range(8):
                    P.op("pe", lambda h, j=j, kc=kc: h.matmul(pj[j][:, :], lhsT=W[:, kc, c0 + j * 128:c0 + (j + 1) * 128],
                                                             rhs=hTs[sl][:, kc, :], start=(kc == 0), stop=(kc == 7)),
                         reads=[("W", kc), ("hTs", sl)], writes=[("pj", j)])
            P.op("dve", lambda h: h.tensor_tensor(out=t1[:, :], in0=pj[0][:, :], in1=Ct[sl][:, :], op=ALU.mult),
                 reads=[("pj", 0), ("Ct", sl)], writes=["t1"])
            P.op("dve", lambda h: h.tensor_tensor(out=t2[:, :], in0=pj[1][:, :], in1=Sn[sl][:, :], op=ALU.mult),
                 reads=[("pj", 1), ("Sn", sl)], writes=["t2"])
            P.op(ELT, lambda h: h.tensor_tensor(out=dst[:, :], in0=t1[:, :], in1=t2[:, :], op=ALU.add),
                 reads=["t1", "t2"], writes=[dname])

        load(0)

        def step(st):
            sl = st % 2
            if st + 1 < NST:
                load(st + 1)
            if st % ST_PER_BATCH == 0:
                P.op("dve", lambda h: h.memset(Sst[:, :], 0.0), writes=["S"], tag="ms")
            P.cur_tag = "s0b"
            for a in range(4):
                P.op("act", lambda h, a=a: h.activation(out=S0b[:, a, :], in_=Sst[:, :], func=AF.Copy, scale=CA[:, 2 + a:3 + a]),
                     reads=["S", "CA"], writes=[("S0b", a)])
            P.cur_tag = "rope"
            proj_rope(sl, 0, qT, "qT")
            proj_rope(sl, 256, kT, "kT")
            P.cur_tag = "ktok"
            for a in range(4):
                P.op("pe", lambda h, a=a: h.transpose(out=ptr[:, a * 128:(a + 1) * 128], in_=kT[:, a * 128:(a + 1) * 128],
                                                      identity=I[:, :]),
                     reads=["kT", "I"], writes=["ptr"])
            P.op("dve", lambda h: h.tensor_tensor(out=ktok[:, :, :], in0=ptr[:, 0:512].rearrange("p (a t) -> p a t", a=4),
                                                  in1=CA[:, 6:10].unsqueeze(2).to_broadcast([128, 4, 128]), op=ALU.mult),
                 reads=["ptr", "CA"], writes=["ktok"])
            P.cur_tag = "vg"
            for a in range(4):
                j = a % 2
                for kc in range(8):
                    P.op("pe", lambda h, a=a, j=j, kc=kc: h.matmul(pj[j][:, :], lhsT=hTs[sl][:, kc, a * 128:(a + 1) * 128],
                                                                   rhs=W[:, kc, 512:1024], start=(kc == 0), stop=(kc == 7)),
                         reads=[("W", kc), ("hTs", sl)], writes=[("pj", j)])
                P.op("dve", lambda h, a=a, j=j: h.tensor_scalar_mul(out=vt[:, a, :], in0=pj[j][:, 0:256], scalar1=CA[:, 0:1]),
                     reads=[("pj", j), "CA"], writes=[("vt", a)])
                P.op("act", lambda h, a=a, j=j: h.activation(out=sg[:, a, :], in_=pj[j][:, 256:512], func=AF.Silu),
                     reads=[("pj", j), ("vt", a)], writes=[("sg", a)])
            P.cur_tag = "scr"
            place = {0: (0, 0), 1: (1, 0), 3: (1, 384), 2: (0, 0)}
            for b in (0, 1, 3, 2):
                n = (4 - b) * 128
                bank, off = place[b]
                P.op("pe", lambda h, b=b, n=n, bank=bank, off=off: h.matmul(
                    sc[bank][:, off:off + n], lhsT=kT[:, b * 128:(b + 1) * 128], rhs=qT[:, b * 128:512], start=True, stop=True),
                    reads=["kT", "qT"], writes=[("sc", bank)])
                P.op("dve", lambda h, b=b, n=n, bank=bank, off=off: h.tensor_tensor(
                    out=sTb[b][:, :], in0=sc[bank][:, off:off + n], in1=MA[:, 0:n], op=ALU.mult),
                    reads=[("sc", bank), "MA"], writes=[("sTb", b)])
            P.cur_tag = "dS"
            for b in range(4):
                P.op("pe", lambda h, b=b: h.matmul(pdS[:, 0:256], lhsT=ktok[:, b, :], rhs=vt[:, b, :], start=(b == 0), stop=(b == 3)),
                     reads=["ktok", ("vt", b)], writes=["pdS"])
            P.cur_tag = "po"
            for a in range(4):
                for b in range(a + 1):
                    P.op("pe", lambda h, a=a, b=b: h.matmul(po[:, a, :], lhsT=sTb[b][:, (a - b) * 128:(a - b + 1) * 128],
                                                            rhs=vt[:, b, :], start=(b == 0), stop=False),
                         reads=[("sTb", b), ("vt", b)], writes=[("po", a // 2)])
                P.op("pe", lambda h, a=a: h.matmul(po[:, a, :], lhsT=qT[:, a * 128:(a + 1) * 128], rhs=S0b[:, a, :],
                                                   start=False, stop=True),
                     reads=["qT", ("S0b", a)], writes=[("po", a // 2)])
            P.cur_tag = "Supd"
            P.op("dve", lambda h: h.scalar_tensor_tensor(out=Sst[:, :], in0=Sst[:, :], scalar=CA[:, 10:11], in1=pdS[:, 0:256],
                                                         op0=ALU.mult, op1=ALU.add),
                 reads=["S", "CA", "pdS"], writes=["S"])
            P.cur_tag = "gn"
            for a in range(4):
                P.op("dve", lambda h, a=a: h.bn_stats(out=stats[:, a, :], in_=po[:, a, :]), reads=[("po", a // 2)], writes=[("stats", a)])
                P.op("dve", lambda h, a=a: h.bn_aggr(out=mv[:, a, :], in_=stats[:, a, :]), reads=[("stats", a)], writes=["mv"])
            P.op("act", lambda h: h.activation(out=rs[:, :], in_=mv[:, :, 1], func=AF.Sqrt, bias=CA[:, 1:2], scale=1.0),
                 reads=["mv", "CA"], writes=["rs"])
            P.op("dve", lambda h: h.reciprocal(out=rs[:, :], in_=rs[:, :]), reads=["rs"], writes=["rs"])
            P.op("dve", lambda h: h.scalar_tensor_tensor(out=nbv[:, :], in0=mv[:, :, 0], scalar=-1.0, in1=rs[:, :],
                                                         op0=ALU.mult, op1=ALU.mult),
                 reads=["mv", "rs"], writes=["nbv"])
            P.cur_tag = "on"
            for a in range(4):
                P.op("act", lambda h, a=a: h.activation(out=on[:, a, :], in_=po[:, a, :], func=AF.Identity,
                                                        bias=nbv[:, a:a + 1], scale=rs[:, a:a + 1]),
                     reads=[("po", a // 2), "rs", "nbv"], writes=[("on", a)])
            P.cur_tag = "og"
            P.op(ELT, lambda h: h.tensor_tensor(out=og[:, :, :], in0=on[:, :, :], in1=sg[:, :, :], op=ALU.mult),
                 reads=[("on", a) for a in range(4)] + [("sg", a) for a in range(4)], writes=["og"])
            P.cur_tag = "ogT"
            for a in range(4):
                for c in range(2):
                    P.op("pe", lambda h, a=a, c=c: h.transpose(out=ptr[:, c * 512 + a * 128:c * 512 + (a + 1) * 128],
                                                               in_=og[:, a, c * 128:(c + 1) * 128], identity=I[:, :]),
                         reads=["og", "I"], writes=["ptr"])
            P.op("act", lambda h: h.copy(out=ogT[sl][:, :, :], in_=ptr[:, :].rearrange("p (c t) -> p c t", c=2)),
                 reads=["ptr"], writes=[("ogT", sl)])
            P.cur_tag = None
            P.dma(STQ, lambda h, st=st: h.dma_start(out=d["og_out"][st], in_=ogT[sl][:, :, :]), ("ogo", sl),
                  reads=[("ogT", sl)], writes=[("ogd", st)], tag="stO")

        for st in range(d.get("nst", NST)):
            step(st)
        _finish(P)


def phase_HB(nc, d):
    lam_init = d["lam_init"]
    with ExitStack() as es:
        A = Alloc(nc, es)
        P = Prog(nc)
        KT = A.sb("KT", [128, NTOK], BF16)
        V = A.sb("V", [128, NTOK // 128, 128], BF16)
        WB = A.sb("WB", [128, 8, 384], BF16)
        WK = A.sb("WK", [128, 8, 384], BF16)
        ones = A.sb("ones", [128, 128], BF16)
        onesF = A.sb("onesF", [128, 128], F32)
        lamv = A.sb("lamv", [128, 2, 2, 64], F32)
        lamt = A.sb("lamt", [128, 2, 64], F32)
        lams = A.sb("lams", [128, 4], F32)
        subg = A.sb("subg", [128, 2], F32)
        hTs = [A.sb("hTs%d" % i, [128, 8, ST], BF16) for i in range(2)]
        Ct = [A.sb("Ct%d" % i, [128, ST], F32) for i in range(2)]
        Sn = [A.sb("Sn%d" % i, [128, ST], F32) for i in range(2)]
        t1 = A.sb("t1", [128, ST], F32)
        t2 = A.sb("t2", [128, ST], F32)
        qT = A.sb("qT", [128, ST], BF16)
        sgT = A.sb("sgT", [128, ST], F32)
        PT = [A.sb("PT%d" % i, [128, 2, ST], BF16) for i in range(2)]
        r1 = A.sb("r1", [128, ST], F32)
        r2 = A.sb("r2", [128, ST], F32)
        a1 = A.sb("a1", [128, ST], F32)
        a2 = A.sb("a2", [128, ST], F32)
        sq = A.sb("sq", [128, ST], F32)
        ogT = [A.sb("ogT%d" % i, [128, 1, ST], BF16) for i in range(2)]
        sc = [A.ps("sc%d" % i, [128, 2, ST], F32) for i in range(2)]
        po = [A.ps("po%d" % i, [128, ST], F32) for i in range(2)]
        pl = [A.ps("pl%d" % i, [128, ST], F32) for i in range(2)]

        _load_w_bf16(P, A, WB, d["w"], 8, 0, 384, "WB")
        _load_w_bf16(P, A, WK, d["wkv"], 8, 0, 384, "WK")
        P.dma("sp", lambda h: h.dma_start(out=lamv[:, :, :, :], in_=d["lamv"][:, :, :, :]), "lamv", writes=["lamv"])
        P.dma("sp", lambda h: h.dma_start(out=subg[:, 0:1], in_=d["subg"][:, :]), "subg", writes=["subg"])
        P.op("dve", lambda h: h.memset(ones[:, :], 1.0), writes=["ones"])
        P.op("dve", lambda h: h.memset(onesF[:, :], 1.0), writes=["onesF"])
        P.op("dve", lambda h: h.tensor_tensor(out=lamt[:, :, :], in0=lamv[:, 0, :, :], in1=lamv[:, 1, :, :], op=ALU.mult),
             reads=["lamv"], writes=["lamt"])
        P.op("dve", lambda h: h.reduce_sum(out=lams[:, 0:2], in_=lamt[:, :, :], axis=AX.X), reads=["lamt"], writes=["lams"])
        P.op("act", lambda h: h.activation(out=lams[:, 0:2], in_=lams[:, 0:2], func=AF.Exp), reads=["lams"], writes=["lams"])
        P.op("dve", lambda h: h.tensor_tensor(out=lams[:, 2:3], in0=lams[:, 1:2], in1=lams[:, 0:1], op=ALU.subtract),
             reads=["lams"], writes=["lams"])
        P.op("dve", lambda h: h.tensor_scalar_add(out=lams[:, 3:4], in0=lams[:, 2:3], scalar1=-lam_init),
             reads=["lams"], writes=["neglam"])
        P.op("dve", lambda h: h.tensor_scalar_mul(out=subg[:, 1:2], in0=subg[:, 0:1], scalar1=1.0 - lam_init),
             reads=["subg"], writes=["subgs"])

        def load(src, st, with_h=True):
            sl = st % 2
            sti = st % ST_PER_BATCH
            P.dma("sp", lambda h: h.dma_start(out=hTs[sl][:, :, :], in_=src[st]), ("hTs", sl), writes=[("hTs", sl)])
            P.dma("sp", lambda h: h.dma_start(out=Ct[sl][:, :], in_=d["cos"][sti]), ("Ct", sl), writes=[("Ct", sl)])
            P.dma("sp", lambda h: h.dma_start(out=Sn[sl][:, :], in_=d["sin"][sti]), ("Sn", sl), writes=[("Sn", sl)])

        def proj_rope(Wt, wkey, sl, dst_ap, dname, bank):
            for j in range(2):
                for kc in range(8):
                    P.op("pe", lambda h, j=j, kc=kc: h.matmul(sc[bank][:, j, :], lhsT=Wt[:, kc, j * 128:(j + 1) * 128],
                                                             rhs=hTs[sl][:, kc, :], start=(kc == 0), stop=(kc == 7)),
                         reads=[(wkey, kc), ("hTs", sl)], writes=[("sc", bank)])
            P.op("dve", lambda h: h.tensor_tensor(out=t1[:, :], in0=sc[bank][:, 0, :], in1=Ct[sl][:, :], op=ALU.mult),
                 reads=[("sc", bank), ("Ct", sl)], writes=["t1"])
            P.op("dve", lambda h: h.tensor_tensor(out=t2[:, :], in0=sc[bank][:, 1, :], in1=Sn[sl][:, :], op=ALU.mult),
                 reads=[("sc", bank), ("Sn", sl)], writes=["t2"])
            P.op(ELT, lambda h: h.tensor_tensor(out=dst_ap, in0=t1[:, :], in1=t2[:, :], op=ALU.add),
                 reads=["t1", "t2"], writes=[dname])

        nst = d.get("nst", NST)
        load(d["hTkv_all"], 0)

        def kv_step(st):
            sl = st % 2
            if st + 1 < nst:
                load(d["hTkv_all"], st + 1)
            else:
                load(d["hT_all"], 0)
            proj_rope(WK, "WK", sl, KT[:, st * ST:(st + 1) * ST], ("KT", st), 0)
            for a in range(4):
                for kc in range(8):
                    P.op("pe", lambda h, a=a, kc=kc: h.matmul(sc[1][:, 0, a * 128:(a + 1) * 128], lhsT=hTs[sl][:, kc, a * 128:(a + 1) * 128],
                                                              rhs=WK[:, kc, 256:384], start=(kc == 0), stop=(kc == 7)),
                         reads=[("WK", kc), ("hTs", sl)], writes=[("sc", 1)])
            P.op("act", lambda h: h.copy(out=V[:, st * 4:(st + 1) * 4, :], in_=sc[1][:, 0, :].rearrange("p (a e) -> p a e", a=4)),
                 reads=[("sc", 1)], writes=[("V", st)])

        for st in range(nst):
            kv_step(st)

        def group(g):
            sl = g % 2
            gi = g % ST_PER_BATCH
            bb = g // ST_PER_BATCH
            if g + 1 < nst:
                load(d["hT_all"], g + 1)
            proj_rope(WB, "WB", sl, qT[:, :], "qT", 0)
            for kc in range(8):
                P.op("pe", lambda h, kc=kc: h.matmul(sc[1][:, 0, :], lhsT=WB[:, kc, 256:384], rhs=hTs[sl][:, kc, :],
                                                     start=(kc == 0), stop=(kc == 7)),
                     reads=[("WB", kc), ("hTs", sl)], writes=[("sc", 1)])
            P.op("act", lambda h: h.activation(out=sgT[:, :], in_=sc[1][:, 0, :], func=AF.Silu), reads=[("sc", 1)], writes=["sgT"])
            nk = 4 * (gi + 1)

            def scores(kt):
                p = kt % 2
                T = bb * (S // 128) + kt
                q0 = max(0, kt - 4 * gi) * 128
                for t in range(2):
                    P.op("pe", lambda h, t=t: h.matmul(sc[p][:, t, q0:ST], lhsT=KT[t * 64:(t + 1) * 64, T * 128:(T + 1) * 128],
                                                       rhs=qT[t * 64:(t + 1) * 64, q0:ST], start=True, stop=True),
                         reads=[("KT", T // 4), "qT"], writes=[("sc", p)], tag="b_sc")
                P.op("act", lambda h: h.activation(out=PT[p][:, :, q0:ST], in_=sc[p][:, :, q0:ST], func=AF.Exp, scale=0.125),
                     reads=[("sc", p)], writes=[("PT", p)], tag="b_exp")
                if kt >= 4 * gi:
                    P.op(ELT, lambda h: h.memset(PT[p][64:128, :, q0:q0 + 64], 0.0), reads=[("PT", p)], writes=[("PT", p)], tag="b_ms")

            def pv(kt):
                p = kt % 2
                T = bb * (S // 128) + kt
                q0 = max(0, kt - 4 * gi) * 128
                for t in range(2):
                    P.op("pe", lambda h, t=t: h.matmul(po[t][:, q0:ST], lhsT=V[:, T, :], rhs=PT[p][:, t, q0:ST],
                                                       start=(kt == 0), stop=(kt == nk - 1)),
                         reads=[("V", T // 4), ("PT", p)], writes=[("po", t)], tag="b_pv")
                    P.op("pe", lambda h, t=t: h.matmul(pl[t][:, q0:ST], lhsT=ones[:, :], rhs=PT[p][:, t, q0:ST],
                                                       start=(kt == 0), stop=(kt == nk - 1)),
                         reads=["ones", ("PT", p)], writes=[("pl", t)], tag="b_ones")

            for kt in range(nk):
                scores(kt)
                if kt > 0:
                    pv(kt - 1)
            pv(nk - 1)
            P.op("dve", lambda h: h.reciprocal(out=r1[:, :], in_=pl[0][:, :]), reads=[("pl", 0)], writes=["r1"])
            P.op("dve", lambda h: h.reciprocal(out=r2[:, :], in_=pl[1][:, :]), reads=[("pl", 1)], writes=["r2"])
            P.op("dve", lambda h: h.tensor_tensor(out=a1[:, :], in0=po[0][:, :], in1=r1[:, :], op=ALU.mult),
                 reads=[("po", 0), "r1"], writes=["a1"])
            P.op("dve", lambda h: h.tensor_tensor(out=a2[:, :], in0=po[1][:, :], in1=r2[:, :], op=ALU.mult),
                 reads=[("po", 1), "r2"], writes=["a2"])
            P.op("dve", lambda h: h.scalar_tensor_tensor(out=a1[:, :], in0=a2[:, :], scalar=lams[:, 3:4], in1=a1[:, :],
                                                         op0=ALU.mult, op1=ALU.add),
                 reads=["a1", "a2", "neglam"], writes=["a1"])
            P.op(ELT, lambda h: h.tensor_tensor(out=sq[:, :], in0=a1[:, :], in1=a1[:, :], op=ALU.mult), reads=["a1"], writes=["sq"])
            P.op("pe", lambda h: h.matmul(sc[1][:, 1, :], lhsT=onesF[:, :], rhs=sq[:, :], start=True, stop=True),
                 reads=["onesF", "sq"], writes=[("sc", 1)])
            P.op("act", lambda h: h.activation(out=r1[:, :], in_=sc[1][:, 1, :], func=AF.Sqrt, scale=1.0 / 128, bias=EPS),
                 reads=[("sc", 1)], writes=["r1"])
            P.op("dve", lambda h: h.reciprocal(out=r1[:, :], in_=r1[:, :]), reads=["r1"], writes=["r1"])
            P.op("dve", lambda h: h.scalar_tensor_tensor(out=a2[:, :], in0=a1[:, :], scalar=subg[:, 1:2], in1=r1[:, :],
                                                         op0=ALU.mult, op1=ALU.mult),
                 reads=["a1", "subgs", "r1"], writes=["a2"])
            P.op(ELT, lambda h: h.tensor_tensor(out=ogT[sl][:, 0, :], in0=a2[:, :], in1=sgT[:, :], op=ALU.mult),
                 reads=["a2", "sgT"], writes=[("ogT", sl)])
            P.dma(STQ, lambda h: h.dma_start(out=d["og_out"][g], in_=ogT[sl][:, :, :]), ("ogo", sl),
                  reads=[("ogT", sl)], writes=[("ogd", g)])

        for g in range(nst):
            group(g)
        _finish(P)


def _dram(nc, name, shape, dt, kind):
    return nc.dram_tensor(name, list(shape), dt, kind=kind).ap()


def _swap_cols(w, blk):
    k, n = w.shape
    return np.ascontiguousarray(w.reshape(k, n // blk, 2, blk // 2)[:, :, ::-1, :].reshape(k, n))


_IDENT = _bf16(np.eye(128, dtype=np.float32))


def _run(nc, in_maps):
    res = run_bass_kernel_spmd(nc, in_maps, core_ids=list(range(NCORES)))
    return res.results


def _build_T(x_in, x_out, t2_nkc, n_t1):
    nc = bass.Bass("TRN2", target_bir_lowering=False)
    d = {"ident": _dram(nc, "ident", [128, 128], BF16, "ExternalInput")}
    if x_in:
        d["x_in"] = _dram(nc, "x_in", [TOK_PER_CORE, D], F32, "ExternalInput")
    if t2_nkc:
        nch = t2_nkc // 8
        d["t2"] = dict(og=_dram(nc, "og", [8, ST_PER_CORE, 128, nch, ST], BF16, "ExternalInput"),
                       wout=_dram(nc, "wout", [t2_nkc * 128, D], F32, "ExternalInput"),
                       gpost=_dram(nc, "gpost", [128, D], F32, "ExternalInput"), nkc=t2_nkc)
    d["t1"] = [(_dram(nc, "g1_%d" % i, [128, D], F32, "ExternalInput"),
                _dram(nc, "hT_%d" % i, [ST_PER_CORE, 128, 8, ST], BF16, "ExternalOutput")) for i in range(n_t1)]
    if x_out:
        d["x_out"] = _dram(nc, "x_out", [TOK_PER_CORE, D], F32, "ExternalOutput")
    with nc.sbuf_tensor("X", [128, TILES_PER_CORE, D], F32) as X:
        phase_T(nc, X, d)
    return nc


def _build_HA(nst=NST):
    nc = bass.Bass("TRN2", target_bir_lowering=False)
    d = dict(hT_all=_dram(nc, "hT_all", [NST, 128, 8, ST], BF16, "ExternalInput"),
             w=_dram(nc, "w", [D, 1024], F32, "ExternalInput"),
             MA=_dram(nc, "MA", [128, 512], F32, "ExternalInput"),
             CA=_dram(nc, "CA", [128, 16], F32, "ExternalInput"),
             cos=_dram(nc, "cos", [ST_PER_BATCH, 128, ST], F32, "ExternalInput"),
             sin=_dram(nc, "sin", [ST_PER_BATCH, 128, ST], F32, "ExternalInput"),
             ident=_dram(nc, "ident", [128, 128], BF16, "ExternalInput"),
             og_out=_dram(nc, "og_out", [NST, 128, 2, ST], BF16, "ExternalOutput"), nst=nst)
    phase_HA(nc, d)
    return nc


def _build_HB(lam_init, nst=NST):
    nc = bass.Bass("TRN2", target_bir_lowering=False)
    d = dict(hT_all=_dram(nc, "hT_all", [NST, 128, 8, ST], BF16, "ExternalInput"),
             hTkv_all=_dram(nc, "hTkv_all", [NST, 128, 8, ST], BF16, "ExternalInput"),
             w=_dram(nc, "w", [D, 384], F32, "ExternalInput"),
             wkv=_dram(nc, "wkv", [D, 384], F32, "ExternalInput"),
             cos=_dram(nc, "cos", [ST_PER_BATCH, 128, ST], F32, "ExternalInput"),
             sin=_dram(nc, "sin", [ST_PER_BATCH, 128, ST], F32, "ExternalInput"),
             lamv=_dram(nc, "lamv", [128, 2, 2, 64], F32, "ExternalInput"),
             subg=_dram(nc, "subg", [128, 1], F32, "ExternalInput"),
             og_out=_dram(nc, "og_out", [NST, 128, 1, ST], BF16, "ExternalOutput"),
             lam_init=lam_init, nst=nst)
    phase_HB(nc, d)
    return nc


def phase_AG(nc, src, dst, cc):
    P = Prog(nc)
    P.ext["cc"] = cc
    P.dma("pool", lambda h: h.collective_compute("AllGather", ALU.bypass, replica_groups=[list(range(NCORES))],
                                                 ins=[src.opt()], outs=[dst.opt()]),
          "cc", writes=["dst"], inc=1)
    _finish(P)


LAM_INIT = [0.8 - 0.6 * math.exp(-0.3 * layer) for layer in range(4)]


def _build_fused(nlayers=4, nst=NST):
    nc = bass.Bass("TRN2", target_bir_lowering=False)
    I = lambda name, shape, dt=F32: _dram(nc, name, shape, dt, "ExternalInput")
    N = lambda name, shape, dt=BF16: nc.dram_tensor(name, list(shape), dt).ap()
    ident = I("ident", [128, 128], BF16)
    x_in = I("x_in", [TOK_PER_CORE, D])
    x_out = _dram(nc, "x_out", [TOK_PER_CORE, D], F32, "ExternalOutput")
    pre = [I("pre%d" % l, [128, D]) for l in range(4)]
    post = [I("post%d" % l, [128, D]) for l in range(4)]
    kvg = I("kvg", [128, D])
    wA = [I("wA%d" % l, [D, 1024]) for l in range(2)]
    wB = [I("wB%d" % j, [D, 384]) for j in range(2)]
    wkv = I("wkv", [D, 384])
    wout = [I("wout%d" % l, [2048 if l < 2 else 1024, D]) for l in range(4)]
    MA, CA = I("MA", [128, 512]), I("CA", [128, 16])
    cosA, sinA = I("cosA", [ST_PER_BATCH, 128, ST]), I("sinA", [ST_PER_BATCH, 128, ST])
    cosB, sinB = I("cosB", [ST_PER_BATCH, 128, ST]), I("sinB", [ST_PER_BATCH, 128, ST])
    lamv = [I("lamv%d" % j, [128, 2, 2, 64]) for j in range(2)]
    subg = [I("subg%d" % j, [128, 1]) for j in range(2)]
    hT_loc = [N("hT_loc%d" % l, [ST_PER_CORE, 128, 8, ST]) for l in range(4)]
    hT_all = [N("hT_all%d" % l, [NST, 128, 8, ST]) for l in range(4)]
    hTkv_loc, hTkv_all = N("hTkv_loc", [ST_PER_CORE, 128, 8, ST]), N("hTkv_all", [NST, 128, 8, ST])
    nch = [2, 2, 1, 1]
    og_loc = [N("og_loc%d" % l, [NST, 128, nch[l], ST]) for l in range(4)]
    og_all = [N("og_all%d" % l, [8, NST, 128, nch[l], ST]) for l in range(4)]
    with ExitStack() as es0:
        X = es0.enter_context(nc.sbuf_tensor("X", [128, TILES_PER_CORE, D], F32))
        pads = [es0.enter_context(nc.semaphore("ccpad%d" % i)) for i in range(8)]
        assert pads[7].num == 162, pads[7].num
        cc = dict(sem=pads[7], count=0)
        phase_T(nc, X, dict(ident=ident, x_in=x_in, t1=[(pre[0], hT_loc[0])]))
        for l in range(nlayers):
            phase_AG(nc, hT_loc[l], hT_all[l], cc)
            if l == 2:
                phase_AG(nc, hTkv_loc, hTkv_all, cc)
            if l < 2:
                phase_HA(nc, dict(hT_all=hT_all[l], w=wA[l], MA=MA, CA=CA, cos=cosA, sin=sinA, ident=ident, og_out=og_loc[l], nst=nst))
            else:
                phase_HB(nc, dict(hT_all=hT_all[l], hTkv_all=hTkv_all, w=wB[l - 2], wkv=wkv, cos=cosB, sin=sinB,
                                  lamv=lamv[l - 2], subg=subg[l - 2], og_out=og_loc[l], lam_init=LAM_INIT[l], nst=nst))
            phase_AG(nc, og_loc[l], og_all[l], cc)
            d = dict(ident=ident, t2=dict(og=og_all[l], dyn=True, dynq=("sp", "sp", "act", "act")[l],
                                          wout=wout[l], gpost=post[l], nkc=8 * nch[l]))
            if l < nlayers - 1:
                d["t1"] = [(pre[l + 1], hT_loc[l + 1])] + ([(kvg, hTkv_loc)] if l == 1 else [])
            else:
                d["x_out"] = x_out
            phase_T(nc, X, d)
    return nc


def kernel_fused(x, pre_norm, post_norm, w_in_a, w_out_a, kv_norm, w_kv, w_in_b,
                 lam_q1, lam_k1, lam_q2, lam_k2, sub_norm_b, w_out_b):
    xs = np.split(np.ascontiguousarray(x.reshape(NTOK, D)), NCORES, axis=0)
    cosA, sinA = _rope_tables(64, 128)
    cosB, sinB = _rope_tables(32, 128)
    nc = _build_fused(int(os.environ.get("KNL", "4")), int(os.environ.get("KNST", str(NST))))
    maps = []
    for r in range(NCORES):
        MA, CA = _ret_consts(r)
        m = {"ident": _IDENT, "x_in": xs[r], "kvg": _rep(kv_norm), "wkv": _wKV(w_kv, r), "MA": MA, "CA": CA,
             "cosA": cosA, "sinA": sinA, "cosB": cosB, "sinB": sinB}
        for l in range(4):
            m["pre%d" % l] = _rep(pre_norm[l])
            m["post%d" % l] = _rep(post_norm[l])
            m["wout%d" % l] = np.ascontiguousarray(w_out_a[l] if l < 2 else w_out_b[l - 2])
        for l in range(2):
            m["wA%d" % l] = _wA(w_in_a[l], r)
            m["wB%d" % l] = _wB(w_in_b[l], r)
            lamv = np.stack([np.stack([lam_q1[l], lam_q2[l]]), np.stack([lam_k1[l], lam_k2[l]])])
            m["lamv%d" % l] = np.ascontiguousarray(np.broadcast_to(lamv[None], (128, 2, 2, 64))).astype(np.float32)
            m["subg%d" % l] = np.ascontiguousarray(sub_norm_b[l].reshape(128, 1))
        maps.append(m)
    res = _run(nc, maps)
    return np.concatenate([np.asarray(r["x_out"]) for r in res], axis=0).reshape(B, S, D).astype(np.float32)


def _wA(w_in, h):
    q = w_in[:, h * 128:(h + 1) * 128]
    k = w_in[:, 1024 + h * 128:1024 + (h + 1) * 128]
    v = w_in[:, 2048 + h * 256:2048 + (h + 1) * 256]
    g = w_in[:, 4096 + h * 256:4096 + (h + 1) * 256]
    return np.ascontiguousarray(np.concatenate([q, _swap_cols(q, 128), k, _swap_cols(k, 128), v, g], axis=1))


def _wB(w_in, h):
    q = w_in[:, h * 128:(h + 1) * 128]
    g = w_in[:, 1024 + h * 128:1024 + (h + 1) * 128]
    return np.ascontiguousarray(np.concatenate([q, _swap_cols(q, 64), g], axis=1))


def _wKV(w_kv, h):
    k = w_kv[:, h * 128:(h + 1) * 128]
    v = w_kv[:, 1024 + h * 128:1024 + (h + 1) * 128]
    return np.ascontiguousarray(np.concatenate([k, _swap_cols(k, 64), v], axis=1))


def _gather_hT(res, key):
    return np.ascontiguousarray(np.concatenate([np.asarray(r[key]) for r in res], axis=0))


def _a2a(res, key):
    ogs = [np.asarray(r[key]) for r in res]
    return [np.ascontiguousarray(np.stack([ogs[h][ST_PER_CORE * r:ST_PER_CORE * (r + 1)] for h in range(8)], axis=0))
            for r in range(NCORES)]


FUSED = os.environ.get("KFUSED", "0") == "1"


def kernel(x, pre_norm, post_norm, w_in_a, w_out_a, kv_norm, w_kv, w_in_b,
           lam_q1, lam_k1, lam_q2, lam_k2, sub_norm_b, w_out_b):
    f = lambda a: np.asarray(a, dtype=np.float32)
    x, pre_norm, post_norm, w_in_a, w_out_a, kv_norm, w_kv, w_in_b = map(f, (x, pre_norm, post_norm, w_in_a, w_out_a, kv_norm, w_kv, w_in_b))
    lam_q1, lam_k1, lam_q2, lam_k2, sub_norm_b, w_out_b = map(f, (lam_q1, lam_k1, lam_q2, lam_k2, sub_norm_b, w_out_b))
    if FUSED:
        return kernel_fused(x, pre_norm, post_norm, w_in_a, w_out_a, kv_norm, w_kv, w_in_b,
                            lam_q1, lam_k1, lam_q2, lam_k2, sub_norm_b, w_out_b)
    xs = np.split(np.ascontiguousarray(x.reshape(NTOK, D)), NCORES, axis=0)
    cosA, sinA = _rope_tables(64, 128)
    cosB, sinB = _rope_tables(32, 128)
    retc = [_ret_consts(h) for h in range(8)]

    nc = _build_T(True, False, 0, 1)
    res = _run(nc, [{"ident": _IDENT, "x_in": xs[r], "g1_0": _rep(pre_norm[0])} for r in range(NCORES)])
    hT = _gather_hT(res, "hT_0")
    ncHA = _build_HA()
    hTkv = None
    for layer in range(4):
        if layer < 2:
            res = _run(ncHA, [{"hT_all": hT, "w": _wA(w_in_a[layer], h), "MA": retc[h][0], "CA": retc[h][1],
                               "cos": cosA, "sin": sinA, "ident": _IDENT} for h in range(NCORES)])
            wout, nkc = w_out_a[layer], 16
        else:
            j = layer - 2
            lam_init = 0.8 - 0.6 * math.exp(-0.3 * layer)
            ncHB = _build_HB(lam_init)
            lamv = np.stack([np.stack([lam_q1[j], lam_q2[j]]), np.stack([lam_k1[j], lam_k2[j]])])
            lamv = np.ascontiguousarray(np.broadcast_to(lamv[None], (128, 2, 2, 64))).astype(np.float32)
            res = _run(ncHB, [{"hT_all": hT, "hTkv_all": hTkv, "w": _wB(w_in_b[j], h), "wkv": _wKV(w_kv, h),
                               "cos": cosB, "sin": sinB, "lamv": lamv,
                               "subg": np.ascontiguousarray(sub_norm_b[j].reshape(128, 1))} for h in range(NCORES)])
            wout, nkc = w_out_b[j], 8
        ogr = _a2a(res, "og_out")
        last = layer == 3
        n_t1 = 0 if last else (2 if layer == 1 else 1)
        ncT = _build_T(True, True, nkc, n_t1)
        maps = []
        for r in range(NCORES):
            m = {"ident": _IDENT, "x_in": xs[r], "og": ogr[r], "wout": np.ascontiguousarray(wout), "gpost": _rep(post_norm[layer])}
            if n_t1 >= 1:
                m["g1_0"] = _rep(pre_norm[layer + 1])
            if n_t1 == 2:
                m["g1_1"] = _rep(kv_norm)
            maps.append(m)
        res = _run(ncT, maps)
        xs = [np.asarray(r["x_out"]) for r in res]
        if n_t1 >= 1:
            hT = _gather_hT(res, "hT_0")
        if n_t1 == 2:
            hTkv = _gather_hT(res, "hT_1")
    return np.concatenate(xs, axis=0).reshape(B, S, D).astype(np.float32)
```

```python
import math
from contextlib import ExitStack

import numpy as np
import ml_dtypes

import concourse.bass as bass
import concourse.mybir as mybir
from concourse.bass_utils import run_bass_kernel_spmd

F32 = mybir.dt.float32
BF16 = mybir.dt.bfloat16
AF = mybir.ActivationFunctionType
ALU = mybir.AluOpType
AX = mybir.AxisListType

NCORES = 8
D = 1024
B = 2
S = 8192
NTOK = B * S
EPS = 1e-6
TOK_PER_CORE = NTOK // NCORES
TILES_PER_CORE = TOK_PER_CORE // 128
ST = 512
NST = NTOK // ST
ST_PER_CORE = TOK_PER_CORE // ST
ST_PER_BATCH = S // ST

SAME_ENGINE_SYNC = True


import os
SKIP = set(os.environ.get("KSKIP", "").split(","))
ELT = os.environ.get("KELT", "dve")
STQ = os.environ.get("KSTQ", "sp")


PSUM_KEYS = {"pj", "sc", "po", "pdS", "ptr", "Y", "psT", "pl"}


class Prog:
    ENGS = ("pe", "act", "dve", "pool", "sp")
    n_emit = 0

    def __init__(self, nc):
        self.nc = nc
        self.ops = {e: [] for e in self.ENGS}
        self.lastw = {}
        self.readers = {}
        self.dma_cnt = {}
        self.cur_tag = None
        self.ext = {}

    def _deps(self, eng, reads, writes):
        toks = set()
        for r in reads:
            if r in self.lastw:
                toks.add(self.lastw[r])
            if (r if isinstance(r, str) else r[0]) in PSUM_KEYS:
                for t in self.readers.get(r, ()):
                    if t[0] == "eng" and t[1] != eng:
                        toks.add(t)
        for w in writes:
            if w in self.lastw:
                toks.add(self.lastw[w])
            for t in self.readers.get(w, ()):
                toks.add(t)
        out = set()
        for t in toks:
            if t[0] == "eng" and t[1] == eng:
                if eng in ("pe", "sp") or not SAME_ENGINE_SYNC:
                    continue
            out.add(t)
        return out

    def _commit(self, tok, reads, writes):
        for r in reads:
            self.readers.setdefault(r, []).append(tok)
        for w in writes:
            self.lastw[w] = tok
            self.readers[w] = []

    def op(self, eng, fn, reads=(), writes=(), tag=None):
        tag = tag or self.cur_tag
        if tag is not None and tag in SKIP:
            return
        deps = self._deps(eng, reads, writes)
        idx = len(self.ops[eng])
        self.ops[eng].append(dict(fn=fn, deps=deps, dma=None))
        self._commit(("eng", eng, idx), reads, writes)

    def dma(self, eng, fn, semkey, reads=(), writes=(), tag=None, inc=16):
        if tag is not None and tag in SKIP:
            return
        deps = self._deps(eng, reads, writes)
        if semkey in self.ext:
            self.ext[semkey]["count"] += inc
            cnt = self.ext[semkey]["count"]
            self.dma_cnt.setdefault(semkey, 0)
        else:
            cnt = self.dma_cnt.get(semkey, 0) + inc
            self.dma_cnt[semkey] = cnt
        self.ops[eng].append(dict(fn=fn, deps=deps, dma=semkey, inc=inc))
        self._commit(("dma", semkey, cnt), reads, writes)

    def dma_multi(self, eng, fns, semkey, reads=(), writes_list=(), tag=None):
        if tag is not None and tag in SKIP:
            return
        final = self.dma_cnt.get(semkey, 0) + 16 * len(fns)
        for fn, w in zip(fns, writes_list):
            deps = self._deps(eng, reads, w)
            self.ops[eng].append(dict(fn=fn, deps=deps, dma=semkey))
            self._commit(("dma", semkey, final), reads, w)
        self.dma_cnt[semkey] = final

    def emit(self, final_waits=()):
        nc = self.nc
        self.op("sp", None, reads=tuple(final_waits))
        needed = set()
        for e in self.ENGS:
            for o in self.ops[e]:
                for t in o["deps"]:
                    if t[0] == "eng":
                        needed.add((t[1], t[2]))
        inc_count = {}
        for e in self.ENGS:
            c = 0
            for i, o in enumerate(self.ops[e]):
                if (e, i) in needed:
                    c += 1
                    inc_count[(e, i)] = c
        with ExitStack() as es:
            Prog.n_emit += 1
            esem = {e: es.enter_context(nc.semaphore("s%d_%s" % (Prog.n_emit, e))) for e in self.ENGS}
            dsem = {k: (self.ext[k]["sem"] if k in self.ext else es.enter_context(nc.semaphore("d%d_%d" % (Prog.n_emit, i))))
                    for i, k in enumerate(sorted(self.dma_cnt, key=str))}
            block = es.enter_context(nc.Block())

            def run(e, h):
                waited = {}
                for i, o in enumerate(self.ops[e]):
                    want = {}
                    for t in o["deps"]:
                        if t[0] == "eng":
                            k, v = ("e", t[1]), inc_count[(t[1], t[2])]
                        else:
                            k, v = ("d", t[1]), t[2]
                        if v > want.get(k, 0):
                            want[k] = v
                    for k, v in want.items():
                        if waited.get(k, 0) >= v:
                            continue
                        waited[k] = v
                        h.wait_ge(esem[k[1]] if k[0] == "e" else dsem[k[1]], v)
                    if o["fn"] is None:
                        continue
                    ins = o["fn"](h)
                    if o["dma"] is not None:
                        ins.then_inc(dsem[o["dma"]], o.get("inc", 16))
                    elif (e, i) in inc_count:
                        ins.then_inc(esem[e], 1)

            block.tensor(lambda h: run("pe", h))
            block.scalar(lambda h: run("act", h))
            block.vector(lambda h: run("dve", h))
            block.gpsimd(lambda h: run("pool", h))
            block.sync(lambda h: run("sp", h))


def _bf16(a):
    return np.asarray(a).astype(ml_dtypes.bfloat16)


def _rope_tables(half, dim_rows):
    inv = np.power(np.float32(10000.0), -np.arange(half, dtype=np.float32) / np.float32(half)).astype(np.float32)
    ang = (np.arange(S, dtype=np.float32)[:, None] * inv[None, :]).astype(np.float32)
    cos = np.cos(ang.astype(np.float64)).astype(np.float32)
    sin = np.sin(ang.astype(np.float64)).astype(np.float32)
    rows = np.arange(dim_rows)
    f = (rows % (2 * half)) % half
    sign = np.where((rows % (2 * half)) < half, -1.0, 1.0).astype(np.float32)
    C = cos[:, f].T
    Sg = (sin[:, f] * sign[None, :]).T
    C = np.ascontiguousarray(C.reshape(dim_rows, ST_PER_BATCH, ST).transpose(1, 0, 2))
    Sg = np.ascontiguousarray(Sg.reshape(dim_rows, ST_PER_BATCH, ST).transpose(1, 0, 2))
    return C.astype(np.float32), Sg.astype(np.float32)


def _ret_consts(h):
    lg = math.log1p(-2.0 ** (-5.0 - h))
    g = lambda e: math.exp(e * lg)
    s = 128.0 ** -0.5
    p = np.arange(128)
    MA = np.zeros((128, 4, 128), np.float64)
    MA[:, 0, :] = s * g(-128) * (p[:, None] <= p[None, :])
    for d in range(1, 4):
        MA[:, d, :] = s * g(128 * (d - 1))
    CA = np.zeros((128, 16), np.float64)
    CA[:, 0] = np.exp((127 - p) * lg)
    CA[:, 1] = EPS * np.exp(-2.0 * (p + 1) * lg)
    for a in range(4):
        CA[:, 2 + a] = g(128 * a)
        CA[:, 6 + a] = s * g(128 * (3 - a))
    CA[:, 10] = g(512)
    return MA.reshape(128, 512).astype(np.float32), CA.astype(np.float32)


def _rep(v, n=128):
    return np.ascontiguousarray(np.broadcast_to(np.asarray(v, np.float32).reshape(1, -1), (n, np.asarray(v).size)))


_PID_CACHE = {}


class Alloc:
    n = 0

    def __init__(self, nc, es):
        self.nc, self.es = nc, es
        Alloc.n += 1
        self.pfx = "p%d_" % Alloc.n

    def sb(self, name, shape, dt):
        nb = int(np.prod(shape[1:])) * (4 if dt == F32 else 2)
        self.total = getattr(self, "total", 0) + nb
        if os.environ.get("KDEBUG"):
            print("  sbuf", self.pfx, name, nb, "total", self.total)
        return self.es.enter_context(self.nc.sbuf_tensor(self.pfx + "sb_" + name, list(shape), dt))

    def ps(self, name, shape, dt):
        return self.es.enter_context(self.nc.psum_tensor(self.pfx + "ps_" + name, list(shape), dt))


def _finish(P, extra=()):
    allres = list(P.lastw.keys())
    P.op("sp", lambda h: h.nop(), reads=allres, writes=["__done"])
    for e in ("pe", "act", "dve", "pool"):
        P.op(e, lambda h: h.nop(), reads=["__done"])
    P.emit(final_waits=allres)


def _load_w_bf16(P, A, Wsb, Wd, nkc, c0, c1, key):
    stg = [A.sb("%s_stg%d" % (key, i), [128, c1 - c0], F32) for i in range(2)]
    for kc in range(nkc):
        i = kc % 2
        P.dma("sp", lambda h, kc=kc, i=i: h.dma_start(out=stg[i][:, :], in_=Wd[kc * 128:(kc + 1) * 128, c0:c1]),
              (key + "_stg", i), writes=[(key + "_stg", i)], tag="ldW")
        if i == 0:
            P.op("act", lambda h, kc=kc, i=i: h.copy(out=Wsb[:, kc, c0:c1], in_=stg[i][:, :]),
                 reads=[(key + "_stg", i)], writes=[(key, kc)], tag="ldW")
        else:
            P.op("dve", lambda h, kc=kc, i=i: h.tensor_copy(out=Wsb[:, kc, c0:c1], in_=stg[i][:, :]),
                 reads=[(key + "_stg", i)], writes=[(key, kc)], tag="ldW")


def phase_T(nc, X, d):
    with ExitStack() as es:
        A = Alloc(nc, es)
        P = Prog(nc)
        t2 = d.get("t2")
        t1s = d.get("t1", [])
        NX = 4
        X = A.sb("Xr", [128, NX, D], F32)
        I = A.sb("I", [128, 128], BF16)
        junk = A.sb("junk", [128, D], BF16)
        st_ = A.sb("stt", [128, 8], F32)
        P.dma("sp", lambda h: h.dma_start(out=I[:, :], in_=d["ident"][:, :]), "I", writes=["I"])
        if t2:
            nkc = t2["nkc"]
            Wo = A.sb("Wo", [128, nkc, D], BF16)
            Gp = A.sb("Gp", [128, D], F32)
            whole = bool(t2.get("whole"))
            if whole:
                OGw = A.sb("OGw", [128, 8, ST_PER_CORE, nkc // 8, ST], BF16)
            else:
                OG = [A.sb("OG%d" % i, [128, nkc, ST], BF16) for i in range(2)]
            tmp = A.sb("tmp", [128, D], F32)
            Y = [A.ps("Y%d" % i, [128, D], F32) for i in range(2)]
            _load_w_bf16(P, A, Wo, t2["wout"], nkc, 0, D, "Wo")
            P.dma("sp", lambda h: h.dma_start(out=Gp[:, :], in_=t2["gpost"][:, :]), "Gp", writes=["Gp"])
            nch = nkc // 8
        G1 = []
        for i, (g, _) in enumerate(t1s):
            Gt = A.sb("G1_%d" % i, [128, D], F32)
            P.dma("sp", lambda h, Gt=Gt, g=g: h.dma_start(out=Gt[:, :], in_=g[:, :]), "G1_%d" % i, writes=["G1_%d" % i])
            G1.append(Gt)
        if t1s:
            hb = A.sb("hb", [128, D], BF16)
            psT = A.ps("psT", [128, 8, 128], BF16)
            hTs = [[A.sb("hTs%d_%d" % (i, k), [128, 8, ST], BF16) for k in range(2)] for i in range(len(t1s))]

        pidc = {}

        def load_og(s):
            sl = s % 2
            if whole:
                if s == 0:
                    def ldw(h):
                        pid = h.partition_id()
                        src = t2["og"][:, bass.ds(pid, 1)]
                        return h.dma_start(out=OGw[:, :, :, :, :], in_=src[:, 0].rearrange("h p s c t -> p h s c t"))
                    P.dma("sp", ldw, "OGw", reads=["og_dram"], writes=["OGw"])
                return
            if t2.get("dyn"):
                def ld(h, sl=sl, s=s):
                    if "pid" not in pidc:
                        pidc["pid"] = h.partition_id()
                    src = t2["og"][:, bass.ds(pidc["pid"], 1)]
                    return h.dma_start(out=OG[sl][:, :, :].rearrange("p (h c) t -> p h c t", h=8),
                                       in_=src[:, 0, :, s].rearrange("h p c t -> p h c t"))
                P.dma(t2.get("dynq", "sp"), ld, ("OG", sl), reads=["og_dram"], writes=[("OG", sl, hh) for hh in range(8)])
            else:
                P.dma_multi("sp", [lambda h, hh=hh, sl=sl, s=s: h.dma_start(
                    out=OG[sl][:, hh * nch:(hh + 1) * nch, :], in_=t2["og"][hh, :, s]) for hh in range(8)],
                    ("OG", sl), writes_list=[[("OG", sl, hh)] for hh in range(8)])

        def load_x(lt):
            P.dma("sp", lambda h, lt=lt: h.dma_start(out=X[:, lt % NX, :], in_=d["x_in"][lt * 128:(lt + 1) * 128, :]),
                  ("Xin", lt % NX), writes=[("X", lt % NX)])

        for lt in range(3):
            load_x(lt)
        if t2:
            load_og(0)
        def stage1(lt):
            s, a = lt // 4, lt % 4
            if t2:
                if a == 0 and s + 1 < ST_PER_CORE:
                    load_og(s + 1)
                sl = s % 2
                y = Y[lt % 2]
                for nb in range(2):
                    for kc in range(nkc):
                        P.op("pe", lambda h, y=y, nb=nb, kc=kc, sl=sl, a=a, s=s: h.matmul(
                            y[:, nb * 512:(nb + 1) * 512],
                            lhsT=(OGw[:, kc // nch, s, kc % nch, a * 128:(a + 1) * 128] if whole else OG[sl][:, kc, a * 128:(a + 1) * 128]),
                            rhs=Wo[:, kc, nb * 512:(nb + 1) * 512], start=(kc == 0), stop=(kc == nkc - 1)),
                            reads=["OGw" if whole else ("OG", sl, kc // nch), ("Wo", kc)], writes=[("Y", lt % 2)], tag="t2mm")

        def stage1_post(lt):
            if t2:
                y = Y[lt % 2]
                P.op("act", lambda h, y=y: h.activation(out=junk[:, :], in_=y[:, :], func=AF.Square, accum_out=st_[:, 0:1]),
                     reads=[("Y", lt % 2)], writes=["junk", "ssq"], tag="t2sq")
                P.op("act", lambda h: h.activation(out=st_[:, 1:2], in_=st_[:, 0:1], func=AF.Sqrt, scale=1.0 / D, bias=EPS),
                     reads=["ssq"], writes=["rstd"])
                P.op("dve", lambda h: h.reciprocal(out=st_[:, 1:2], in_=st_[:, 1:2]), reads=["rstd"], writes=["rstd"])
                P.op("dve", lambda h, y=y: h.scalar_tensor_tensor(out=tmp[:, :], in0=y[:, :], scalar=st_[:, 1:2], in1=Gp[:, :],
                                                                  op0=ALU.mult, op1=ALU.mult),
                     reads=[("Y", lt % 2), "rstd", "Gp"], writes=["tmp"], tag="t2stt")
                P.op(ELT, lambda h, lt=lt: h.tensor_tensor(out=X[:, lt % NX, :], in0=X[:, lt % NX, :], in1=tmp[:, :], op=ALU.add),
                     reads=[("X", lt % NX), "tmp"], writes=[("X", lt % NX)], tag="t2add")
            if d.get("x_out") is not None:
                P.dma(STQ, lambda h, lt=lt: h.dma_start(out=d["x_out"][lt * 128:(lt + 1) * 128, :], in_=X[:, lt % NX, :]),
                      ("Xout", lt % 4), reads=[("X", lt % NX)], writes=[("xo", lt)])

        def stage2(lt):
            s, a = lt // 4, lt % 4
            for i, (g, hT_out) in enumerate(t1s):
                sl = s % 2
                P.op("act", lambda h, lt=lt: h.activation(out=junk[:, :], in_=X[:, lt % NX, :], func=AF.Square, accum_out=st_[:, 2:3]),
                     reads=[("X", lt % NX)], writes=["junk", "ssq1"])
                P.op("act", lambda h: h.activation(out=st_[:, 3:4], in_=st_[:, 2:3], func=AF.Sqrt, scale=1.0 / D, bias=EPS),
                     reads=["ssq1"], writes=["rstd1"])
                P.op("dve", lambda h: h.reciprocal(out=st_[:, 3:4], in_=st_[:, 3:4]), reads=["rstd1"], writes=["rstd1"])
                P.op("dve", lambda h, lt=lt, i=i: h.scalar_tensor_tensor(out=hb[:, :], in0=X[:, lt % NX, :], scalar=st_[:, 3:4],
                                                                         in1=G1[i][:, :], op0=ALU.mult, op1=ALU.mult),
                     reads=[("X", lt % NX), "rstd1", "G1_%d" % i], writes=["hb"])
                for kc in range(8):
                    P.op("pe", lambda h, kc=kc: h.transpose(out=psT[:, kc, :], in_=hb[:, kc * 128:(kc + 1) * 128], identity=I[:, :]),
                         reads=["hb", "I"], writes=["psT"])
                P.op("act", lambda h, i=i, sl=sl, a=a: h.copy(out=hTs[i][sl][:, :, a * 128:(a + 1) * 128], in_=psT[:, :, :]),
                     reads=["psT"], writes=[("hTs", i, sl)])
                if a == 3:
                    P.dma(STQ, lambda h, i=i, sl=sl, s=s, hT_out=hT_out: h.dma_start(out=hT_out[s], in_=hTs[i][sl][:, :, :]),
                          ("hTo", i, sl), reads=[("hTs", i, sl)], writes=[("hTd", i, s)])

        stage1(0)
        stage1_post(0)
        for lt in range(TILES_PER_CORE):
            if lt + 3 < TILES_PER_CORE:
                load_x(lt + 3)
            if lt + 1 < TILES_PER_CORE:
                stage1(lt + 1)
            stage2(lt)
            if lt + 1 < TILES_PER_CORE:
                stage1_post(lt + 1)
        _finish(P)


def phase_HA(nc, d):
    with ExitStack() as es:
        A = Alloc(nc, es)
        P = Prog(nc)
        W = A.sb("W", [128, 8, 1024], BF16)
        MA = A.sb("MA", [128, 512], F32)
        CA = A.sb("CA", [128, 16], F32)
        I = A.sb("I", [128, 128], BF16)
        Sst = A.sb("Sst", [128, 256], F32)
        S0b = A.sb("S0b", [128, 4, 256], BF16)
        hTs = [A.sb("hTs%d" % i, [128, 8, ST], BF16) for i in range(2)]
        Ct = [A.sb("Ct%d" % i, [128, ST], F32) for i in range(2)]
        Sn = [A.sb("Sn%d" % i, [128, ST], F32) for i in range(2)]
        t1 = A.sb("t1", [128, ST], F32)
        t2 = A.sb("t2", [128, ST], F32)
        qT = A.sb("qT", [128, ST], BF16)
        kT = A.sb("kT", [128, ST], BF16)
        ktok = A.sb("ktok", [128, 4, 128], BF16)
        vt = A.sb("vt", [128, 4, 256], BF16)
        sg = A.sb("sg", [128, 4, 256], BF16)
        sTb = [A.sb("sTb%d" % b, [128, (4 - b) * 128], BF16) for b in range(4)]
        on = A.sb("on", [128, 4, 256], F32)
        og = A.sb("og", [128, 4, 256], BF16)
        ogT = [A.sb("ogT%d" % i, [128, 2, ST], BF16) for i in range(2)]
        stats = A.sb("stats", [128, 4, 6], F32)
        mv = A.sb("mv", [128, 4, 2], F32)
        rs = A.sb("rs", [128, 4], F32)
        nbv = A.sb("nbv", [128, 4], F32)
        pj = [A.ps("pj%d" % i, [128, 512], F32) for i in range(2)]
        sc = [A.ps("sc%d" % i, [128, 512], F32) for i in range(2)]
        po = A.ps("po", [128, 4, 256], F32)
        pdS = A.ps("pdS", [128, 512], F32)
        ptr = A.ps("ptr", [128, 1024], BF16)

        _load_w_bf16(P, A, W, d["w"], 8, 0, 1024, "W")
        P.dma("sp", lambda h: h.dma_start(out=MA[:, :], in_=d["MA"][:, :]), "MA", writes=["MA"], tag="ldC")
        P.dma("sp", lambda h: h.dma_start(out=CA[:, :], in_=d["CA"][:, :]), "CA", writes=["CA"], tag="ldC")
        P.dma("sp", lambda h: h.dma_start(out=I[:, :], in_=d["ident"][:, :]), "I", writes=["I"], tag="ldC")

        def load(st):
            sl = st % 2
            sti = st % ST_PER_BATCH
            P.dma("sp", lambda h: h.dma_start(out=hTs[sl][:, :, :], in_=d["hT_all"][st]), ("hTs", sl), writes=[("hTs", sl)], tag="ldS")
            P.dma("sp", lambda h: h.dma_start(out=Ct[sl][:, :], in_=d["cos"][sti]), ("Ct", sl), writes=[("Ct", sl)], tag="ldS2")
            P.dma("sp", lambda h: h.dma_start(out=Sn[sl][:, :], in_=d["sin"][sti]), ("Sn", sl), writes=[("Sn", sl)], tag="ldS2")

        def proj_rope(sl, c0, dst, dname):
            for j in range(2):
                for kc in range(8):
                    P.op("pe", lambda h, j=j, kc=kc: h.matmul(pj[j][:, :], lhsT=W[:, kc, c0 + j * 128:c0 + (j + 1) * 128],
                                                             rhs=hTs[sl][:, kc, :], start=(kc == 0), stop=(kc == 7)),
                         reads=[("W", kc), ("hTs", sl)], writes=[("pj", j)])
            P.op("dve", lambda h: h.tensor_tensor(out=t1[:, :], in0=pj[0][:, :], in1=Ct[sl][:, :], op=ALU.mult),
                 reads=[("pj", 0), ("Ct", sl)], writes=["t1"])
            P.op("dve", lambda h: h.tensor_tensor(out=t2[:, :], in0=pj[1][:, :], in1=Sn[sl][:, :], op=ALU.mult),
                 reads=[("pj", 1), ("Sn", sl)], writes=["t2"])
            P.op(ELT, lambda h: h.tensor_tensor(out=dst[:, :], in0=t1[:, :], in1=t2[:, :], op=ALU.add),
                 reads=["t1", "t2"], writes=[dname])

        load(0)

        def step(st):
            sl = st % 2
            if st + 1 < NST:
                load(st + 1)
            if st % ST_PER_BATCH == 0:
                P.op("dve", lambda h: h.memset(Sst[:, :], 0.0), writes=["S"], tag="ms")
            P.cur_tag = "s0b"
            for a in range(4):
                P.op("act", lambda h, a=a: h.activation(out=S0b[:, a, :], in_=Sst[:, :], func=AF.Copy, scale=CA[:, 2 + a:3 + a]),
                     reads=["S", "CA"], writes=[("S0b", a)])
            P.cur_tag = "rope"
            proj_rope(sl, 0, qT, "qT")
            proj_rope(sl, 256, kT, "kT")
            P.cur_tag = "ktok"
            for a in range(4):
                P.op("pe", lambda h, a=a: h.transpose(out=ptr[:, a * 128:(a + 1) * 128], in_=kT[:, a * 128:(a + 1) * 128],
                                                      identity=I[:, :]),
                     reads=["kT", "I"], writes=["ptr"])
            P.op("dve", lambda h: h.tensor_tensor(out=ktok[:, :, :], in0=ptr[:, 0:512].rearrange("p (a t) -> p a t", a=4),
                                                  in1=CA[:, 6:10].unsqueeze(2).to_broadcast([128, 4, 128]), op=ALU.mult),
                 reads=["ptr", "CA"], writes=["ktok"])
            P.cur_tag = "vg"
            for a in range(4):
                j = a % 2
                for kc in range(8):
                    P.op("pe", lambda h, a=a, j=j, kc=kc: h.matmul(pj[j][:, :], lhsT=hTs[sl][:, kc, a * 128:(a + 1) * 128],
                                                                   rhs=W[:, kc, 512:1024], start=(kc == 0), stop=(kc == 7)),
                         reads=[("W", kc), ("hTs", sl)], writes=[("pj", j)])
                P.op("dve", lambda h, a=a, j=j: h.tensor_scalar_mul(out=vt[:, a, :], in0=pj[j][:, 0:256], scalar1=CA[:, 0:1]),
                     reads=[("pj", j), "CA"], writes=[("vt", a)])
                P.op("act", lambda h, a=a, j=j: h.activation(out=sg[:, a, :], in_=pj[j][:, 256:512], func=AF.Silu),
                     reads=[("pj", j), ("vt", a)], writes=[("sg", a)])
            P.cur_tag = "scr"
            place = {0: (0, 0), 1: (1, 0), 3: (1, 384), 2: (0, 0)}
            for b in (0, 1, 3, 2):
                n = (4 - b) * 128
                bank, off = place[b]
                P.op("pe", lambda h, b=b, n=n, bank=bank, off=off: h.matmul(
                    sc[bank][:, off:off + n], lhsT=kT[:, b * 128:(b + 1) * 128], rhs=qT[:, b * 128:512], start=True, stop=True),
                    reads=["kT", "qT"], writes=[("sc", bank)])
                P.op("dve", lambda h, b=b, n=n, bank=bank, off=off: h.tensor_tensor(
                    out=sTb[b][:, :], in0=sc[bank][:, off:off + n], in1=MA[:, 0:n], op=ALU.mult),
                    reads=[("sc", bank), "MA"], writes=[("sTb", b)])
            P.cur_tag = "dS"
            for b in range(4):
                P.op("pe", lambda h, b=b: h.matmul(pdS[:, 0:256], lhsT=ktok[:, b, :], rhs=vt[:, b, :], start=(b == 0), stop=(b == 3)),
                     reads=["ktok", ("vt", b)], writes=["pdS"])
            P.cur_tag = "po"
            for a in range(4):
                for b in range(a + 1):
                    P.op("pe", lambda h, a=a, b=b: h.matmul(po[:, a, :], lhsT=sTb[b][:, (a - b) * 128:(a - b + 1) * 128],
                                                            rhs=vt[:, b, :], start=(b == 0), stop=False),
                         reads=[("sTb", b), ("vt", b)], writes=[("po", a // 2)])
                P.op("pe", lambda h, a=a: h.matmul(po[:, a, :], lhsT=qT[:, a * 128:(a + 1) * 128], rhs=S0b[:, a, :],
                                                   start=False, stop=True),
                     reads=["qT", ("S0b", a)], writes=[("po", a // 2)])
            P.cur_tag = "Supd"
            P.op("dve", lambda h: h.scalar_tensor_tensor(out=Sst[:, :], in0=Sst[:, :], scalar=CA[:, 10:11], in1=pdS[:, 0:256],
                                                         op0=ALU.mult, op1=ALU.add),
                 reads=["S", "CA", "pdS"], writes=["S"])
            P.cur_tag = "gn"
            for a in range(4):
                P.op("dve", lambda h, a=a: h.bn_stats(out=stats[:, a, :], in_=po[:, a, :]), reads=[("po", a // 2)], writes=[("stats", a)])
                P.op("dve", lambda h, a=a: h.bn_aggr(out=mv[:, a, :], in_=stats[:, a, :]), reads=[("stats", a)], writes=["mv"])
            P.op("act", lambda h: h.activation(out=rs[:, :], in_=mv[:, :, 1], func=AF.Sqrt, bias=CA[:, 1:2], scale=1.0),
                 reads=["mv", "CA"], writes=["rs"])
            P.op("dve", lambda h: h.reciprocal(out=rs[:, :], in_=rs[:, :]), reads=["rs"], writes=["rs"])
            P.op("dve", lambda h: h.scalar_tensor_tensor(out=nbv[:, :], in0=mv[:, :, 0], scalar=-1.0, in1=rs[:, :],
                                                         op0=ALU.mult, op1=ALU.mult),
                 reads=["mv", "rs"], writes=["nbv"])
            P.cur_tag = "on"
            for a in range(4):
                P.op("act", lambda h, a=a: h.activation(out=on[:, a, :], in_=po[:, a, :], func=AF.Identity,
                                                        bias=nbv[:, a:a + 1], scale=rs[:, a:a + 1]),
                     reads=[("po", a // 2), "rs", "nbv"], writes=[("on", a)])
            P.cur_tag = "og"
            P.op(ELT, lambda h: h.tensor_tensor(out=og[:, :, :], in0=on[:, :, :], in1=sg[:, :, :], op=ALU.mult),
                 reads=[("on", a) for a in range(4)] + [("sg", a) for a in range(4)], writes=["og"])
            P.cur_tag = "ogT"
            for a in range(4):
                for c in range(2):
                    P.op("pe", lambda h, a=a, c=c: h.transpose(out=ptr[:, c * 512 + a * 128:c * 512 + (a + 1) * 128],
                                                               in_=og[:, a, c * 128:(c + 1) * 128], identity=I[:, :]),
                         reads=["og", "I"], writes=["ptr"])
            P.op("act", lambda h: h.copy(out=ogT[sl][:, :, :], in_=ptr[:, :].rearrange("p (c t) -> p c t", c=2)),
                 reads=["ptr"], writes=[("ogT", sl)])
            P.cur_tag = None
            P.dma(STQ, lambda h, st=st: h.dma_start(out=d["og_out"][st // ST_PER_CORE, :, st % ST_PER_CORE], in_=ogT[sl][:, :, :]), ("ogo", sl),
                  reads=[("ogT", sl)], writes=[("ogd", st)], tag="stO")

        for st in range(d.get("nst", NST)):
            step(st)
        _finish(P)


def phase_HB(nc, d):
    lam_init = d["lam_init"]
    with ExitStack() as es:
        A = Alloc(nc, es)
        P = Prog(nc)
        KT = A.sb("KT", [128, NTOK], BF16)
        V = A.sb("V", [128, NTOK // 128, 128], BF16)
        WB = A.sb("WB", [128, 8, 384], BF16)
        WK = A.sb("WK", [128, 8, 384], BF16)
        ones = A.sb("ones", [128, 128], BF16)
        onesF = A.sb("onesF", [128, 128], F32)
        lamv = A.sb("lamv", [128, 2, 2, 64], F32)
        lamt = A.sb("lamt", [128, 2, 64], F32)
        lams = A.sb("lams", [128, 4], F32)
        subg = A.sb("subg", [128, 2], F32)
        hTs = [A.sb("hTs%d" % i, [128, 8, ST], BF16) for i in range(2)]
        Ct = [A.sb("Ct%d" % i, [128, ST], F32) for i in range(2)]
        Sn = [A.sb("Sn%d" % i, [128, ST], F32) for i in range(2)]
        t1 = A.sb("t1", [128, ST], F32)
        t2 = A.sb("t2", [128, ST], F32)
        qT = A.sb("qT", [128, ST], BF16)
        sgT = A.sb("sgT", [128, ST], F32)
        PT = [A.sb("PT%d" % i, [128, 2, ST], BF16) for i in range(2)]
        r1 = A.sb("r1", [128, ST], F32)
        r2 = A.sb("r2", [128, ST], F32)
        a1 = A.sb("a1", [128, ST], F32)
        a2 = A.sb("a2", [128, ST], F32)
        sq = A.sb("sq", [128, ST], F32)
        ogT = [A.sb("ogT%d" % i, [128, 1, ST], BF16) for i in range(2)]
        sc = [A.ps("sc%d" % i, [128, 2, ST], F32) for i in range(2)]
        po = [A.ps("po%d" % i, [128, ST], F32) for i in range(2)]
        pl = [A.ps("pl%d" % i, [128, ST], F32) for i in range(2)]

        _load_w_bf16(P, A, WB, d["w"], 8, 0, 384, "WB")
        _load_w_bf16(P, A, WK, d["wkv"], 8, 0, 384, "WK")
        P.dma("sp", lambda h: h.dma_start(out=lamv[:, :, :, :], in_=d["lamv"][:, :, :, :]), "lamv", writes=["lamv"])
        P.dma("sp", lambda h: h.dma_start(out=subg[:, 0:1], in_=d["subg"][:, :]), "subg", writes=["subg"])
        P.op("dve", lambda h: h.memset(ones[:, :], 1.0), writes=["ones"])
        P.op("dve", lambda h: h.memset(onesF[:, :], 1.0), writes=["onesF"])
        P.op("dve", lambda h: h.tensor_tensor(out=lamt[:, :, :], in0=lamv[:, 0, :, :], in1=lamv[:, 1, :, :], op=ALU.mult),
             reads=["lamv"], writes=["lamt"])
        P.op("dve", lambda h: h.reduce_sum(out=lams[:, 0:2], in_=lamt[:, :, :], axis=AX.X), reads=["lamt"], writes=["lams"])
        P.op("act", lambda h: h.activation(out=lams[:, 0:2], in_=lams[:, 0:2], func=AF.Exp), reads=["lams"], writes=["lams"])
        P.op("dve", lambda h: h.tensor_tensor(out=lams[:, 2:3], in0=lams[:, 1:2], in1=lams[:, 0:1], op=ALU.subtract),
             reads=["lams"], writes=["lams"])
        P.op("dve", lambda h: h.tensor_scalar_add(out=lams[:, 3:4], in0=lams[:, 2:3], scalar1=-lam_init),
             reads=["lams"], writes=["neglam"])
        P.op("dve", lambda h: h.tensor_scalar_mul(out=subg[:, 1:2], in0=subg[:, 0:1], scalar1=1.0 - lam_init),
             reads=["subg"], writes=["subgs"])

        def load(src, st, with_h=True):
            sl = st % 2
            sti = st % ST_PER_BATCH
            P.dma("sp", lambda h: h.dma_start(out=hTs[sl][:, :, :], in_=src[st]), ("hTs", sl), writes=[("hTs", sl)])
            P.dma("sp", lambda h: h.dma_start(out=Ct[sl][:, :], in_=d["cos"][sti]), ("Ct", sl), writes=[("Ct", sl)])
            P.dma("sp", lambda h: h.dma_start(out=Sn[sl][:, :], in_=d["sin"][sti]), ("Sn", sl), writes=[("Sn", sl)])

        def proj_rope(Wt, wkey, sl, dst_ap, dname, bank):
            for j in range(2):
                for kc in range(8):
                    P.op("pe", lambda h, j=j, kc=kc: h.matmul(sc[bank][:, j, :], lhsT=Wt[:, kc, j * 128:(j + 1) * 128],
                                                             rhs=hTs[sl][:, kc, :], start=(kc == 0), stop=(kc == 7)),
                         reads=[(wkey, kc), ("hTs", sl)], writes=[("sc", bank)])
            P.op("dve", lambda h: h.tensor_tensor(out=t1[:, :], in0=sc[bank][:, 0, :], in1=Ct[sl][:, :], op=ALU.mult),
                 reads=[("sc", bank), ("Ct", sl)], writes=["t1"])
            P.op("dve", lambda h: h.tensor_tensor(out=t2[:, :], in0=sc[bank][:, 1, :], in1=Sn[sl][:, :], op=ALU.mult),
                 reads=[("sc", bank), ("Sn", sl)], writes=["t2"])
            P.op(ELT, lambda h: h.tensor_tensor(out=dst_ap, in0=t1[:, :], in1=t2[:, :], op=ALU.add),
                 reads=["t1", "t2"], writes=[dname])

        nst = d.get("nst", NST)
        load(d["hTkv_all"], 0)

        def kv_step(st):
            sl = st % 2
            if st + 1 < nst:
                load(d["hTkv_all"], st + 1)
            else:
                load(d["hT_all"], 0)
            proj_rope(WK, "WK", sl, KT[:, st * ST:(st + 1) * ST], ("KT", st), 0)
            for a in range(4):
                for kc in range(8):
                    P.op("pe", lambda h, a=a, kc=kc: h.matmul(sc[1][:, 0, a * 128:(a + 1) * 128], lhsT=hTs[sl][:, kc, a * 128:(a + 1) * 128],
                                                              rhs=WK[:, kc, 256:384], start=(kc == 0), stop=(kc == 7)),
                         reads=[("WK", kc), ("hTs", sl)], writes=[("sc", 1)])
            P.op("act", lambda h: h.copy(out=V[:, st * 4:(st + 1) * 4, :], in_=sc[1][:, 0, :].rearrange("p (a e) -> p a e", a=4)),
                 reads=[("sc", 1)], writes=[("V", st)])

        for st in range(nst):
            kv_step(st)

        def group(g):
            sl = g % 2
            gi = g % ST_PER_BATCH
            bb = g // ST_PER_BATCH
            if g + 1 < nst:
                load(d["hT_all"], g + 1)
            proj_rope(WB, "WB", sl, qT[:, :], "qT", 0)
            for kc in range(8):
                P.op("pe", lambda h, kc=kc: h.matmul(sc[1][:, 0, :], lhsT=WB[:, kc, 256:384], rhs=hTs[sl][:, kc, :],
                                                     start=(kc == 0), stop=(kc == 7)),
                     reads=[("WB", kc), ("hTs", sl)], writes=[("sc", 1)])
            P.op("act", lambda h: h.activation(out=sgT[:, :], in_=sc[1][:, 0, :], func=AF.Silu), reads=[("sc", 1)], writes=["sgT"])
            nk = 4 * (gi + 1)

            def scores(kt):
                p = kt % 2
                T = bb * (S // 128) + kt
                q0 = max(0, kt - 4 * gi) * 128
                for t in range(2):
                    P.op("pe", lambda h, t=t: h.matmul(sc[p][:, t, q0:ST], lhsT=KT[t * 64:(t + 1) * 64, T * 128:(T + 1) * 128],
                                                       rhs=qT[t * 64:(t + 1) * 64, q0:ST], start=True, stop=True),
                         reads=[("KT", T // 4), "qT"], writes=[("sc", p)], tag="b_sc")
                P.op("act", lambda h: h.activation(out=PT[p][:, :, q0:ST], in_=sc[p][:, :, q0:ST], func=AF.Exp, scale=0.125),
                     reads=[("sc", p)], writes=[("PT", p)], tag="b_exp")
                if kt >= 4 * gi:
                    P.op(ELT, lambda h: h.memset(PT[p][64:128, :, q0:q0 + 64], 0.0), reads=[("PT", p)], writes=[("PT", p)], tag="b_ms")

            def pv(kt):
                p = kt % 2
                T = bb * (S // 128) + kt
                q0 = max(0, kt - 4 * gi) * 128
                for t in range(2):
                    P.op("pe", lambda h, t=t: h.matmul(po[t][:, q0:ST], lhsT=V[:, T, :], rhs=PT[p][:, t, q0:ST],
                                                       start=(kt == 0), stop=(kt == nk - 1)),
                         reads=[("V", T // 4), ("PT", p)], writes=[("po", t)], tag="b_pv")
                    P.op("pe", lambda h, t=t: h.matmul(pl[t][:, q0:ST], lhsT=ones[:, :], rhs=PT[p][:, t, q0:ST],
                                                       start=(kt == 0), stop=(kt == nk - 1)),
                         reads=["ones", ("PT", p)], writes=[("pl", t)], tag="b_ones")

            for kt in range(nk):
                scores(kt)
                if kt > 0:
                    pv(kt - 1)
            pv(nk - 1)
            P.op("dve", lambda h: h.reciprocal(out=r1[:, :], in_=pl[0][:, :]), reads=[("pl", 0)], writes=["r1"])
            P.op("dve", lambda h: h.reciprocal(out=r2[:, :], in_=pl[1][:, :]), reads=[("pl", 1)], writes=["r2"])
            P.op("dve", lambda h: h.tensor_tensor(out=a1[:, :], in0=po[0][:, :], in1=r1[:, :], op=ALU.mult),
                 reads=[("po", 0), "r1"], writes=["a1"])
            P.op("dve", lambda h: h.tensor_tensor(out=a2[:, :], in0=po[1][:, :], in1=r2[:, :], op=ALU.mult),
                 reads=[("po", 1), "r2"], writes=["a2"])
            P.op("dve", lambda h: h.scalar_tensor_tensor(out=a1[:, :], in0=a2[:, :], scalar=lams[:, 3:4], in1=a1[:, :],
                                                         op0=ALU.mult, op1=ALU.add),
                 reads=["a1", "a2", "neglam"], writes=["a1"])
            P.op(ELT, lambda h: h.tensor_tensor(out=sq[:, :], in0=a1[:, :], in1=a1[:, :], op=ALU.mult), reads=["a1"], writes=["sq"])
            P.op("pe", lambda h: h.matmul(sc[1][:, 1, :], lhsT=onesF[:, :], rhs=sq[:, :], start=True, stop=True),
                 reads=["onesF", "sq"], writes=[("sc", 1)])
            P.op("act", lambda h: h.activation(out=r1[:, :], in_=sc[1][:, 1, :], func=AF.Sqrt, scale=1.0 / 128, bias=EPS),
                 reads=[("sc", 1)], writes=["r1"])
            P.op("dve", lambda h: h.reciprocal(out=r1[:, :], in_=r1[:, :]), reads=["r1"], writes=["r1"])
            P.op("dve", lambda h: h.scalar_tensor_tensor(out=a2[:, :], in0=a1[:, :], scalar=subg[:, 1:2], in1=r1[:, :],
                                                         op0=ALU.mult, op1=ALU.mult),
                 reads=["a1", "subgs", "r1"], writes=["a2"])
            P.op(ELT, lambda h: h.tensor_tensor(out=ogT[sl][:, 0, :], in0=a2[:, :], in1=sgT[:, :], op=ALU.mult),
                 reads=["a2", "sgT"], writes=[("ogT", sl)])
            P.dma(STQ, lambda h: h.dma_start(out=d["og_out"][g // ST_PER_CORE, :, g % ST_PER_CORE], in_=ogT[sl][:, :, :]), ("ogo", sl),
                  reads=[("ogT", sl)], writes=[("ogd", g)])

        for g in range(nst):
            group(g)
        _finish(P)


def _dram(nc, name, shape, dt, kind):
    return nc.dram_tensor(name, list(shape), dt, kind=kind).ap()


def _swap_cols(w, blk):
    k, n = w.shape
    return np.ascontiguousarray(w.reshape(k, n // blk, 2, blk // 2)[:, :, ::-1, :].reshape(k, n))


_IDENT = _bf16(np.eye(128, dtype=np.float32))


def _run(nc, in_maps):
    res = run_bass_kernel_spmd(nc, in_maps, core_ids=list(range(NCORES)))
    return res.results


def _build_T(x_in, x_out, t2_nkc, n_t1):
    nc = bass.Bass("TRN2", target_bir_lowering=False)
    d = {"ident": _dram(nc, "ident", [128, 128], BF16, "ExternalInput")}
    if x_in:
        d["x_in"] = _dram(nc, "x_in", [TOK_PER_CORE, D], F32, "ExternalInput")
    if t2_nkc:
        nch = t2_nkc // 8
        d["t2"] = dict(og=_dram(nc, "og", [8, 128, ST_PER_CORE, nch, ST], BF16, "ExternalInput"),
                       wout=_dram(nc, "wout", [t2_nkc * 128, D], F32, "ExternalInput"),
                       gpost=_dram(nc, "gpost", [128, D], F32, "ExternalInput"), nkc=t2_nkc)
    d["t1"] = [(_dram(nc, "g1_%d" % i, [128, D], F32, "ExternalInput"),
                _dram(nc, "hT_%d" % i, [ST_PER_CORE, 128, 8, ST], BF16, "ExternalOutput")) for i in range(n_t1)]
    if x_out:
        d["x_out"] = _dram(nc, "x_out", [TOK_PER_CORE, D], F32, "ExternalOutput")
    phase_T(nc, None, d)
    return nc


def _build_HA(nst=NST):
    nc = bass.Bass("TRN2", target_bir_lowering=False)
    d = dict(hT_all=_dram(nc, "hT_all", [NST, 128, 8, ST], BF16, "ExternalInput"),
             w=_dram(nc, "w", [D, 1024], F32, "ExternalInput"),
             MA=_dram(nc, "MA", [128, 512], F32, "ExternalInput"),
             CA=_dram(nc, "CA", [128, 16], F32, "ExternalInput"),
             cos=_dram(nc, "cos", [ST_PER_BATCH, 128, ST], F32, "ExternalInput"),
             sin=_dram(nc, "sin", [ST_PER_BATCH, 128, ST], F32, "ExternalInput"),
             ident=_dram(nc, "ident", [128, 128], BF16, "ExternalInput"),
             og_out=_dram(nc, "og_out", [NCORES, 128, ST_PER_CORE, 2, ST], BF16, "ExternalOutput"), nst=nst)
    phase_HA(nc, d)
    return nc


def _build_HB(lam_init, nst=NST):
    nc = bass.Bass("TRN2", target_bir_lowering=False)
    d = dict(hT_all=_dram(nc, "hT_all", [NST, 128, 8, ST], BF16, "ExternalInput"),
             hTkv_all=_dram(nc, "hTkv_all", [NST, 128, 8, ST], BF16, "ExternalInput"),
             w=_dram(nc, "w", [D, 384], F32, "ExternalInput"),
             wkv=_dram(nc, "wkv", [D, 384], F32, "ExternalInput"),
             cos=_dram(nc, "cos", [ST_PER_BATCH, 128, ST], F32, "ExternalInput"),
             sin=_dram(nc, "sin", [ST_PER_BATCH, 128, ST], F32, "ExternalInput"),
             lamv=_dram(nc, "lamv", [128, 2, 2, 64], F32, "ExternalInput"),
             subg=_dram(nc, "subg", [128, 1], F32, "ExternalInput"),
             og_out=_dram(nc, "og_out", [NCORES, 128, ST_PER_CORE, 1, ST], BF16, "ExternalOutput"),
             lam_init=lam_init, nst=nst)
    phase_HB(nc, d)
    return nc


def phase_AG(nc, src, dst, cc):
    P = Prog(nc)
    P.ext["cc"] = cc
    P.dma("pool", lambda h: h.collective_compute("AllGather", ALU.bypass, replica_groups=[list(range(NCORES))],
                                                 ins=[src.opt()], outs=[dst.opt()]),
          "cc", writes=["dst"], inc=1)
    _finish(P)


LAM_INIT = [0.8 - 0.6 * math.exp(-0.3 * layer) for layer in range(4)]


def _build_fused(nlayers=4, nst=NST):
    nc = bass.Bass("TRN2", target_bir_lowering=False)
    I = lambda name, shape, dt=F32: _dram(nc, name, shape, dt, "ExternalInput")
    N = lambda name, shape, dt=BF16: nc.dram_tensor(name, list(shape), dt).ap()
    ident = I("ident", [128, 128], BF16)
    x_in = I("x_in", [TOK_PER_CORE, D])
    x_out = _dram(nc, "x_out", [TOK_PER_CORE, D], F32, "ExternalOutput")
    pre = [I("pre%d" % l, [128, D]) for l in range(4)]
    post = [I("post%d" % l, [128, D]) for l in range(4)]
    kvg = I("kvg", [128, D])
    wA = [I("wA%d" % l, [D, 1024]) for l in range(2)]
    wB = [I("wB%d" % j, [D, 384]) for j in range(2)]
    wkv = I("wkv", [D, 384])
    wout = [I("wout%d" % l, [2048 if l < 2 else 1024, D]) for l in range(4)]
    MA, CA = I("MA", [128, 512]), I("CA", [128, 16])
    cosA, sinA = I("cosA", [ST_PER_BATCH, 128, ST]), I("sinA", [ST_PER_BATCH, 128, ST])
    cosB, sinB = I("cosB", [ST_PER_BATCH, 128, ST]), I("sinB", [ST_PER_BATCH, 128, ST])
    lamv = [I("lamv%d" % j, [128, 2, 2, 64]) for j in range(2)]
    subg = [I("subg%d" % j, [128, 1]) for j in range(2)]
    _hl, _ha = N("hT_loc", [ST_PER_CORE, 128, 8, ST]), N("hT_all", [NST, 128, 8, ST])
    hT_loc, hT_all = [_hl] * 4, [_ha] * 4
    hTkv_loc, hTkv_all = N("hTkv_loc", [ST_PER_CORE, 128, 8, ST]), N("hTkv_all", [NST, 128, 8, ST])
    nch = [2, 2, 1, 1]
    _ol = {2: N("og_locA", [NCORES, 128, ST_PER_CORE, 2, ST]), 1: N("og_locB", [NCORES, 128, ST_PER_CORE, 1, ST])}
    _oa = {2: N("og_allA", [8, NCORES, 128, ST_PER_CORE, 2, ST]), 1: N("og_allB", [8, NCORES, 128, ST_PER_CORE, 1, ST])}
    og_loc = [_ol[nch[l]] for l in range(4)]
    og_all = [_oa[nch[l]] for l in range(4)]
    with ExitStack() as es0:
        X = None
        xbuf = nc.dram_tensor("xbuf", [TOK_PER_CORE, D], F32).ap()
        pads = [es0.enter_context(nc.semaphore("ccpad%d" % i)) for i in range(8)]
        assert pads[7].num == 162, pads[7].num
        cc = dict(sem=pads[7], count=0)
        phase_T(nc, None, dict(ident=ident, x_in=x_in, t1=[(pre[0], hT_loc[0])]))
        for l in range(nlayers):
            phase_AG(nc, hT_loc[l], hT_all[l], cc)
            if l == 2:
                phase_AG(nc, hTkv_loc, hTkv_all, cc)
            if l < 2:
                phase_HA(nc, dict(hT_all=hT_all[l], w=wA[l], MA=MA, CA=CA, cos=cosA, sin=sinA, ident=ident, og_out=og_loc[l], nst=nst))
            else:
                phase_HB(nc, dict(hT_all=hT_all[l], hTkv_all=hTkv_all, w=wB[l - 2], wkv=wkv, cos=cosB, sin=sinB,
                                  lamv=lamv[l - 2], subg=subg[l - 2], og_out=og_loc[l], lam_init=LAM_INIT[l], nst=nst))
            phase_AG(nc, og_loc[l], og_all[l], cc)
            d = dict(ident=ident, x_in=(x_in if l == 0 else xbuf), x_out=xbuf,
                     t2=dict(og=og_all[l], dyn=True, dynq="sp", whole=(l >= 2),
                                          wout=wout[l], gpost=post[l], nkc=8 * nch[l]))
            if l < nlayers - 1:
                d["t1"] = [(pre[l + 1], hT_loc[l + 1])] + ([(kvg, hTkv_loc)] if l == 1 else [])
            else:
                d["x_out"] = x_out
            phase_T(nc, X, d)
    return nc


def kernel_fused(x, pre_norm, post_norm, w_in_a, w_out_a, kv_norm, w_kv, w_in_b,
                 lam_q1, lam_k1, lam_q2, lam_k2, sub_norm_b, w_out_b):
    xs = np.split(np.ascontiguousarray(x.reshape(NTOK, D)), NCORES, axis=0)
    cosA, sinA = _rope_tables(64, 128)
    cosB, sinB = _rope_tables(32, 128)
    nc = _build_fused(int(os.environ.get("KNL", "4")), int(os.environ.get("KNST", str(NST))))
    maps = []
    for r in range(NCORES):
        MA, CA = _ret_consts(r)
        m = {"ident": _IDENT, "x_in": xs[r], "kvg": _rep(kv_norm), "wkv": _wKV(w_kv, r), "MA": MA, "CA": CA,
             "cosA": cosA, "sinA": sinA, "cosB": cosB, "sinB": sinB}
        for l in range(4):
            m["pre%d" % l] = _rep(pre_norm[l])
            m["post%d" % l] = _rep(post_norm[l])
            m["wout%d" % l] = np.ascontiguousarray(w_out_a[l] if l < 2 else w_out_b[l - 2])
        for l in range(2):
            m["wA%d" % l] = _wA(w_in_a[l], r)
            m["wB%d" % l] = _wB(w_in_b[l], r)
            lamv = np.stack([np.stack([lam_q1[l], lam_q2[l]]), np.stack([lam_k1[l], lam_k2[l]])])
            m["lamv%d" % l] = np.ascontiguousarray(np.broadcast_to(lamv[None], (128, 2, 2, 64))).astype(np.float32)
            m["subg%d" % l] = np.ascontiguousarray(sub_norm_b[l].reshape(128, 1))
        maps.append(m)
    res = _run(nc, maps)
    return np.concatenate([np.asarray(r["x_out"]) for r in res], axis=0).reshape(B, S, D).astype(np.float32)


def _wA(w_in, h):
    q = w_in[:, h * 128:(h + 1) * 128]
    k = w_in[:, 1024 + h * 128:1024 + (h + 1) * 128]
    v = w_in[:, 2048 + h * 256:2048 + (h + 1) * 256]
    g = w_in[:, 4096 + h * 256:4096 + (h + 1) * 256]
    return np.ascontiguousarray(np.concatenate([q, _swap_cols(q, 128), k, _swap_cols(k, 128), v, g], axis=1))


def _wB(w_in, h):
    q = w_in[:, h * 128:(h + 1) * 128]
    g = w_in[:, 1024 + h * 128:1024 + (h + 1) * 128]
    return np.ascontiguousarray(np.concatenate([q, _swap_cols(q, 64), g], axis=1))


def _wKV(w_kv, h):
    k = w_kv[:, h * 128:(h + 1) * 128]
    v = w_kv[:, 1024 + h * 128:1024 + (h + 1) * 128]
    return np.ascontiguousarray(np.concatenate([k, _swap_cols(k, 64), v], axis=1))


def _gather_hT(res, key):
    return np.ascontiguousarray(np.concatenate([np.asarray(r[key]) for r in res], axis=0))


def _a2a(res, key):
    ogs = [np.asarray(r[key]) for r in res]
    return [np.ascontiguousarray(np.stack([ogs[h][r] for h in range(8)], axis=0)) for r in range(NCORES)]


FUSED = os.environ.get("KFUSED", "0") == "1"


def kernel(x, pre_norm, post_norm, w_in_a, w_out_a, kv_norm, w_kv, w_in_b,
           lam_q1, lam_k1, lam_q2, lam_k2, sub_norm_b, w_out_b):
    f = lambda a: np.asarray(a, dtype=np.float32)
    x, pre_norm, post_norm, w_in_a, w_out_a, kv_norm, w_kv, w_in_b = map(f, (x, pre_norm, post_norm, w_in_a, w_out_a, kv_norm, w_kv, w_in_b))
    lam_q1, lam_k1, lam_q2, lam_k2, sub_norm_b, w_out_b = map(f, (lam_q1, lam_k1, lam_q2, lam_k2, sub_norm_b, w_out_b))
    if FUSED:
        return kernel_fused(x, pre_norm, post_norm, w_in_a, w_out_a, kv_norm, w_kv, w_in_b,
                            lam_q1, lam_k1, lam_q2, lam_k2, sub_norm_b, w_out_b)
    xs = np.split(np.ascontiguousarray(x.reshape(NTOK, D)), NCORES, axis=0)
    cosA, sinA = _rope_tables(64, 128)
    cosB, sinB = _rope_tables(32, 128)
    retc = [_ret_consts(h) for h in range(8)]

    nc = _build_T(True, False, 0, 1)
    res = _run(nc, [{"ident": _IDENT, "x_in": xs[r], "g1_0": _rep(pre_norm[0])} for r in range(NCORES)])
    hT = _gather_hT(res, "hT_0")
    ncHA = _build_HA()
    hTkv = None
    for layer in range(4):
        if layer < 2:
            res = _run(ncHA, [{"hT_all": hT, "w": _wA(w_in_a[layer], h), "MA": retc[h][0], "CA": retc[h][1],
                               "cos": cosA, "sin": sinA, "ident": _IDENT} for h in range(NCORES)])
            wout, nkc = w_out_a[layer], 16
        else:
            j = layer - 2
            lam_init = 0.8 - 0.6 * math.exp(-0.3 * layer)
            ncHB = _build_HB(lam_init)
            lamv = np.stack([np.stack([lam_q1[j], lam_q2[j]]), np.stack([lam_k1[j], lam_k2[j]])])
            lamv = np.ascontiguousarray(np.broadcast_to(lamv[None], (128, 2, 2, 64))).astype(np.float32)
            res = _run(ncHB, [{"hT_all": hT, "hTkv_all": hTkv, "w": _wB(w_in_b[j], h), "wkv": _wKV(w_kv, h),
                               "cos": cosB, "sin": sinB, "lamv": lamv,
                               "subg": np.ascontiguousarray(sub_norm_b[j].reshape(128, 1))} for h in range(NCORES)])
            wout, nkc = w_out_b[j], 8
        ogr = _a2a(res, "og_out")
        last = layer == 3
        n_t1 = 0 if last else (2 if layer == 1 else 1)
        ncT = _build_T(True, True, nkc, n_t1)
        maps = []
        for r in range(NCORES):
            m = {"ident": _IDENT, "x_in": xs[r], "og": ogr[r], "wout": np.ascontiguousarray(wout), "gpost": _rep(post_norm[layer])}
            if n_t1 >= 1:
                m["g1_0"] = _rep(pre_norm[layer + 1])
            if n_t1 == 2:
                m["g1_1"] = _rep(kv_norm)
            maps.append(m)
        res = _run(ncT, maps)
        xs = [np.asarray(r["x_out"]) for r in res]
        if n_t1 >= 1:
            hT = _gather_hT(res, "hT_0")
        if n_t1 == 2:
            hTkv = _gather_hT(res, "hT_1")
    return np.concatenate(xs, axis=0).reshape(B, S, D).astype(np.float32)
```

```python
import math
from contextlib import ExitStack

import numpy as np
import ml_dtypes

import concourse.bass as bass
import concourse.mybir as mybir
from concourse.bass_utils import run_bass_kernel_spmd

F32 = mybir.dt.float32
BF16 = mybir.dt.bfloat16
AF = mybir.ActivationFunctionType
ALU = mybir.AluOpType
AX = mybir.AxisListType

NCORES = 8
D = 1024
B = 2
S = 8192
NTOK = B * S
EPS = 1e-6
TOK_PER_CORE = NTOK // NCORES
TILES_PER_CORE = TOK_PER_CORE // 128
ST = 512
NST = NTOK // ST
ST_PER_CORE = TOK_PER_CORE // ST
ST_PER_BATCH = S // ST

SAME_ENGINE_SYNC = True


import os
SKIP = set(os.environ.get("KSKIP", "").split(","))
ELT = os.environ.get("KELT", "dve")
STQ = os.environ.get("KSTQ", "sp")


PSUM_KEYS = {"pj", "sc", "po", "pdS", "ptr", "Y", "psT", "pl"}


class Prog:
    ENGS = ("pe", "act", "dve", "pool", "sp")
    n_emit = 0

    def __init__(self, nc):
        self.nc = nc
        self.ops = {e: [] for e in self.ENGS}
        self.lastw = {}
        self.readers = {}
        self.dma_cnt = {}
        self.cur_tag = None
        self.ext = {}

    def _deps(self, eng, reads, writes):
        toks = set()
        for r in reads:
            if r in self.lastw:
                toks.add(self.lastw[r])
            if (r if isinstance(r, str) else r[0]) in PSUM_KEYS:
                for t in self.readers.get(r, ()):
                    if t[0] == "eng" and t[1] != eng:
                        toks.add(t)
        for w in writes:
            if w in self.lastw:
                toks.add(self.lastw[w])
            for t in self.readers.get(w, ()):
                toks.add(t)
        out = set()
        for t in toks:
            if t[0] == "eng" and t[1] == eng:
                if eng in ("pe", "sp") or not SAME_ENGINE_SYNC:
                    continue
            out.add(t)
        return out

    def _commit(self, tok, reads, writes):
        for r in reads:
            self.readers.setdefault(r, []).append(tok)
        for w in writes:
            self.lastw[w] = tok
            self.readers[w] = []

    def op(self, eng, fn, reads=(), writes=(), tag=None):
        tag = tag or self.cur_tag
        if tag is not None and tag in SKIP:
            return
        deps = self._deps(eng, reads, writes)
        idx = len(self.ops[eng])
        self.ops[eng].append(dict(fn=fn, deps=deps, dma=None))
        self._commit(("eng", eng, idx), reads, writes)

    def dma(self, eng, fn, semkey, reads=(), writes=(), tag=None, inc=16):
        if tag is not None and tag in SKIP:
            return
        deps = self._deps(eng, reads, writes)
        if semkey in self.ext:
            self.ext[semkey]["count"] += inc
            cnt = self.ext[semkey]["count"]
            self.dma_cnt.setdefault(semkey, 0)
        else:
            cnt = self.dma_cnt.get(semkey, 0) + inc
            self.dma_cnt[semkey] = cnt
        self.ops[eng].append(dict(fn=fn, deps=deps, dma=semkey, inc=inc))
        self._commit(("dma", semkey, cnt), reads, writes)

    def dma_multi(self, eng, fns, semkey, reads=(), writes_list=(), tag=None):
        if tag is not None and tag in SKIP:
            return
        final = self.dma_cnt.get(semkey, 0) + 16 * len(fns)
        for fn, w in zip(fns, writes_list):
            deps = self._deps(eng, reads, w)
            self.ops[eng].append(dict(fn=fn, deps=deps, dma=semkey))
            self._commit(("dma", semkey, final), reads, w)
        self.dma_cnt[semkey] = final

    def emit(self, final_waits=()):
        nc = self.nc
        self.op("sp", None, reads=tuple(final_waits))
        needed = set()
        for e in self.ENGS:
            for o in self.ops[e]:
                for t in o["deps"]:
                    if t[0] == "eng":
                        needed.add((t[1], t[2]))
        inc_count = {}
        for e in self.ENGS:
            c = 0
            for i, o in enumerate(self.ops[e]):
                if (e, i) in needed:
                    c += 1
                    inc_count[(e, i)] = c
        with ExitStack() as es:
            Prog.n_emit += 1
            esem = {e: es.enter_context(nc.semaphore("s%d_%s" % (Prog.n_emit, e))) for e in self.ENGS}
            dsem = {k: (self.ext[k]["sem"] if k in self.ext else es.enter_context(nc.semaphore("d%d_%d" % (Prog.n_emit, i))))
                    for i, k in enumerate(sorted(self.dma_cnt, key=str))}
            block = es.enter_context(nc.Block())

            def run(e, h):
                waited = {}
                for i, o in enumerate(self.ops[e]):
                    want = {}
                    for t in o["deps"]:
                        if t[0] == "eng":
                            k, v = ("e", t[1]), inc_count[(t[1], t[2])]
                        else:
                            k, v = ("d", t[1]), t[2]
                        if v > want.get(k, 0):
                            want[k] = v
                    for k, v in want.items():
                        if waited.get(k, 0) >= v:
                            continue
                        waited[k] = v
                        h.wait_ge(esem[k[1]] if k[0] == "e" else dsem[k[1]], v)
                    if o["fn"] is None:
                        continue
                    ins = o["fn"](h)
                    if o["dma"] is not None:
                        ins.then_inc(dsem[o["dma"]], o.get("inc", 16))
                    elif (e, i) in inc_count:
                        ins.then_inc(esem[e], 1)

            block.tensor(lambda h: run("pe", h))
            block.scalar(lambda h: run("act", h))
            block.vector(lambda h: run("dve", h))
            block.gpsimd(lambda h: run("pool", h))
            block.sync(lambda h: run("sp", h))


def _bf16(a):
    return np.asarray(a).astype(ml_dtypes.bfloat16)


def _rope_tables(half, dim_rows):
    inv = np.power(np.float32(10000.0), -np.arange(half, dtype=np.float32) / np.float32(half)).astype(np.float32)
    ang = (np.arange(S, dtype=np.float32)[:, None] * inv[None, :]).astype(np.float32)
    cos = np.cos(ang.astype(np.float64)).astype(np.float32)
    sin = np.sin(ang.astype(np.float64)).astype(np.float32)
    rows = np.arange(dim_rows)
    f = (rows % (2 * half)) % half
    sign = np.where((rows % (2 * half)) < half, -1.0, 1.0).astype(np.float32)
    C = cos[:, f].T
    Sg = (sin[:, f] * sign[None, :]).T
    C = np.ascontiguousarray(C.reshape(dim_rows, ST_PER_BATCH, ST).transpose(1, 0, 2))
    Sg = np.ascontiguousarray(Sg.reshape(dim_rows, ST_PER_BATCH, ST).transpose(1, 0, 2))
    return C.astype(np.float32), Sg.astype(np.float32)


def _ret_consts(h):
    lg = math.log1p(-2.0 ** (-5.0 - h))
    g = lambda e: math.exp(e * lg)
    s = 128.0 ** -0.5
    p = np.arange(128)
    MA = np.zeros((128, 4, 128), np.float64)
    MA[:, 0, :] = s * g(-128) * (p[:, None] <= p[None, :])
    for d in range(1, 4):
        MA[:, d, :] = s * g(128 * (d - 1))
    CA = np.zeros((128, 16), np.float64)
    CA[:, 0] = np.exp((127 - p) * lg)
    CA[:, 1] = EPS * np.exp(-2.0 * (p + 1) * lg)
    for a in range(4):
        CA[:, 2 + a] = g(128 * a)
        CA[:, 6 + a] = s * g(128 * (3 - a))
    CA[:, 10] = g(512)
    return MA.reshape(128, 512).astype(np.float32), CA.astype(np.float32)


def _rep(v, n=128):
    return np.ascontiguousarray(np.broadcast_to(np.asarray(v, np.float32).reshape(1, -1), (n, np.asarray(v).size)))


_PID_CACHE = {}


class Alloc:
    n = 0

    def __init__(self, nc, es):
        self.nc, self.es = nc, es
        Alloc.n += 1
        self.pfx = "p%d_" % Alloc.n

    def sb(self, name, shape, dt):
        nb = int(np.prod(shape[1:])) * (4 if dt == F32 else 2)
        self.total = getattr(self, "total", 0) + nb
        if os.environ.get("KDEBUG"):
            print("  sbuf", self.pfx, name, nb, "total", self.total)
        return self.es.enter_context(self.nc.sbuf_tensor(self.pfx + "sb_" + name, list(shape), dt))

    def ps(self, name, shape, dt):
        return self.es.enter_context(self.nc.psum_tensor(self.pfx + "ps_" + name, list(shape), dt))


def _finish(P, extra=()):
    allres = list(P.lastw.keys())
    P.op("sp", lambda h: h.nop(), reads=allres, writes=["__done"])
    for e in ("pe", "act", "dve", "pool"):
        P.op(e, lambda h: h.nop(), reads=["__done"])
    P.emit(final_waits=allres)


def _load_w_bf16(P, A, Wsb, Wd, nkc, c0, c1, key):
    stg = [A.sb("%s_stg%d" % (key, i), [128, c1 - c0], F32) for i in range(2)]
    for kc in range(nkc):
        i = kc % 2
        P.dma("sp", lambda h, kc=kc, i=i: h.dma_start(out=stg[i][:, :], in_=Wd[kc * 128:(kc + 1) * 128, c0:c1]),
              (key + "_stg", i), writes=[(key + "_stg", i)], tag="ldW")
        if i == 0:
            P.op("act", lambda h, kc=kc, i=i: h.copy(out=Wsb[:, kc, c0:c1], in_=stg[i][:, :]),
                 reads=[(key + "_stg", i)], writes=[(key, kc)], tag="ldW")
        else:
            P.op("dve", lambda h, kc=kc, i=i: h.tensor_copy(out=Wsb[:, kc, c0:c1], in_=stg[i][:, :]),
                 reads=[(key + "_stg", i)], writes=[(key, kc)], tag="ldW")


def phase_T(nc, X, d):
    with ExitStack() as es:
        A = Alloc(nc, es)
        P = Prog(nc)
        t2 = d.get("t2")
        t1s = d.get("t1", [])
        NX = 4
        X = A.sb("Xr", [128, NX, D], F32)
        I = A.sb("I", [128, 128], BF16)
        junk = A.sb("junk", [128, D], BF16)
        st_ = A.sb("stt", [128, 8], F32)
        P.dma("sp", lambda h: h.dma_start(out=I[:, :], in_=d["ident"][:, :]), "I", writes=["I"])
        if t2:
            nkc = t2["nkc"]
            Wo = A.sb("Wo", [128, nkc, D], BF16)
            Gp = A.sb("Gp", [128, D], F32)
            whole = bool(t2.get("whole"))
            if whole:
                OGw = A.sb("OGw", [128, 8, ST_PER_CORE, nkc // 8, ST], BF16)
            else:
                OG = [A.sb("OG%d" % i, [128, nkc, ST], BF16) for i in range(2)]
            tmp = A.sb("tmp", [128, D], F32)
            Y = [A.ps("Y%d" % i, [128, D], F32) for i in range(2)]
            _load_w_bf16(P, A, Wo, t2["wout"], nkc, 0, D, "Wo")
            P.dma("sp", lambda h: h.dma_start(out=Gp[:, :], in_=t2["gpost"][:, :]), "Gp", writes=["Gp"])
            nch = nkc // 8
        G1 = []
        for i, (g, _) in enumerate(t1s):
            Gt = A.sb("G1_%d" % i, [128, D], F32)
            P.dma("sp", lambda h, Gt=Gt, g=g: h.dma_start(out=Gt[:, :], in_=g[:, :]), "G1_%d" % i, writes=["G1_%d" % i])
            G1.append(Gt)
        if t1s:
            hb = A.sb("hb", [128, D], BF16)
            psT = A.ps("psT", [128, 8, 128], BF16)
            hTs = [[A.sb("hTs%d_%d" % (i, k), [128, 8, ST], BF16) for k in range(2)] for i in range(len(t1s))]

        pidc = {}

        def load_og(s):
            sl = s % 2
            if whole:
                if s == 0:
                    def ldw(h):
                        pid = h.partition_id()
                        src = t2["og"][:, bass.ds(pid, 1)]
                        return h.dma_start(out=OGw[:, :, :, :, :], in_=src[:, 0].rearrange("h p s c t -> p h s c t"))
                    P.dma("sp", ldw, "OGw", reads=["og_dram"], writes=["OGw"])
                return
            if t2.get("dyn"):
                def ld(h, sl=sl, s=s):
                    if "pid" not in pidc:
                        pidc["pid"] = h.partition_id()
                    src = t2["og"][:, bass.ds(pidc["pid"], 1)]
                    return h.dma_start(out=OG[sl][:, :, :].rearrange("p (h c) t -> p h c t", h=8),
                                       in_=src[:, 0, :, s].rearrange("h p c t -> p h c t"))
                P.dma(t2.get("dynq", "sp"), ld, ("OG", sl), reads=["og_dram"], writes=[("OG", sl, hh) for hh in range(8)])
            else:
                P.dma_multi("sp", [lambda h, hh=hh, sl=sl, s=s: h.dma_start(
                    out=OG[sl][:, hh * nch:(hh + 1) * nch, :], in_=t2["og"][hh, :, s]) for hh in range(8)],
                    ("OG", sl), writes_list=[[("OG", sl, hh)] for hh in range(8)])

        def load_x(lt):
            P.dma("sp", lambda h, lt=lt: h.dma_start(out=X[:, lt % NX, :], in_=d["x_in"][lt * 128:(lt + 1) * 128, :]),
                  ("Xin", lt % NX), writes=[("X", lt % NX)])

        for lt in range(3):
            load_x(lt)
        if t2:
            load_og(0)
        def stage1(lt):
            s, a = lt // 4, lt % 4
            if t2:
                if a == 0 and s + 1 < ST_PER_CORE:
                    load_og(s + 1)
                sl = s % 2
                y = Y[lt % 2]
                for nb in range(2):
                    for kc in range(nkc):
                        P.op("pe", lambda h, y=y, nb=nb, kc=kc, sl=sl, a=a, s=s: h.matmul(
                            y[:, nb * 512:(nb + 1) * 512],
                            lhsT=(OGw[:, kc // nch, s, kc % nch, a * 128:(a + 1) * 128] if whole else OG[sl][:, kc, a * 128:(a + 1) * 128]),
                            rhs=Wo[:, kc, nb * 512:(nb + 1) * 512], start=(kc == 0), stop=(kc == nkc - 1)),
                            reads=["OGw" if whole else ("OG", sl, kc // nch), ("Wo", kc)], writes=[("Y", lt % 2)], tag="t2mm")

        def stage1_post(lt):
            if t2:
                y = Y[lt % 2]
                P.op("act", lambda h, y=y: h.activation(out=junk[:, :], in_=y[:, :], func=AF.Square, accum_out=st_[:, 0:1]),
                     reads=[("Y", lt % 2)], writes=["junk", "ssq"], tag="t2sq")
                P.op("act", lambda h: h.activation(out=st_[:, 1:2], in_=st_[:, 0:1], func=AF.Sqrt, scale=1.0 / D, bias=EPS),
                     reads=["ssq"], writes=["rstd"])
                P.op("dve", lambda h: h.reciprocal(out=st_[:, 1:2], in_=st_[:, 1:2]), reads=["rstd"], writes=["rstd"])
                P.op("dve", lambda h, y=y: h.scalar_tensor_tensor(out=tmp[:, :], in0=y[:, :], scalar=st_[:, 1:2], in1=Gp[:, :],
                                                                  op0=ALU.mult, op1=ALU.mult),
                     reads=[("Y", lt % 2), "rstd", "Gp"], writes=["tmp"], tag="t2stt")
                P.op(ELT, lambda h, lt=lt: h.tensor_tensor(out=X[:, lt % NX, :], in0=X[:, lt % NX, :], in1=tmp[:, :], op=ALU.add),
                     reads=[("X", lt % NX), "tmp"], writes=[("X", lt % NX)], tag="t2add")
            if d.get("x_out") is not None:
                P.dma(STQ, lambda h, lt=lt: h.dma_start(out=d["x_out"][lt * 128:(lt + 1) * 128, :], in_=X[:, lt % NX, :]),
                      ("Xout", lt % 4), reads=[("X", lt % NX)], writes=[("xo", lt)])

        def stage2(lt):
            s, a = lt // 4, lt % 4
            for i, (g, hT_out) in enumerate(t1s):
                sl = s % 2
                P.op("act", lambda h, lt=lt: h.activation(out=junk[:, :], in_=X[:, lt % NX, :], func=AF.Square, accum_out=st_[:, 2:3]),
                     reads=[("X", lt % NX)], writes=["junk", "ssq1"])
                P.op("act", lambda h: h.activation(out=st_[:, 3:4], in_=st_[:, 2:3], func=AF.Sqrt, scale=1.0 / D, bias=EPS),
                     reads=["ssq1"], writes=["rstd1"])
                P.op("dve", lambda h: h.reciprocal(out=st_[:, 3:4], in_=st_[:, 3:4]), reads=["rstd1"], writes=["rstd1"])
                P.op("dve", lambda h, lt=lt, i=i: h.scalar_tensor_tensor(out=hb[:, :], in0=X[:, lt % NX, :], scalar=st_[:, 3:4],
                                                                         in1=G1[i][:, :], op0=ALU.mult, op1=ALU.mult),
                     reads=[("X", lt % NX), "rstd1", "G1_%d" % i], writes=["hb"])
                for kc in range(8):
                    P.op("pe", lambda h, kc=kc: h.transpose(out=psT[:, kc, :], in_=hb[:, kc * 128:(kc + 1) * 128], identity=I[:, :]),
                         reads=["hb", "I"], writes=["psT"])
                P.op("act", lambda h, i=i, sl=sl, a=a: h.copy(out=hTs[i][sl][:, :, a * 128:(a + 1) * 128], in_=psT[:, :, :]),
                     reads=["psT"], writes=[("hTs", i, sl)])
                if a == 3:
                    P.dma(STQ, lambda h, i=i, sl=sl, s=s, hT_out=hT_out: h.dma_start(out=hT_out[s], in_=hTs[i][sl][:, :, :]),
                          ("hTo", i, sl), reads=[("hTs", i, sl)], writes=[("hTd", i, s)])

        stage1(0)
        stage1_post(0)
        for lt in range(TILES_PER_CORE):
            if lt + 3 < TILES_PER_CORE:
                load_x(lt + 3)
            if lt + 1 < TILES_PER_CORE:
                stage1(lt + 1)
            stage2(lt)
            if lt + 1 < TILES_PER_CORE:
                stage1_post(lt + 1)
        _finish(P)


def phase_HA(nc, d):
    with ExitStack() as es:
        A = Alloc(nc, es)
        P = Prog(nc)
        W = A.sb("W", [128, 8, 1024], BF16)
        MA = A.sb("MA", [128, 512], F32)
        CA = A.sb("CA", [128, 16], F32)
        I = A.sb("I", [128, 128], BF16)
        Sst = A.sb("Sst", [128, 256], F32)
        S0b = A.sb("S0b", [128, 4, 256], BF16)
        hTs = [A.sb("hTs%d" % i, [128, 8, ST], BF16) for i in range(2)]
        Ct = [A.sb("Ct%d" % i, [128, ST], F32) for i in range(2)]
        Sn = [A.sb("Sn%d" % i, [128, ST], F32) for i in range(2)]
        t1 = A.sb("t1", [128, ST], F32)
        t2 = A.sb("t2", [128, ST], F32)
        qT = A.sb("qT", [128, ST], BF16)
        kT = A.sb("kT", [128, ST], BF16)
        ktok = A.sb("ktok", [128, 4, 128], BF16)
        vt = A.sb("vt", [128, 4, 256], BF16)
        sg = A.sb("sg", [128, 4, 256], BF16)
        sTb = [A.sb("sTb%d" % b, [128, (4 - b) * 128], BF16) for b in range(4)]
        on = A.sb("on", [128, 4, 256], F32)
        og = A.sb("og", [128, 4, 256], BF16)
        ogT = [A.sb("ogT%d" % i, [128, 2, ST], BF16) for i in range(2)]
        stats = A.sb("stats", [128, 4, 6], F32)
        mv = A.sb("mv", [128, 4, 2], F32)
        rs = A.sb("rs", [128, 4], F32)
        nbv = A.sb("nbv", [128, 4], F32)
        pj = [A.ps("pj%d" % i, [128, 512], F32) for i in range(2)]
        sc = [A.ps("sc%d" % i, [128, 512], F32) for i in range(2)]
        po = A.ps("po", [128, 4, 256], F32)
        pdS = A.ps("pdS", [128, 512], F32)
        ptr = A.ps("ptr", [128, 1024], BF16)

        _load_w_bf16(P, A, W, d["w"], 8, 0, 1024, "W")
        P.dma("sp", lambda h: h.dma_start(out=MA[:, :], in_=d["MA"][:, :]), "MA", writes=["MA"], tag="ldC")
        P.dma("sp", lambda h: h.dma_start(out=CA[:, :], in_=d["CA"][:, :]), "CA", writes=["CA"], tag="ldC")
        P.dma("sp", lambda h: h.dma_start(out=I[:, :], in_=d["ident"][:, :]), "I", writes=["I"], tag="ldC")

        def load(st):
            sl = st % 2
            sti = st % ST_PER_BATCH
            P.dma("sp", lambda h: h.dma_start(out=hTs[sl][:, :, :], in_=d["hT_all"][st]), ("hTs", sl), writes=[("hTs", sl)], tag="ldS")
            P.dma("sp", lambda h: h.dma_start(out=Ct[sl][:, :], in_=d["cos"][sti]), ("Ct", sl), writes=[("Ct", sl)], tag="ldS2")
            P.dma("sp", lambda h: h.dma_start(out=Sn[sl][:, :], in_=d["sin"][sti]), ("Sn", sl), writes=[("Sn", sl)], tag="ldS2")

        def proj_rope(sl, c0, dst, dname):
            for j in range(2):
                for kc in range(8):
                    P.op("pe", lambda h, j=j, kc=kc: h.matmul(pj[j][:, :], lhsT=W[:, kc, c0 + j * 128:c0 + (j + 1) * 128],
                                                             rhs=hTs[sl][:, kc, :], start=(kc == 0), stop=(kc == 7)),
                         reads=[("W", kc), ("hTs", sl)], writes=[("pj", j)])
            P.op("dve", lambda h: h.tensor_tensor(out=t1[:, :], in0=pj[0][:, :], in1=Ct[sl][:, :], op=ALU.mult),
                 reads=[("pj", 0), ("Ct", sl)], writes=["t1"])
            P.op("dve", lambda h: h.tensor_tensor(out=t2[:, :], in0=pj[1][:, :], in1=Sn[sl][:, :], op=ALU.mult),
                 reads=[("pj", 1), ("Sn", sl)], writes=["t2"])
            P.op(ELT, lambda h: h.tensor_tensor(out=dst[:, :], in0=t1[:, :], in1=t2[:, :], op=ALU.add),
                 reads=["t1", "t2"], writes=[dname])

        load(0)

        def step(st):
            sl = st % 2
            if st + 1 < NST:
                load(st + 1)
            if st % ST_PER_BATCH == 0:
                P.op("dve", lambda h: h.memset(Sst[:, :], 0.0), writes=["S"], tag="ms")
            P.cur_tag = "s0b"
            for a in range(4):
                P.op("act", lambda h, a=a: h.activation(out=S0b[:, a, :], in_=Sst[:, :], func=AF.Copy, scale=CA[:, 2 + a:3 + a]),
                     reads=["S", "CA"], writes=[("S0b", a)])
            P.cur_tag = "rope"
            proj_rope(sl, 0, qT, "qT")
            proj_rope(sl, 256, kT, "kT")
            P.cur_tag = "ktok"
            for a in range(4):
                P.op("pe", lambda h, a=a: h.transpose(out=ptr[:, a * 128:(a + 1) * 128], in_=kT[:, a * 128:(a + 1) * 128],
                                                      identity=I[:, :]),
                     reads=["kT", "I"], writes=["ptr"])
            P.op("dve", lambda h: h.tensor_tensor(out=ktok[:, :, :], in0=ptr[:, 0:512].rearrange("p (a t) -> p a t", a=4),
                                                  in1=CA[:, 6:10].unsqueeze(2).to_broadcast([128, 4, 128]), op=ALU.mult),
                 reads=["ptr", "CA"], writes=["ktok"])
            P.cur_tag = "vg"
            for a in range(4):
                j = a % 2
                for kc in range(8):
                    P.op("pe", lambda h, a=a, j=j, kc=kc: h.matmul(pj[j][:, :], lhsT=hTs[sl][:, kc, a * 128:(a + 1) * 128],
                                                                   rhs=W[:, kc, 512:1024], start=(kc == 0), stop=(kc == 7)),
                         reads=[("W", kc), ("hTs", sl)], writes=[("pj", j)])
                P.op("dve", lambda h, a=a, j=j: h.tensor_scalar_mul(out=vt[:, a, :], in0=pj[j][:, 0:256], scalar1=CA[:, 0:1]),
                     reads=[("pj", j), "CA"], writes=[("vt", a)])
                P.op("act", lambda h, a=a, j=j: h.activation(out=sg[:, a, :], in_=pj[j][:, 256:512], func=AF.Silu),
                     reads=[("pj", j), ("vt", a)], writes=[("sg", a)])
            P.cur_tag = "scr"
            place = {0: (0, 0), 1: (1, 0), 3: (1, 384), 2: (0, 0)}
            for b in (0, 1, 3, 2):
                n = (4 - b) * 128
                bank, off = place[b]
                P.op("pe", lambda h, b=b, n=n, bank=bank, off=off: h.matmul(
                    sc[bank][:, off:off + n], lhsT=kT[:, b * 128:(b + 1) * 128], rhs=qT[:, b * 128:512], start=True, stop=True),
                    reads=["kT", "qT"], writes=[("sc", bank)])
                P.op("dve", lambda h, b=b, n=n, bank=bank, off=off: h.tensor_tensor(
                    out=sTb[b][:, :], in0=sc[bank][:, off:off + n], in1=MA[:, 0:n], op=ALU.mult),
                    reads=[("sc", bank), "MA"], writes=[("sTb", b)])
            P.cur_tag = "dS"
            for b in range(4):
                P.op("pe", lambda h, b=b: h.matmul(pdS[:, 0:256], lhsT=ktok[:, b, :], rhs=vt[:, b, :], start=(b == 0), stop=(b == 3)),
                     reads=["ktok", ("vt", b)], writes=["pdS"])
            P.cur_tag = "po"
            for a in range(4):
                for b in range(a + 1):
                    P.op("pe", lambda h, a=a, b=b: h.matmul(po[:, a, :], lhsT=sTb[b][:, (a - b) * 128:(a - b + 1) * 128],
                                                            rhs=vt[:, b, :], start=(b == 0), stop=False),
                         reads=[("sTb", b), ("vt", b)], writes=[("po", a // 2)])
                P.op("pe", lambda h, a=a: h.matmul(po[:, a, :], lhsT=qT[:, a * 128:(a + 1) * 128], rhs=S0b[:, a, :],
                                                   start=False, stop=True),
                     reads=["qT", ("S0b", a)], writes=[("po", a // 2)])
            P.cur_tag = "Supd"
            P.op("dve", lambda h: h.scalar_tensor_tensor(out=Sst[:, :], in0=Sst[:, :], scalar=CA[:, 10:11], in1=pdS[:, 0:256],
                                                         op0=ALU.mult, op1=ALU.add),
                 reads=["S", "CA", "pdS"], writes=["S"])
            P.cur_tag = "gn"
            for a in range(4):
                P.op("dve", lambda h, a=a: h.bn_stats(out=stats[:, a, :], in_=po[:, a, :]), reads=[("po", a // 2)], writes=[("stats", a)])
                P.op("dve", lambda h, a=a: h.bn_aggr(out=mv[:, a, :], in_=stats[:, a, :]), reads=[("stats", a)], writes=["mv"])
            P.op("act", lambda h: h.activation(out=rs[:, :], in_=mv[:, :, 1], func=AF.Sqrt, bias=CA[:, 1:2], scale=1.0),
                 reads=["mv", "CA"], writes=["rs"])
            P.op("dve", lambda h: h.reciprocal(out=rs[:, :], in_=rs[:, :]), reads=["rs"], writes=["rs"])
            P.op("dve", lambda h: h.scalar_tensor_tensor(out=nbv[:, :], in0=mv[:, :, 0], scalar=-1.0, in1=rs[:, :],
                                                         op0=ALU.mult, op1=ALU.mult),
                 reads=["mv", "rs"], writes=["nbv"])
            P.cur_tag = "on"
            for a in range(4):
                P.op("act", lambda h, a=a: h.activation(out=on[:, a, :], in_=po[:, a, :], func=AF.Identity,
                                                        bias=nbv[:, a:a + 1], scale=rs[:, a:a + 1]),
                     reads=[("po", a // 2), "rs", "nbv"], writes=[("on", a)])
            P.cur_tag = "og"
            P.op(ELT, lambda h: h.tensor_tensor(out=og[:, :, :], in0=on[:, :, :], in1=sg[:, :, :], op=ALU.mult),
                 reads=[("on", a) for a in range(4)] + [("sg", a) for a in range(4)], writes=["og"])
            P.cur_tag = "ogT"
            for a in range(4):
                for c in range(2):
                    P.op("pe", lambda h, a=a, c=c: h.transpose(out=ptr[:, c * 512 + a * 128:c * 512 + (a + 1) * 128],
                                                               in_=og[:, a, c * 128:(c + 1) * 128], identity=I[:, :]),
                         reads=["og", "I"], writes=["ptr"])
            P.op("act", lambda h: h.copy(out=ogT[sl][:, :, :], in_=ptr[:, :].rearrange("p (c t) -> p c t", c=2)),
                 reads=["ptr"], writes=[("ogT", sl)])
            P.cur_tag = None
            P.dma(STQ, lambda h, st=st: h.dma_start(out=d["og_out"][st // ST_PER_CORE, :, st % ST_PER_CORE], in_=ogT[sl][:, :, :]), ("ogo", sl),
                  reads=[("ogT", sl)], writes=[("ogd", st)], tag="stO")

        for st in range(d.get("nst", NST)):
            step(st)
        _finish(P)


def phase_HB(nc, d):
    lam_init = d["lam_init"]
    with ExitStack() as es:
        A = Alloc(nc, es)
        P = Prog(nc)
        KT = A.sb("KT", [128, NTOK], BF16)
        V = A.sb("V", [128, NTOK // 128, 128], BF16)
        WB = A.sb("WB", [128, 8, 384], BF16)
        WK = A.sb("WK", [128, 8, 384], BF16)
        ones = A.sb("ones", [128, 128], BF16)
        onesF = A.sb("onesF", [128, 128], F32)
        lamv = A.sb("lamv", [128, 2, 2, 64], F32)
        lamt = A.sb("lamt", [128, 2, 64], F32)
        lams = A.sb("lams", [128, 4], F32)
        subg = A.sb("subg", [128, 2], F32)
        hTs = [A.sb("hTs%d" % i, [128, 8, ST], BF16) for i in range(2)]
        Ct = [A.sb("Ct%d" % i, [128, ST], F32) for i in range(2)]
        Sn = [A.sb("Sn%d" % i, [128, ST], F32) for i in range(2)]
        t1 = A.sb("t1", [128, ST], F32)
        t2 = A.sb("t2", [128, ST], F32)
        qT = A.sb("qT", [128, ST], BF16)
        sgT = A.sb("sgT", [128, ST], F32)
        PT = [A.sb("PT%d" % i, [128, 2, ST], BF16) for i in range(2)]
        r1 = A.sb("r1", [128, ST], F32)
        r2 = A.sb("r2", [128, ST], F32)
        a1 = A.sb("a1", [128, ST], F32)
        a2 = A.sb("a2", [128, ST], F32)
        sq = A.sb("sq", [128, ST], F32)
        ogT = [A.sb("ogT%d" % i, [128, 1, ST], BF16) for i in range(2)]
        sc = [A.ps("sc%d" % i, [128, 2, ST], F32) for i in range(2)]
        po = [A.ps("po%d" % i, [128, ST], F32) for i in range(2)]
        pl = [A.ps("pl%d" % i, [128, ST], F32) for i in range(2)]

        _load_w_bf16(P, A, WB, d["w"], 8, 0, 384, "WB")
        if d.get("kv_in") is None:
            _load_w_bf16(P, A, WK, d["wkv"], 8, 0, 384, "WK")
        P.dma("sp", lambda h: h.dma_start(out=lamv[:, :, :, :], in_=d["lamv"][:, :, :, :]), "lamv", writes=["lamv"])
        P.dma("sp", lambda h: h.dma_start(out=subg[:, 0:1], in_=d["subg"][:, :]), "subg", writes=["subg"])
        P.op("dve", lambda h: h.memset(ones[:, :], 1.0), writes=["ones"])
        P.op("dve", lambda h: h.memset(onesF[:, :], 1.0), writes=["onesF"])
        P.op("dve", lambda h: h.tensor_tensor(out=lamt[:, :, :], in0=lamv[:, 0, :, :], in1=lamv[:, 1, :, :], op=ALU.mult),
             reads=["lamv"], writes=["lamt"])
        P.op("dve", lambda h: h.reduce_sum(out=lams[:, 0:2], in_=lamt[:, :, :], axis=AX.X), reads=["lamt"], writes=["lams"])
        P.op("act", lambda h: h.activation(out=lams[:, 0:2], in_=lams[:, 0:2], func=AF.Exp), reads=["lams"], writes=["lams"])
        P.op("dve", lambda h: h.tensor_tensor(out=lams[:, 2:3], in0=lams[:, 1:2], in1=lams[:, 0:1], op=ALU.subtract),
             reads=["lams"], writes=["lams"])
        P.op("dve", lambda h: h.tensor_scalar_add(out=lams[:, 3:4], in0=lams[:, 2:3], scalar1=-lam_init),
             reads=["lams"], writes=["neglam"])
        P.op("dve", lambda h: h.tensor_scalar_mul(out=subg[:, 1:2], in0=subg[:, 0:1], scalar1=1.0 - lam_init),
             reads=["subg"], writes=["subgs"])

        def load(src, st, with_h=True):
            sl = st % 2
            sti = st % ST_PER_BATCH
            P.dma("sp", lambda h: h.dma_start(out=hTs[sl][:, :, :], in_=src[st]), ("hTs", sl), writes=[("hTs", sl)])
            P.dma("sp", lambda h: h.dma_start(out=Ct[sl][:, :], in_=d["cos"][sti]), ("Ct", sl), writes=[("Ct", sl)])
            P.dma("sp", lambda h: h.dma_start(out=Sn[sl][:, :], in_=d["sin"][sti]), ("Sn", sl), writes=[("Sn", sl)])

        def proj_rope(Wt, wkey, sl, dst_ap, dname, bank):
            for j in range(2):
                for kc in range(8):
                    P.op("pe", lambda h, j=j, kc=kc: h.matmul(sc[bank][:, j, :], lhsT=Wt[:, kc, j * 128:(j + 1) * 128],
                                                             rhs=hTs[sl][:, kc, :], start=(kc == 0), stop=(kc == 7)),
                         reads=[(wkey, kc), ("hTs", sl)], writes=[("sc", bank)])
            P.op("dve", lambda h: h.tensor_tensor(out=t1[:, :], in0=sc[bank][:, 0, :], in1=Ct[sl][:, :], op=ALU.mult),
                 reads=[("sc", bank), ("Ct", sl)], writes=["t1"])
            P.op("dve", lambda h: h.tensor_tensor(out=t2[:, :], in0=sc[bank][:, 1, :], in1=Sn[sl][:, :], op=ALU.mult),
                 reads=[("sc", bank), ("Sn", sl)], writes=["t2"])
            P.op(ELT, lambda h: h.tensor_tensor(out=dst_ap, in0=t1[:, :], in1=t2[:, :], op=ALU.add),
                 reads=["t1", "t2"], writes=[dname])

        nst = d.get("nst", NST)

        def kv_step(st):
            sl = st % 2
            if st + 1 < nst:
                load(d["hTkv_all"], st + 1)
            else:
                load(d["hT_all"], 0)
            proj_rope(WK, "WK", sl, KT[:, st * ST:(st + 1) * ST], ("KT", st), 0)
            for a in range(4):
                for kc in range(8):
                    P.op("pe", lambda h, a=a, kc=kc: h.matmul(sc[1][:, 0, a * 128:(a + 1) * 128], lhsT=hTs[sl][:, kc, a * 128:(a + 1) * 128],
                                                              rhs=WK[:, kc, 256:384], start=(kc == 0), stop=(kc == 7)),
                         reads=[("WK", kc), ("hTs", sl)], writes=[("sc", 1)])
            P.op("act", lambda h: h.copy(out=V[:, st * 4:(st + 1) * 4, :], in_=sc[1][:, 0, :].rearrange("p (a e) -> p a e", a=4)),
                 reads=[("sc", 1)], writes=[("V", st)])

        if d.get("kv_in") is not None:
            ktd, vd = d["kv_in"]
            nchunk = 8
            cw = NTOK // nchunk
            P.dma_multi("sp", [lambda h, c=c: h.dma_start(out=KT[:, c * cw:(c + 1) * cw], in_=ktd[:, c * cw:(c + 1) * cw])
                               for c in range(nchunk)], "KTld",
                        writes_list=[[("KT", st) for st in range(c * cw // ST, (c + 1) * cw // ST)] for c in range(nchunk)])
            tw = (NTOK // 128) // nchunk
            P.dma_multi("sp", [lambda h, c=c: h.dma_start(out=V[:, c * tw:(c + 1) * tw, :], in_=vd[:, c * tw:(c + 1) * tw, :])
                               for c in range(nchunk)], "Vld",
                        writes_list=[[("V", st) for st in range(c * tw // 4, (c + 1) * tw // 4)] for c in range(nchunk)])
            load(d["hT_all"], 0)
        else:
            load(d["hTkv_all"], 0)
            for st in range(nst):
                kv_step(st)
            if d.get("kv_out") is not None:
                ktd, vd = d["kv_out"]
                allkt = [("KT", st) for st in range(nst)]
                allv = [("V", st) for st in range(nst)]
                for c in range(4):
                    cw = NTOK // 4
                    P.dma(STQ, lambda h, c=c, cw=cw: h.dma_start(out=ktd[:, c * cw:(c + 1) * cw], in_=KT[:, c * cw:(c + 1) * cw]),
                          ("KTst", c), reads=allkt, writes=[("KTd", c)])
                    tw = (NTOK // 128) // 4
                    P.dma(STQ, lambda h, c=c, tw=tw: h.dma_start(out=vd[:, c * tw:(c + 1) * tw, :], in_=V[:, c * tw:(c + 1) * tw, :]),
                          ("Vst", c), reads=allv, writes=[("Vd", c)])

        def group(g):
            sl = g % 2
            gi = g % ST_PER_BATCH
            bb = g // ST_PER_BATCH
            if g + 1 < nst:
                load(d["hT_all"], g + 1)
            proj_rope(WB, "WB", sl, qT[:, :], "qT", 0)
            for kc in range(8):
                P.op("pe", lambda h, kc=kc: h.matmul(sc[1][:, 0, :], lhsT=WB[:, kc, 256:384], rhs=hTs[sl][:, kc, :],
                                                     start=(kc == 0), stop=(kc == 7)),
                     reads=[("WB", kc), ("hTs", sl)], writes=[("sc", 1)])
            P.op("act", lambda h: h.activation(out=sgT[:, :], in_=sc[1][:, 0, :], func=AF.Silu), reads=[("sc", 1)], writes=["sgT"])
            nk = 4 * (gi + 1)

            def scores(kt):
                p = kt % 2
                T = bb * (S // 128) + kt
                q0 = max(0, kt - 4 * gi) * 128
                for t in range(2):
                    P.op("pe", lambda h, t=t: h.matmul(sc[p][:, t, q0:ST], lhsT=KT[t * 64:(t + 1) * 64, T * 128:(T + 1) * 128],
                                                       rhs=qT[t * 64:(t + 1) * 64, q0:ST], start=True, stop=True),
                         reads=[("KT", T // 4), "qT"], writes=[("sc", p)], tag="b_sc")
                P.op("act", lambda h: h.activation(out=PT[p][:, :, q0:ST], in_=sc[p][:, :, q0:ST], func=AF.Exp, scale=0.125),
                     reads=[("sc", p)], writes=[("PT", p)], tag="b_exp")
                if kt >= 4 * gi:
                    P.op(ELT, lambda h: h.memset(PT[p][64:128, :, q0:q0 + 64], 0.0), reads=[("PT", p)], writes=[("PT", p)], tag="b_ms")

            def pv(kt):
                p = kt % 2
                T = bb * (S // 128) + kt
                q0 = max(0, kt - 4 * gi) * 128
                for t in range(2):
                    P.op("pe", lambda h, t=t: h.matmul(po[t][:, q0:ST], lhsT=V[:, T, :], rhs=PT[p][:, t, q0:ST],
                                                       start=(kt == 0), stop=(kt == nk - 1)),
                         reads=[("V", T // 4), ("PT", p)], writes=[("po", t)], tag="b_pv")
                    P.op("pe", lambda h, t=t: h.matmul(pl[t][:, q0:ST], lhsT=ones[:, :], rhs=PT[p][:, t, q0:ST],
                                                       start=(kt == 0), stop=(kt == nk - 1)),
                         reads=["ones", ("PT", p)], writes=[("pl", t)], tag="b_ones")

            for kt in range(nk):
                scores(kt)
                if kt > 0:
                    pv(kt - 1)
            pv(nk - 1)
            P.op("dve", lambda h: h.reciprocal(out=r1[:, :], in_=pl[0][:, :]), reads=[("pl", 0)], writes=["r1"])
            P.op("dve", lambda h: h.reciprocal(out=r2[:, :], in_=pl[1][:, :]), reads=[("pl", 1)], writes=["r2"])
            P.op("dve", lambda h: h.tensor_tensor(out=a1[:, :], in0=po[0][:, :], in1=r1[:, :], op=ALU.mult),
                 reads=[("po", 0), "r1"], writes=["a1"])
            P.op("dve", lambda h: h.tensor_tensor(out=a2[:, :], in0=po[1][:, :], in1=r2[:, :], op=ALU.mult),
                 reads=[("po", 1), "r2"], writes=["a2"])
            P.op("dve", lambda h: h.scalar_tensor_tensor(out=a1[:, :], in0=a2[:, :], scalar=lams[:, 3:4], in1=a1[:, :],
                                                         op0=ALU.mult, op1=ALU.add),
                 reads=["a1", "a2", "neglam"], writes=["a1"])
            P.op(ELT, lambda h: h.tensor_tensor(out=sq[:, :], in0=a1[:, :], in1=a1[:, :], op=ALU.mult), reads=["a1"], writes=["sq"])
            P.op("pe", lambda h: h.matmul(sc[1][:, 1, :], lhsT=onesF[:, :], rhs=sq[:, :], start=True, stop=True),
                 reads=["onesF", "sq"], writes=[("sc", 1)])
            P.op("act", lambda h: h.activation(out=r1[:, :], in_=sc[1][:, 1, :], func=AF.Sqrt, scale=1.0 / 128, bias=EPS),
                 reads=[("sc", 1)], writes=["r1"])
            P.op("dve", lambda h: h.reciprocal(out=r1[:, :], in_=r1[:, :]), reads=["r1"], writes=["r1"])
            P.op("dve", lambda h: h.scalar_tensor_tensor(out=a2[:, :], in0=a1[:, :], scalar=subg[:, 1:2], in1=r1[:, :],
                                                         op0=ALU.mult, op1=ALU.mult),
                 reads=["a1", "subgs", "r1"], writes=["a2"])
            P.op(ELT, lambda h: h.tensor_tensor(out=ogT[sl][:, 0, :], in0=a2[:, :], in1=sgT[:, :], op=ALU.mult),
                 reads=["a2", "sgT"], writes=[("ogT", sl)])
            P.dma(STQ, lambda h: h.dma_start(out=d["og_out"][g // ST_PER_CORE, :, g % ST_PER_CORE], in_=ogT[sl][:, :, :]), ("ogo", sl),
                  reads=[("ogT", sl)], writes=[("ogd", g)])

        for g in range(nst):
            group(g)
        _finish(P)


def _dram(nc, name, shape, dt, kind):
    return nc.dram_tensor(name, list(shape), dt, kind=kind).ap()


def _swap_cols(w, blk):
    k, n = w.shape
    return np.ascontiguousarray(w.reshape(k, n // blk, 2, blk // 2)[:, :, ::-1, :].reshape(k, n))


_IDENT = _bf16(np.eye(128, dtype=np.float32))


def _run(nc, in_maps):
    res = run_bass_kernel_spmd(nc, in_maps, core_ids=list(range(NCORES)))
    return res.results


def _build_T(x_in, x_out, t2_nkc, n_t1):
    nc = bass.Bass("TRN2", target_bir_lowering=False)
    d = {"ident": _dram(nc, "ident", [128, 128], BF16, "ExternalInput")}
    if x_in:
        d["x_in"] = _dram(nc, "x_in", [TOK_PER_CORE, D], F32, "ExternalInput")
    if t2_nkc:
        nch = t2_nkc // 8
        d["t2"] = dict(og=_dram(nc, "og", [8, 128, ST_PER_CORE, nch, ST], BF16, "ExternalInput"),
                       wout=_dram(nc, "wout", [t2_nkc * 128, D], F32, "ExternalInput"),
                       gpost=_dram(nc, "gpost", [128, D], F32, "ExternalInput"), nkc=t2_nkc)
    d["t1"] = [(_dram(nc, "g1_%d" % i, [128, D], F32, "ExternalInput"),
                _dram(nc, "hT_%d" % i, [ST_PER_CORE, 128, 8, ST], BF16, "ExternalOutput")) for i in range(n_t1)]
    if x_out:
        d["x_out"] = _dram(nc, "x_out", [TOK_PER_CORE, D], F32, "ExternalOutput")
    phase_T(nc, None, d)
    return nc


def _build_HA(nst=NST):
    nc = bass.Bass("TRN2", target_bir_lowering=False)
    d = dict(hT_all=_dram(nc, "hT_all", [NST, 128, 8, ST], BF16, "ExternalInput"),
             w=_dram(nc, "w", [D, 1024], F32, "ExternalInput"),
             MA=_dram(nc, "MA", [128, 512], F32, "ExternalInput"),
             CA=_dram(nc, "CA", [128, 16], F32, "ExternalInput"),
             cos=_dram(nc, "cos", [ST_PER_BATCH, 128, ST], F32, "ExternalInput"),
             sin=_dram(nc, "sin", [ST_PER_BATCH, 128, ST], F32, "ExternalInput"),
             ident=_dram(nc, "ident", [128, 128], BF16, "ExternalInput"),
             og_out=_dram(nc, "og_out", [NCORES, 128, ST_PER_CORE, 2, ST], BF16, "ExternalOutput"), nst=nst)
    phase_HA(nc, d)
    return nc


def _build_HB(lam_init, nst=NST, kv="compute"):
    nc = bass.Bass("TRN2", target_bir_lowering=False)
    kvd = {}
    if kv == "out":
        kvd["kv_out"] = (_dram(nc, "kt_out", [128, NTOK], BF16, "ExternalOutput"), _dram(nc, "v_out", [128, NTOK // 128, 128], BF16, "ExternalOutput"))
    if kv == "in":
        kvd["kv_in"] = (_dram(nc, "kt_in", [128, NTOK], BF16, "ExternalInput"), _dram(nc, "v_in", [128, NTOK // 128, 128], BF16, "ExternalInput"))
    if kv != "in":
        kvd["hTkv_all"] = _dram(nc, "hTkv_all", [NST, 128, 8, ST], BF16, "ExternalInput")
        kvd["wkv"] = _dram(nc, "wkv", [D, 384], F32, "ExternalInput")
    d = dict(**kvd, hT_all=_dram(nc, "hT_all", [NST, 128, 8, ST], BF16, "ExternalInput"),
             w=_dram(nc, "w", [D, 384], F32, "ExternalInput"),
             cos=_dram(nc, "cos", [ST_PER_BATCH, 128, ST], F32, "ExternalInput"),
             sin=_dram(nc, "sin", [ST_PER_BATCH, 128, ST], F32, "ExternalInput"),
             lamv=_dram(nc, "lamv", [128, 2, 2, 64], F32, "ExternalInput"),
             subg=_dram(nc, "subg", [128, 1], F32, "ExternalInput"),
             og_out=_dram(nc, "og_out", [NCORES, 128, ST_PER_CORE, 1, ST], BF16, "ExternalOutput"),
             lam_init=lam_init, nst=nst)
    phase_HB(nc, d)
    return nc


def phase_AG(nc, src, dst, cc):
    P = Prog(nc)
    P.ext["cc"] = cc
    P.dma("pool", lambda h: h.collective_compute("AllGather", ALU.bypass, replica_groups=[list(range(NCORES))],
                                                 ins=[src.opt()], outs=[dst.opt()]),
          "cc", writes=["dst"], inc=1)
    _finish(P)


LAM_INIT = [0.8 - 0.6 * math.exp(-0.3 * layer) for layer in range(4)]


def _build_fused(nlayers=4, nst=NST):
    nc = bass.Bass("TRN2", target_bir_lowering=False)
    I = lambda name, shape, dt=F32: _dram(nc, name, shape, dt, "ExternalInput")
    N = lambda name, shape, dt=BF16: nc.dram_tensor(name, list(shape), dt).ap()
    ident = I("ident", [128, 128], BF16)
    x_in = I("x_in", [TOK_PER_CORE, D])
    x_out = _dram(nc, "x_out", [TOK_PER_CORE, D], F32, "ExternalOutput")
    pre = [I("pre%d" % l, [128, D]) for l in range(4)]
    post = [I("post%d" % l, [128, D]) for l in range(4)]
    kvg = I("kvg", [128, D])
    wA = [I("wA%d" % l, [D, 1024]) for l in range(2)]
    wB = [I("wB%d" % j, [D, 384]) for j in range(2)]
    wkv = I("wkv", [D, 384])
    wout = [I("wout%d" % l, [2048 if l < 2 else 1024, D]) for l in range(4)]
    MA, CA = I("MA", [128, 512]), I("CA", [128, 16])
    cosA, sinA = I("cosA", [ST_PER_BATCH, 128, ST]), I("sinA", [ST_PER_BATCH, 128, ST])
    cosB, sinB = I("cosB", [ST_PER_BATCH, 128, ST]), I("sinB", [ST_PER_BATCH, 128, ST])
    lamv = [I("lamv%d" % j, [128, 2, 2, 64]) for j in range(2)]
    subg = [I("subg%d" % j, [128, 1]) for j in range(2)]
    _hl, _ha = N("hT_loc", [ST_PER_CORE, 128, 8, ST]), N("hT_all", [NST, 128, 8, ST])
    hT_loc, hT_all = [_hl] * 4, [_ha] * 4
    hTkv_loc, hTkv_all = N("hTkv_loc", [ST_PER_CORE, 128, 8, ST]), N("hTkv_all", [NST, 128, 8, ST])
    nch = [2, 2, 1, 1]
    _ol = {2: N("og_locA", [NCORES, 128, ST_PER_CORE, 2, ST]), 1: N("og_locB", [NCORES, 128, ST_PER_CORE, 1, ST])}
    _oa = {2: N("og_allA", [8, NCORES, 128, ST_PER_CORE, 2, ST]), 1: N("og_allB", [8, NCORES, 128, ST_PER_CORE, 1, ST])}
    og_loc = [_ol[nch[l]] for l in range(4)]
    og_all = [_oa[nch[l]] for l in range(4)]
    with ExitStack() as es0:
        X = None
        xbuf = nc.dram_tensor("xbuf", [TOK_PER_CORE, D], F32).ap()
        pads = [es0.enter_context(nc.semaphore("ccpad%d" % i)) for i in range(8)]
        assert pads[7].num == 162, pads[7].num
        cc = dict(sem=pads[7], count=0)
        phase_T(nc, None, dict(ident=ident, x_in=x_in, t1=[(pre[0], hT_loc[0])]))
        for l in range(nlayers):
            phase_AG(nc, hT_loc[l], hT_all[l], cc)
            if l == 2:
                phase_AG(nc, hTkv_loc, hTkv_all, cc)
            if l < 2:
                phase_HA(nc, dict(hT_all=hT_all[l], w=wA[l], MA=MA, CA=CA, cos=cosA, sin=sinA, ident=ident, og_out=og_loc[l], nst=nst))
            else:
                phase_HB(nc, dict(hT_all=hT_all[l], hTkv_all=hTkv_all, w=wB[l - 2], wkv=wkv, cos=cosB, sin=sinB,
                                  lamv=lamv[l - 2], subg=subg[l - 2], og_out=og_loc[l], lam_init=LAM_INIT[l], nst=nst))
            phase_AG(nc, og_loc[l], og_all[l], cc)
            d = dict(ident=ident, x_in=(x_in if l == 0 else xbuf), x_out=xbuf,
                     t2=dict(og=og_all[l], dyn=True, dynq="sp", whole=(l >= 2),
                                          wout=wout[l], gpost=post[l], nkc=8 * nch[l]))
            if l < nlayers - 1:
                d["t1"] = [(pre[l + 1], hT_loc[l + 1])] + ([(kvg, hTkv_loc)] if l == 1 else [])
            else:
                d["x_out"] = x_out
            phase_T(nc, X, d)
    return nc


def kernel_fused(x, pre_norm, post_norm, w_in_a, w_out_a, kv_norm, w_kv, w_in_b,
                 lam_q1, lam_k1, lam_q2, lam_k2, sub_norm_b, w_out_b):
    xs = np.split(np.ascontiguousarray(x.reshape(NTOK, D)), NCORES, axis=0)
    cosA, sinA = _rope_tables(64, 128)
    cosB, sinB = _rope_tables(32, 128)
    nc = _build_fused(int(os.environ.get("KNL", "4")), int(os.environ.get("KNST", str(NST))))
    maps = []
    for r in range(NCORES):
        MA, CA = _ret_consts(r)
        m = {"ident": _IDENT, "x_in": xs[r], "kvg": _rep(kv_norm), "wkv": _wKV(w_kv, r), "MA": MA, "CA": CA,
             "cosA": cosA, "sinA": sinA, "cosB": cosB, "sinB": sinB}
        for l in range(4):
            m["pre%d" % l] = _rep(pre_norm[l])
            m["post%d" % l] = _rep(post_norm[l])
            m["wout%d" % l] = np.ascontiguousarray(w_out_a[l] if l < 2 else w_out_b[l - 2])
        for l in range(2):
            m["wA%d" % l] = _wA(w_in_a[l], r)
            m["wB%d" % l] = _wB(w_in_b[l], r)
            lamv = np.stack([np.stack([lam_q1[l], lam_q2[l]]), np.stack([lam_k1[l], lam_k2[l]])])
            m["lamv%d" % l] = np.ascontiguousarray(np.broadcast_to(lamv[None], (128, 2, 2, 64))).astype(np.float32)
            m["subg%d" % l] = np.ascontiguousarray(sub_norm_b[l].reshape(128, 1))
        maps.append(m)
    res = _run(nc, maps)
    return np.concatenate([np.asarray(r["x_out"]) for r in res], axis=0).reshape(B, S, D).astype(np.float32)


def _wA(w_in, h):
    q = w_in[:, h * 128:(h + 1) * 128]
    k = w_in[:, 1024 + h * 128:1024 + (h + 1) * 128]
    v = w_in[:, 2048 + h * 256:2048 + (h + 1) * 256]
    g = w_in[:, 4096 + h * 256:4096 + (h + 1) * 256]
    return np.ascontiguousarray(np.concatenate([q, _swap_cols(q, 128), k, _swap_cols(k, 128), v, g], axis=1))


def _wB(w_in, h):
    q = w_in[:, h * 128:(h + 1) * 128]
    g = w_in[:, 1024 + h * 128:1024 + (h + 1) * 128]
    return np.ascontiguousarray(np.concatenate([q, _swap_cols(q, 64), g], axis=1))


def _wKV(w_kv, h):
    k = w_kv[:, h * 128:(h + 1) * 128]
    v = w_kv[:, 1024 + h * 128:1024 + (h + 1) * 128]
    return np.ascontiguousarray(np.concatenate([k, _swap_cols(k, 64), v], axis=1))


def _gather_hT(res, key):
    return np.ascontiguousarray(np.concatenate([np.asarray(r[key]) for r in res], axis=0))


def _a2a(res, key):
    ogs = [np.asarray(r[key]) for r in res]
    return [np.ascontiguousarray(np.stack([ogs[h][r] for h in range(8)], axis=0)) for r in range(NCORES)]


FUSED = os.environ.get("KFUSED", "0") == "1"


def kernel(x, pre_norm, post_norm, w_in_a, w_out_a, kv_norm, w_kv, w_in_b,
           lam_q1, lam_k1, lam_q2, lam_k2, sub_norm_b, w_out_b):
    f = lambda a: np.asarray(a, dtype=np.float32)
    x, pre_norm, post_norm, w_in_a, w_out_a, kv_norm, w_kv, w_in_b = map(f, (x, pre_norm, post_norm, w_in_a, w_out_a, kv_norm, w_kv, w_in_b))
    lam_q1, lam_k1, lam_q2, lam_k2, sub_norm_b, w_out_b = map(f, (lam_q1, lam_k1, lam_q2, lam_k2, sub_norm_b, w_out_b))
    if FUSED:
        return kernel_fused(x, pre_norm, post_norm, w_in_a, w_out_a, kv_norm, w_kv, w_in_b,
                            lam_q1, lam_k1, lam_q2, lam_k2, sub_norm_b, w_out_b)
    xs = np.split(np.ascontiguousarray(x.reshape(NTOK, D)), NCORES, axis=0)
    cosA, sinA = _rope_tables(64, 128)
    cosB, sinB = _rope_tables(32, 128)
    retc = [_ret_consts(h) for h in range(8)]

    nc = _build_T(True, False, 0, 1)
    res = _run(nc, [{"ident": _IDENT, "x_in": xs[r], "g1_0": _rep(pre_norm[0])} for r in range(NCORES)])
    hT = _gather_hT(res, "hT_0")
    ncHA = _build_HA()
    hTkv = None
    for layer in range(4):
        if layer < 2:
            res = _run(ncHA, [{"hT_all": hT, "w": _wA(w_in_a[layer], h), "MA": retc[h][0], "CA": retc[h][1],
                               "cos": cosA, "sin": sinA, "ident": _IDENT} for h in range(NCORES)])
            wout, nkc = w_out_a[layer], 16
        else:
            j = layer - 2
            lam_init = 0.8 - 0.6 * math.exp(-0.3 * layer)
            ncHB = _build_HB(lam_init, kv=("out" if layer == 2 else "in"))
            lamv = np.stack([np.stack([lam_q1[j], lam_q2[j]]), np.stack([lam_k1[j], lam_k2[j]])])
            lamv = np.ascontiguousarray(np.broadcast_to(lamv[None], (128, 2, 2, 64))).astype(np.float32)
            maps = [{"hT_all": hT, "hTkv_all": hTkv, "w": _wB(w_in_b[j], h), "wkv": _wKV(w_kv, h),
                     "cos": cosB, "sin": sinB, "lamv": lamv,
                     "subg": np.ascontiguousarray(sub_norm_b[j].reshape(128, 1))} for h in range(NCORES)]
            if layer == 3:
                for h in range(NCORES):
                    maps[h]["kt_in"], maps[h]["v_in"] = kvs[h]
                    del maps[h]["hTkv_all"], maps[h]["wkv"]
            res = _run(ncHB, maps)
            if layer == 2:
                kvs = [(np.asarray(r["kt_out"]), np.asarray(r["v_out"])) for r in res]
            wout, nkc = w_out_b[j], 8
        ogr = _a2a(res, "og_out")
        last = layer == 3
        n_t1 = 0 if last else (2 if layer == 1 else 1)
        ncT = _build_T(True, True, nkc, n_t1)
        maps = []
        for r in range(NCORES):
            m = {"ident": _IDENT, "x_in": xs[r], "og": ogr[r], "wout": np.ascontiguousarray(wout), "gpost": _rep(post_norm[layer])}
            if n_t1 >= 1:
                m["g1_0"] = _rep(pre_norm[layer + 1])
            if n_t1 == 2:
                m["g1_1"] = _rep(kv_norm)
            maps.append(m)
        res = _run(ncT, maps)
        xs = [np.asarray(r["x_out"]) for r in res]
        if n_t1 >= 1:
            hT = _gather_hT(res, "hT_0")
        if n_t1 == 2:
            hTkv = _gather_hT(res, "hT_1")
    return np.concatenate(xs, axis=0).reshape(B, S, D).astype(np.float32)
```
